# Optimizing a Trainium2 kernel written in Bass

```python
import math
import functools
import jax
import jax.numpy as jnp
from jax import lax
import numpy as np

D_MODEL = 1024
BATCH = 2
SEQ = 8192
DEPTH = 1
DEC_BATCH = 128
DEC_SEQ = 1
PAST_LEN = 2048
PAGE_SIZE = 128

HEAD_DIM = 64
HPG = 4
GROUPS = ((128, 1), (512, 4), (2048, 16))
N_HEADS = len(GROUPS) * HPG
ATTN_W = N_HEADS * HEAD_DIM
ATTN_OUT_W = HPG * HEAD_DIM
D_CONV = D_MODEL
CONV_K = 31
D_FF = int(math.ceil(8 * D_MODEL / 3 / 256)) * 256
D_PLE = 256
N_BUCKETS = 32
MAX_DIST = 2048
BLK = 128
EPS = 1e-6
NEG = -1e30
SCALE = HEAD_DIM ** -0.5
IN_SPLITS = (ATTN_W, 2 * ATTN_W, 3 * ATTN_W, 3 * ATTN_W + 2 * D_CONV)
IN_W = 3 * ATTN_W + 2 * D_CONV + 2 * D_MODEL

kernel_name = 'hybrid_dilated_attn_conformer_conv_decoder_step'


def t5_bucket(dist):
    dist = np.asarray(dist).astype(np.int32)
    max_exact = N_BUCKETS // 2
    d = np.maximum(dist, 1).astype(np.float32)
    large = max_exact + np.floor(np.log(d / max_exact) / np.log(MAX_DIST / max_exact)
                                 * (N_BUCKETS - max_exact)).astype(np.int32)
    large = np.minimum(large, N_BUCKETS - 1)
    return np.where(dist < max_exact, dist, large).astype(np.int32)


def rms_norm(x, g):
    xf = x.astype(jnp.float32)
    y = xf * lax.rsqrt(jnp.mean(xf * xf, axis=-1, keepdims=True) + EPS)
    return (y * g.astype(jnp.float32)).astype(x.dtype)


def layer_norm(x, g, b):
    xf = x.astype(jnp.float32)
    mu = jnp.mean(xf, axis=-1, keepdims=True)
    xc = xf - mu
    y = xc * lax.rsqrt(jnp.mean(xc * xc, axis=-1, keepdims=True) + EPS)
    return (y * g.astype(jnp.float32) + b.astype(jnp.float32)).astype(x.dtype)


def band_dilated_attention(q, k, v, table_g, dil, n_back):
    B, S, H, Dh = q.shape
    L = S // dil
    nb = -(-L // BLK)
    Lp = nb * BLK

    def to_blocks(a):
        a = a.reshape(B, L, dil, H, Dh)
        a = jnp.pad(a, ((0, 0), (0, Lp - L), (0, 0), (0, 0), (0, 0)))
        return a.reshape(B, nb, BLK, dil, H, Dh)

    def with_prev(a):
        prev = jnp.pad(a[:, :-1], ((0, 0), (1, 0), (0, 0), (0, 0), (0, 0), (0, 0)))
        return jnp.concatenate([prev, a], axis=2)

    qb = to_blocks(q)
    kw = with_prev(to_blocks(k))
    vw = with_prev(to_blocks(v))
    qq = np.arange(BLK)[:, None]
    kk = np.arange(2 * BLK)[None, :]
    j = qq + BLK - kk
    band = (j >= 0) & (j <= n_back)
    valid = band[None] & ((np.arange(nb)[:, None, None] > 0) | (kk[None] >= BLK))
    bias = jnp.transpose(table_g[t5_bucket(np.clip(j, 0, n_back) * dil)], (2, 0, 1)).astype(jnp.float32)
    s = jnp.einsum('bnqrhd,bnkrhd->bnrhqk', qb, kw, preferred_element_type=jnp.float32) * SCALE + bias
    s = jnp.where(valid[None, :, None, None], s, NEG)
    lse = jax.nn.logsumexp(s, axis=-1)
    pr = jnp.exp(s - lse[..., None])
    o = jnp.einsum('bnrhqk,bnkrhd->bnqrhd', pr, vw.astype(jnp.float32))
    o = o.reshape(B, Lp, dil, H, Dh)[:, :L].reshape(B, S, H, Dh)
    lse = jnp.transpose(lse, (0, 1, 4, 2, 3)).reshape(B, Lp, dil, H)[:, :L].reshape(B, S, H)
    return o, lse


def gathered_dilated_attention(q, k_all, v_all, table_g, dil, n_back):
    T = q.shape[1]
    hist = k_all.shape[1] - T
    j = np.arange(n_back + 1)
    idx = hist + np.arange(T)[:, None] - j[None, :] * dil
    valid = idx >= 0
    idx = np.maximum(idx, 0)
    kg = k_all[:, idx]
    vg = v_all[:, idx]
    bias = jnp.transpose(table_g[t5_bucket(j * dil)]).astype(jnp.float32)
    s = jnp.einsum('bthd,btjhd->bthj', q, kg, preferred_element_type=jnp.float32) * SCALE + bias[None, None]
    s = jnp.where(valid[None, :, None, :], s, NEG)
    lse = jax.nn.logsumexp(s, axis=-1)
    pr = jnp.exp(s - lse[..., None])
    o = jnp.einsum('bthj,btjhd->bthd', pr, vg.astype(jnp.float32))
    return o, lse


def combine_groups(outs, lses):
    w = jax.nn.softmax(jnp.stack(lses, axis=0), axis=0)
    return jnp.einsum('gbth,gbthd->bthd', w, jnp.stack(outs, axis=0))


def prompt_attention(q, k, v, rel_bias):
    S = q.shape[1]
    outs, lses, bufs = [], [], []
    for g, (win, dil) in enumerate(GROUPS):
        hs = slice(g * HPG, (g + 1) * HPG)
        o, lse = band_dilated_attention(q[:, :, hs], k[:, :, hs], v[:, :, hs], rel_bias[:, hs], dil, win // dil)
        outs.append(o)
        lses.append(lse)
        keep = min(win, S)
        bufs.append(jnp.stack([k[:, S - keep:, hs], v[:, S - keep:, hs]], axis=2))
    return combine_groups(outs, lses), bufs


def sample_attention(q, k, v, rel_bias, kv_caches):
    outs, lses, bufs = [], [], []
    for g, (win, dil) in enumerate(GROUPS):
        hs = slice(g * HPG, (g + 1) * HPG)
        hist = kv_caches[g]
        Lg = hist.shape[1]
        k_all = jnp.concatenate([hist[:, :, 0], k[:, :, hs].astype(hist.dtype)], axis=1)
        v_all = jnp.concatenate([hist[:, :, 1], v[:, :, hs].astype(hist.dtype)], axis=1)
        o, lse = gathered_dilated_attention(q[:, :, hs], k_all, v_all, rel_bias[:, hs], dil, win // dil)
        outs.append(o)
        lses.append(lse)
        n = k_all.shape[1]
        bufs.append(jnp.stack([k_all[:, n - Lg:], v_all[:, n - Lg:]], axis=2))
    return combine_groups(outs, lses), bufs


def conformer_conv(u, hist, conv_w, conv_b, ln_g, ln_b, w_conv_out):
    a, gte = jnp.split(u, 2, axis=-1)
    z = a * jax.nn.sigmoid(gte)
    zc = jnp.concatenate([hist.astype(z.dtype), z], axis=1)
    y = lax.conv_general_dilated(zc, conv_w[:, None, :].astype(z.dtype), window_strides=(1,),
                                 padding='VALID', dimension_numbers=('NWC', 'WIO', 'NWC'),
                                 feature_group_count=D_CONV) + conv_b
    y = layer_norm(y, ln_g, ln_b)
    y = y * jax.nn.sigmoid(y)
    return y @ w_conv_out, zc[:, zc.shape[1] - (CONV_K - 1):]


def decoder_layer(x, p, attn_fn, conv_hist, norm_mix_pre, w_in, conv_w, conv_b, conv_ln_g, conv_ln_b,
                  w_conv_out, w_attn_out, w_out, norm_mix_post, norm_ffn_pre, w_ffn_in, w_ffn_out,
                  norm_ffn_post, ple_norm, w_ple_gate, w_ple_proj):
    B, T, _ = x.shape
    h = rms_norm(x, norm_mix_pre)
    q, k, v, u, gates = jnp.split(h @ w_in, IN_SPLITS, axis=-1)
    q = q.reshape(B, T, N_HEADS, HEAD_DIM)
    k = k.reshape(B, T, N_HEADS, HEAD_DIM)
    v = v.reshape(B, T, N_HEADS, HEAD_DIM)
    o, kv_bufs = attn_fn(q, k, v)
    a = o.reshape(B, T, ATTN_OUT_W).astype(x.dtype) @ w_attn_out
    c, conv_tail = conformer_conv(u, conv_hist, conv_w, conv_b, conv_ln_g, conv_ln_b, w_conv_out)
    g = jax.nn.sigmoid(gates).reshape(B, T, 2, D_MODEL)
    mix = (g[:, :, 0] * a + g[:, :, 1] * c) @ w_out
    x = x + rms_norm(mix, norm_mix_post)
    h2 = rms_norm(x, norm_ffn_pre)
    gg, uu = jnp.split(h2 @ w_ffn_in, 2, axis=-1)
    f = (jax.nn.silu(gg) * uu) @ w_ffn_out
    x = x + rms_norm(f, norm_ffn_post)
    x = x + jax.nn.sigmoid(rms_norm(x, ple_norm) @ w_ple_gate) * (p @ w_ple_proj)
    return x, kv_bufs, conv_tail


def setup_inputs(seed: int = 0) -> dict:
    key = jax.random.key(seed)
    ks = jax.random.split(key, 32)
    f32 = jnp.float32

    def nrm(k, shape, scale):
        return scale * jax.random.normal(k, shape, f32)

    def gain(k, shape):
        return 1.0 + 0.05 * jax.random.normal(k, shape, f32)

    cl = [min(w, PAST_LEN) for (w, _) in GROUPS]
    return {
        'x_prompt': nrm(ks[0], (BATCH, SEQ, D_MODEL), 1.0),
        'x_sample': nrm(ks[1], (DEC_BATCH, DEC_SEQ, D_MODEL), 1.0),
        'p_prompt': nrm(ks[2], (DEPTH, BATCH, SEQ, D_PLE), 1.0),
        'p_sample': nrm(ks[3], (DEPTH, DEC_BATCH, DEC_SEQ, D_PLE), 1.0),
        'cache_kv_w128': nrm(ks[4], (DEPTH, DEC_BATCH, cl[0], 2, HPG, HEAD_DIM), 1.0),
        'cache_kv_w512': nrm(ks[5], (DEPTH, DEC_BATCH, cl[1], 2, HPG, HEAD_DIM), 1.0),
        'cache_kv_w2048': nrm(ks[6], (DEPTH, DEC_BATCH, cl[2], 2, HPG, HEAD_DIM), 1.0),
        'state_conv': nrm(ks[7], (DEPTH, DEC_BATCH, CONV_K - 1, D_CONV), 0.5),
        'rel_bias': nrm(ks[8], (N_BUCKETS, N_HEADS), 0.5),
        'norm_mix_pre': gain(ks[9], (DEPTH, D_MODEL)),
        'w_in': nrm(ks[10], (DEPTH, D_MODEL, IN_W), D_MODEL ** -0.5),
        'conv_w': nrm(ks[11], (DEPTH, CONV_K, D_CONV), CONV_K ** -0.5),
        'conv_b': nrm(ks[12], (DEPTH, D_CONV), 0.02),
        'conv_ln_g': gain(ks[13], (DEPTH, D_CONV)),
        'conv_ln_b': nrm(ks[14], (DEPTH, D_CONV), 0.02),
        'w_conv_out': nrm(ks[15], (DEPTH, D_CONV, D_MODEL), D_CONV ** -0.5),
        'w_attn_out': nrm(ks[16], (DEPTH, ATTN_OUT_W, D_MODEL), ATTN_OUT_W ** -0.5),
        'w_out': nrm(ks[17], (DEPTH, D_MODEL, D_MODEL), D_MODEL ** -0.5),
        'norm_mix_post': gain(ks[18], (DEPTH, D_MODEL)),
        'norm_ffn_pre': gain(ks[19], (DEPTH, D_MODEL)),
        'w_ffn_in': nrm(ks[20], (DEPTH, D_MODEL, 2 * D_FF), D_MODEL ** -0.5),
        'w_ffn_out': nrm(ks[21], (DEPTH, D_FF, D_MODEL), D_FF ** -0.5),
        'norm_ffn_post': gain(ks[22], (DEPTH, D_MODEL)),
        'ple_norm': gain(ks[23], (DEPTH, D_MODEL)),
        'w_ple_gate': nrm(ks[24], (DEPTH, D_MODEL, D_MODEL), D_MODEL ** -0.5),
        'w_ple_proj': nrm(ks[25], (DEPTH, D_PLE, D_MODEL), D_PLE ** -0.5),
    }


def reference(x_prompt, x_sample, p_prompt, p_sample, cache_kv_w128, cache_kv_w512, cache_kv_w2048,
              state_conv, rel_bias, norm_mix_pre, w_in, conv_w, conv_b, conv_ln_g, conv_ln_b, w_conv_out,
              w_attn_out, w_out, norm_mix_post, norm_ffn_pre, w_ffn_in, w_ffn_out, norm_ffn_post,
              ple_norm, w_ple_gate, w_ple_proj):
    caches = (cache_kv_w128, cache_kv_w512, cache_kv_w2048)
    yp, ys = x_prompt, x_sample
    kvp = [[] for _ in GROUPS]
    kvs = [[] for _ in GROUPS]
    convp, convs = [], []
    for i in range(DEPTH):
        lw = (norm_mix_pre[i], w_in[i], conv_w[i], conv_b[i], conv_ln_g[i], conv_ln_b[i], w_conv_out[i],
              w_attn_out[i], w_out[i], norm_mix_post[i], norm_ffn_pre[i], w_ffn_in[i], w_ffn_out[i],
              norm_ffn_post[i], ple_norm[i], w_ple_gate[i], w_ple_proj[i])
        prompt_fn = functools.partial(prompt_attention, rel_bias=rel_bias)
        sample_fn = functools.partial(sample_attention, rel_bias=rel_bias, kv_caches=[c[i] for c in caches])
        zero_hist = jnp.zeros((yp.shape[0], CONV_K - 1, D_CONV), yp.dtype)
        yp, bp, cp = decoder_layer(yp, p_prompt[i], prompt_fn, zero_hist, *lw)
        ys, bs, cs = decoder_layer(ys, p_sample[i], sample_fn, state_conv[i], *lw)
        for g in range(len(GROUPS)):
            kvp[g].append(bp[g])
            kvs[g].append(bs[g])
        convp.append(cp)
        convs.append(cs)
    kv128_p = jnp.stack(kvp[0], axis=0)
    kv512_p = jnp.stack(kvp[1], axis=0)
    kv2048_p = jnp.stack(kvp[2], axis=0)
    conv_p = jnp.stack(convp, axis=0)
    kv128_s = jnp.stack(kvs[0], axis=0)
    kv512_s = jnp.stack(kvs[1], axis=0)
    kv2048_s = jnp.stack(kvs[2], axis=0)
    conv_s = jnp.stack(convs, axis=0)
    return (yp, ys, kv128_p, kv512_p, kv2048_p, conv_p, kv128_s, kv512_s, kv2048_s, conv_s)
```

```python
import numpy as np
import concourse.bass as bass
import concourse.mybir as mybir
from concourse.bass_utils import run_bass_kernel_spmd
from contextlib import ExitStack
import types

F32 = mybir.dt.float32
BF16 = mybir.dt.bfloat16
AF = mybir.ActivationFunctionType
ALU = mybir.AluOpType
AX = mybir.AxisListType

D = 1024
NT = 2048
NS = 16
DFF = 2816
NFF = 22
EPS = 1e-6
SCALE = 0.125
GROUPS = ((128, 1), (512, 4), (2048, 16))
ENGS = ("pe", "act", "dve", "pool", "sp")


class _Op:
    __slots__ = ("eng", "fn", "deps", "dma", "chan", "ndma", "cum", "cidx")

    def __init__(self, eng, fn, dma, chan):
        self.eng, self.fn, self.dma, self.chan = eng, fn, dma, chan
        self.deps = []
        self.ndma = 0
        self.cum = 0
        self.cidx = 0


def _freeze(fn, depth=0):
    if not isinstance(fn, types.FunctionType) or depth > 6:
        return fn
    dfl = fn.__defaults__
    if dfl:
        dfl = tuple(_freeze(d, depth + 1) if isinstance(d, types.FunctionType) else d for d in dfl)
    if fn.__closure__ is None:
        if dfl is fn.__defaults__:
            return fn
        return types.FunctionType(fn.__code__, fn.__globals__, fn.__name__, dfl, None)
    cells = []
    for c in fn.__closure__:
        try:
            v = c.cell_contents
        except ValueError:
            cells.append(c)
            continue
        if isinstance(v, types.FunctionType) and v is not fn:
            v = _freeze(v, depth + 1)
        cells.append(types.CellType(v))
    return types.FunctionType(fn.__code__, fn.__globals__, fn.__name__, dfl, tuple(cells))


def _summ(ops):
    best = {}
    for o in ops:
        if o.dma:
            k = ("c", o.chan)
            v = o.cum
        else:
            k = ("e", o.eng)
            v = o.cidx
        b = best.get(k)
        if b is None or v > b[0]:
            best[k] = (v, o)
    return [b[1] for b in best.values()]


class Prog:
    def __init__(self, nc):
        self.nc = nc
        self.ops = {e: [] for e in ENGS}
        self.state = {}
        self.by_name = {}
        self.ghost_ops = {}
        self.chan_cnt = {}
        self.chan_last = {}
        self.misc_rr = 0
        self.ccount = {e: 0 for e in ENGS}

    def _st(self, k):
        s = self.state.get(k)
        if s is None:
            s = [None, []]
            if isinstance(k, tuple):
                g = self.ghost_ops.get(k[0])
                if g:
                    s[1] = list(g)
                self.by_name.setdefault(k[0], []).append(k)
            self.state[k] = s
        return s

    def retire(self, name):
        ops = []
        for k in self.by_name.pop(name, []):
            s = self.state.pop(k)
            if s[0] is not None:
                ops.append(s[0])
            ops.extend(s[1])
        ops.extend(self.ghost_ops.pop(name, []))
        return _summ(ops)

    def add(self, eng, fn, reads=(), writes=(), dma=False, chan=None, ndma=1, extra=()):
        if dma and chan == "misc":
            self.misc_rr += 1
            chan = ("misc", self.misc_rr % 4)
        op = _Op(eng, _freeze(fn), dma, chan)
        deps = {}
        for o in extra:
            if o is not None:
                deps[id(o)] = o
        if dma:
            prev = self.chan_last.get(chan)
            if prev is not None:
                deps[id(prev)] = prev
            self.chan_last[chan] = op
        for k in reads:
            s = self._st(k)
            if s[0] is not None:
                deps[id(s[0])] = s[0]
        for k in writes:
            s = self._st(k)
            if s[0] is not None:
                deps[id(s[0])] = s[0]
            for r in s[1]:
                deps[id(r)] = r
        op.deps = _summ(deps.values())
        if dma:
            assert chan is not None
            op.ndma = ndma
            c = self.chan_cnt.get(chan, 0) + ndma
            self.chan_cnt[chan] = c
            op.cum = c
        else:
            self.ccount[eng] += 1
            op.cidx = self.ccount[eng]
        for k in reads:
            s = self._st(k)
            s[1].append(op)
            if len(s[1]) > 24:
                s[1] = _summ(s[1])
        for k in writes:
            s = self._st(k)
            s[0] = op
            s[1] = []
        self.ops[eng].append(op)
        return op

    def emit(self, stack):
        nc = self.nc
        sems = {e: stack.enter_context(nc.semaphore("s_" + e)) for e in ENGS}
        chan_sems = {}
        for i, c in enumerate(self.chan_cnt):
            chan_sems[c] = stack.enter_context(nc.semaphore("c%d" % i))
        block = stack.enter_context(nc.Block())
        engobj = {"pe": block.tensor, "act": block.scalar, "dve": block.vector,
                  "pool": block.gpsimd, "sp": block.sync}
        prog = self

        def make(ename):
            def body(eng):
                waited = {}
                for op in prog.ops[ename]:
                    for d in op.deps:
                        if d.dma:
                            sem, val, key = chan_sems[d.chan], 16 * d.cum, ("c", d.chan)
                        else:
                            if d.eng == ename and ename == "pe":
                                continue
                            sem, val, key = sems[d.eng], d.cidx, ("e", d.eng)
                        if waited.get(key, 0) >= val:
                            continue
                        waited[key] = val
                        eng.wait_ge(sem, val)
                    res = op.fn(eng)
                    if op.dma:
                        if not isinstance(res, (list, tuple)):
                            res = [res]
                        assert len(res) == op.ndma, (len(res), op.ndma)
                        for r in res:
                            r.then_inc(chan_sems[op.chan], 16)
                    else:
                        if isinstance(res, (list, tuple)):
                            res = res[-1]
                        res.then_inc(sems[ename], 1)
                if ename == "sp":
                    for c, n in prog.chan_cnt.items():
                        eng.wait_ge(chan_sems[c], 16 * n)
            return body

        for e in ENGS:
            engobj[e](make(e))


class Arena:
    def __init__(self, prog, tensor, nbytes):
        self.P, self.t, self.n = prog, tensor, nbytes
        self.free = [(0, nbytes)]
        self.bufs = {}
        self.ghosts = []
        self.uid = 0
        self.peak = 0

    def alloc(self, base, nbytes):
        nbytes = (nbytes + 63) // 64 * 64
        self.uid += 1
        name = "%s#%d" % (base, self.uid)
        for i, (o, s) in enumerate(self.free):
            if s >= nbytes:
                if s == nbytes:
                    self.free.pop(i)
                else:
                    self.free[i] = (o + nbytes, s - nbytes)
                off = o
                break
        else:
            raise RuntimeError("arena full: %s %d free=%s" % (base, nbytes, self.free))
        self.bufs[name] = (off, nbytes)
        self.peak = max(self.peak, off + nbytes)
        ops = []
        for (go, gs, gops) in self.ghosts:
            if go < off + nbytes and off < go + gs:
                ops.extend(gops)
        if ops:
            self.P.ghost_ops[name] = _summ(ops)
        return name, off

    def release(self, name):
        off, nbytes = self.bufs.pop(name)
        ops = self.P.retire(name)
        self.ghosts = [g for g in self.ghosts if not (g[0] >= off and g[0] + g[1] <= off + nbytes)]
        if ops:
            self.ghosts.append((off, nbytes, ops))
        self.free.append((off, nbytes))
        self.free.sort()
        m = []
        for o, s in self.free:
            if m and m[-1][0] + m[-1][1] == o:
                m[-1] = (m[-1][0], m[-1][1] + s)
            else:
                m.append((o, s))
        self.free = m

    def f32(self, off, n):
        return self.t[:, off // 4: off // 4 + n]

    def bf16(self, off, n):
        return self.t[:, off // 4: off // 4 + (n + 1) // 2].bitcast(BF16)


class Buf:
    def __init__(self, ar, base, shape, dt):
        self.ar = ar
        n = int(np.prod(shape))
        self.shape = shape
        self.dt = dt
        self.name, self.off = ar.alloc(base, n * (4 if dt == F32 else 2))
        flat = ar.f32(self.off, n) if dt == F32 else ar.bf16(self.off, n)
        if len(shape) == 1:
            self.ap = flat
        elif len(shape) == 2:
            self.ap = flat.rearrange("p (a b) -> p a b", a=shape[0])
        elif len(shape) == 3:
            self.ap = flat.rearrange("p (a b c) -> p a b c", a=shape[0], b=shape[1])
        else:
            self.ap = flat.rearrange("p (a b c d) -> p a b c d", a=shape[0], b=shape[1], c=shape[2])

    def k(self, *idx):
        return (self.name,) + idx

    def free(self):
        self.ar.release(self.name)


DEBUG = False


def build_program():
    nc = bass.Bass("TRN2", target_bir_lowering=False)

    def din(name, shape):
        return nc.dram_tensor(name, list(shape), F32, kind="ExternalInput").ap()

    def dout(name, shape):
        return nc.dram_tensor(name, list(shape), F32, kind="ExternalOutput").ap()

    xo = din("xo", [NT, D]); xh = din("xh", [NT, D]); po = din("po", [NT, 256])
    xs = din("xs", [NS, D]); psm = din("psm", [NS, 256])
    cch = [din("c128", [NS, 128, 512]), din("c512", [NS, 512, 512]), din("c2048", [NS, 2048, 512])]
    sconv = din("sconv", [NS, 30, D])
    hv = din("hv", [128, 1])
    ebsrc = din("ebsrc", [128, 12 * 2 * 128])
    sbias = din("sbias", [128, 12]); sbias0 = din("sbias0", [NS, 12])
    c_selB = din("c_selB", [NS, NS * 128]); c_selT = din("c_selT", [128, NS * NS])
    w_in = din("w_in", [D, 6400]); conv_w = din("conv_w", [31, D])
    vecs = din("vecs", [8, D])
    w_conv_out = din("w_conv_out", [D, D]); w_attn_out = din("w_attn_out", [256, D]); w_out = din("w_out", [D, D])
    w_ffn_in = din("w_ffn_in", [D, 2 * DFF]); w_ffn_out = din("w_ffn_out", [DFF, D])
    w_ple_gate = din("w_ple_gate", [D, D]); w_ple_proj = din("w_ple_proj", [256, D])

    y = dout("y", [NT, D]); ysm = dout("ysm", [NS, D])
    kvo = [dout("kv128o", [128, 512]), dout("kv512o", [512, 512]), dout("kv2048o", [2048, 512])]
    convo = dout("convo", [30, D])
    kvs = [dout("kv128s", [NS, 128, 512]), dout("kv512s", [NS, 512, 512]), dout("kv2048s", [NS, 2048, 512])]
    convs = dout("convs", [NS, 30, D])

    wscr = nc.dram_tensor("wscr", [16, 128, 8192], BF16, kind="Internal").ap()
    gscr = nc.dram_tensor("gscr", [5, 128, D], F32, kind="Internal").ap()

    st = ExitStack()
    with st:
        ARENA_BYTES = 206 * 1024
        arena_t = st.enter_context(nc.sbuf_tensor("arena", [128, ARENA_BYTES // 4], F32))
        psum = st.enter_context(nc.psum_tensor("psum", [128, 4096], F32))
        P = Prog(nc)
        AR = Arena(P, arena_t, ARENA_BYTES)

        def B(base, shape, dt=F32):
            return Buf(AR, base, shape, dt)

        def dbg(name, ap2d, rows, cols, keys):
            if not DEBUG:
                return
            tt = nc.dram_tensor("dbg_" + name, [rows, cols], F32, kind="ExternalOutput").ap()
            P.add("pool", lambda e: e.dma_start(out=tt, in_=ap2d), reads=keys, writes=[("dram", "dbg", name)], dma=True, chan="dbg")

        def ps(i, n=512, o=0):
            return psum[:, 512 * i + o: 512 * i + o + n]

        def psk(*banks):
            return [("ps", b) for b in banks]

        def psbf(i):
            return psum[:, 512 * i: 512 * (i + 1)].bitcast(BF16)

        rr = {}

        def rot(name, n):
            v = rr.get(name, 0)
            rr[name] = v + 1
            return v % n

        cst = B("cst", [128 + 128 + 64], BF16)
        ident = cst.ap[:, 0:128]
        ones_bf = cst.ap[:, 128:256]
        identf = B("identf", [128], F32)
        epsb = B("eps", [1], F32)
        hvb = B("hvb", [1], F32)
        vcol = B("vcol", [3, 8], F32)
        wT = B("wT", [8, 31], F32)

        P.add("pool", lambda e: e.memset(identf.ap, 0.0), writes=[identf.k()])
        P.add("pool", lambda e: e.affine_select(out=identf.ap, in_=identf.ap, pattern=[[-1, 128]], compare_op=ALU.not_equal,
                                                fill=1.0, base=0, channel_multiplier=1), writes=[identf.k()])
        P.add("dve", lambda e: e.tensor_copy(out=ident, in_=identf.ap), reads=[identf.k()], writes=[cst.k("i")])
        P.add("dve", lambda e: e.memset(cst.ap[:, 128:320], 1.0), writes=[cst.k("o")])
        P.add("dve", lambda e: e.memset(epsb.ap, EPS), writes=[epsb.k()])
        P.add("sp", lambda e: e.dma_start(out=hvb.ap, in_=hv), writes=[hvb.k()], dma=True, chan="misc")

        bulk = []
        def _rows(dst, src, b_, r0, r1):
            n = r1 - r0
            n8 = n - n % 8
            if n8:
                bulk.append((dst[b_, r0:r0 + n8, :].rearrange("(a l) c -> a (l c)", l=8), src[b_, r0 + 1:r0 + 1 + n8, :].rearrange("(a l) c -> a (l c)", l=8)))
            if n % 8:
                bulk.append((dst[b_, r0 + n8:r1, :], src[b_, r0 + 1 + n8:r1 + 1, :]))
        for b_ in range(NS):
            for (r0, r1) in ((0, 512), (512, 1024), (1024, 1536), (1536, 2047)):
                _rows(kvs[2], cch[2], b_, r0, r1)
        for b_ in range(NS):
            _rows(kvs[1], cch[1], b_, 0, 511)
        for j in range(4):
            bulk.append((kvs[0][4 * j:4 * j + 4, 0:127, :].rearrange("b l c -> b (l c)"), cch[0][4 * j:4 * j + 4, 1:128, :].rearrange("b l c -> b (l c)")))
        for j in range(2):
            bulk.append((convs[8 * j:8 * j + 8, 0:29, :].rearrange("b l c -> b (l c)"), sconv[8 * j:8 * j + 8, 1:30, :].rearrange("b l c -> b (l c)")))
        bulk_i = {"i": 0}

        def bulk_step(n=1):
            for _ in range(n):
                i = bulk_i["i"]
                if i >= len(bulk):
                    return
                bulk_i["i"] = i + 1
                o_, i_ = bulk[i]
                P.add("act", lambda e, o_=o_, i_=i_: e.dma_start(out=o_, in_=i_), reads=[], writes=[("dram", "bulk", i)],
                      dma=True, chan=("cpy", i % 2))

        vrow = B("vrow", [D], F32)
        P.add("sp", lambda e: e.dma_start(out=vrow.ap[0:3, :], in_=vecs[5:8, :]), writes=[vrow.k()], dma=True, chan="misc")

        def tr_vc(e):
            r = None
            for kc in range(8):
                r = e.transpose(out=ps(1, 3, 3 * kc), in_=vrow.ap[0:3, kc * 128:(kc + 1) * 128], identity=identf.ap[0:3, 0:3])
            return r
        P.add("pe", tr_vc, reads=[vrow.k(), identf.k()], writes=psk(1))
        P.add("act", lambda e: e.activation(out=vcol.ap.rearrange("p v k -> p k v"), in_=ps(1, 24).rearrange("p (k v) -> p k v", v=3), func=AF.Copy),
              reads=[], writes=psk(1) + [vcol.k()])
        vrow.free()

        cw = B("cw", [D], F32)
        P.add("sp", lambda e: e.dma_start(out=cw.ap[0:31, :], in_=conv_w), writes=[cw.k()], dma=True, chan="misc")

        def tr_cw(e):
            r = None
            for kc in range(8):
                r = e.transpose(out=ps(0, 31, 31 * kc), in_=cw.ap[0:31, kc * 128:(kc + 1) * 128], identity=identf.ap[0:31, 0:31])
            return r
        P.add("pe", tr_cw, reads=[cw.k(), identf.k()], writes=psk(0))
        P.add("act", lambda e: e.activation(out=wT.ap.rearrange("p a b -> p (a b)"), in_=ps(0, 248), func=AF.Copy),
              reads=[], writes=psk(0) + [wT.k()])
        cw.free()

        onesf = B("onesf", [128], F32)
        grow = B("grow", [5, D], F32)
        gtmp = B("gtmp", [D], F32)
        P.add("dve", lambda e: e.memset(onesf.ap, 1.0), writes=[onesf.k()])
        for r_ in range(5):
            P.add("sp", lambda e, r_=r_: e.dma_start(out=grow.ap[0:1, r_, :], in_=vecs[r_:r_ + 1, :]), writes=[grow.k(r_)], dma=True, chan="misc")

            def gb_mm(e, r_=r_):
                e.matmul(out=ps(2), lhsT=onesf.ap[0:1, :], rhs=grow.ap[0:1, r_, 0:512], start=True, stop=True)
                return e.matmul(out=ps(3), lhsT=onesf.ap[0:1, :], rhs=grow.ap[0:1, r_, 512:1024], start=True, stop=True)
            P.add("pe", gb_mm, reads=[onesf.k(), grow.k(r_)], writes=psk(2, 3))
            P.add("act", lambda e: e.activation(out=gtmp.ap, in_=psum[:, 1024:2048], func=AF.Copy), reads=[], writes=psk(2, 3) + [gtmp.k()])
            P.add("sp", lambda e, r_=r_: e.dma_start(out=gscr[r_], in_=gtmp.ap), reads=[gtmp.k()], writes=[("dram", "gscr", r_)], dma=True, chan="misc")
        onesf.free(); grow.free(); gtmp.free()

        def load_gain(row, name):
            gb = B(name, [D], F32)
            P.add("sp", lambda e: e.dma_start(out=gb.ap, in_=gscr[row]), reads=[("dram", "gscr", row)],
                  writes=[gb.k()], dma=True, chan="misc")
            return gb

        small = B("small", [3, 4], F32)

        def rms_to_bf16(src_ap, src_keys, M, gain, hb_ap, hb_key, src_is_psum=False):
            i = rot("small", 4)
            ms = small.ap[0:M, 0, i:i + 1]; sd = small.ap[0:M, 1, i:i + 1]; rs = small.ap[0:M, 2, i:i + 1]
            rk = src_keys if not src_is_psum else []
            wk = src_keys if src_is_psum else []
            P.add("act", lambda e: e.activation(out=junk.ap[0:M, :], in_=src_ap, func=AF.Square, scale=1.0 / 32.0, accum_out=ms),
                  reads=rk, writes=wk + [small.k(0, i)])
            P.add("act", lambda e: e.activation(out=sd, in_=ms, func=AF.Sqrt, bias=epsb.ap[0:M, 0:1], scale=1.0),
                  reads=[small.k(0, i), epsb.k()], writes=[small.k(1, i)])
            P.add("dve", lambda e: e.reciprocal(out=rs, in_=sd), reads=[small.k(1, i)], writes=[small.k(2, i)])
            P.add("dve", lambda e: e.scalar_tensor_tensor(out=hb_ap, in0=src_ap, scalar=rs, in1=gain.ap[0:M, :], op0=ALU.mult, op1=ALU.mult),
                  reads=rk + [small.k(2, i), gain.k()], writes=wk + [hb_key])

        def transpose_rows(hb_ap, hb_key, M, nchunk, dst_fn, dst_keys, bank, evac="act"):
            pb = psbf(bank)

            def f(e):
                r = None
                for kc in range(nchunk):
                    r = e.transpose(out=pb[:, kc * M:(kc + 1) * M], in_=hb_ap[:, kc * 128:(kc + 1) * 128], identity=ident[0:M, 0:M])
                return r
            P.add("pe", f, reads=[hb_key, cst.k("i")], writes=psk(bank))
            src = pb[:, 0:nchunk * M].rearrange("p (a b) -> p a b", a=nchunk)
            if evac == "act":
                P.add("act", lambda e: e.activation(out=dst_fn(), in_=src, func=AF.Copy), reads=[], writes=psk(bank) + dst_keys)
            else:
                P.add("dve", lambda e: e.tensor_copy(out=dst_fn(), in_=src), reads=[], writes=psk(bank) + dst_keys)

        def load_w_own(name, src2d, nchunk, ncol, chan):
            wb = B(name, [nchunk, ncol], BF16)
            P.add("pool", lambda e: e.dma_start(out=wb.ap, in_=src2d.rearrange("(kc p) c -> p kc c", p=128)),
                  writes=[wb.k()], dma=True, chan=chan)
            return wb

        wring = []
        wr_i = {"i": 0}

        class WView:
            def __init__(self, slot, ap):
                self.slot, self.ap = slot, ap

            def k(self, *a):
                return self.slot.k()

            def free(self):
                pass

        def wslot():
            sl = wring[wr_i["i"] % len(wring)]
            wr_i["i"] += 1
            return sl

        def add_wslot():
            sl = B("wr%d" % len(wring), [8192], BF16)
            sl.idx = len(wring)
            wring.append(sl)

        wl = {"tile": None, "li": 0}

        def ring_load(sl, issue_fn, ndma, tile="cur", li=None):
            if tile == "cur":
                tile = wl["tile"]
            if tile is not None and li is None:
                li = wl["li"]
                wl["li"] += 1
            if tile is None or tile == 0:
                P.add("pool", issue_fn, writes=[sl.k()], dma=True, chan=("w", sl.idx), ndma=ndma)
                if tile == 0:
                    P.add("act", lambda e: e.dma_start(out=wscr[li], in_=sl.ap), reads=[sl.k()], writes=[("dram", "wscr", li)],
                          dma=True, chan=("wst", li % 2))
            else:
                P.add("pool", lambda e: e.dma_start(out=sl.ap, in_=wscr[li]), reads=[("dram", "wscr", li)], writes=[sl.k()],
                      dma=True, chan=("w", sl.idx))

        def load_w_pair(srcA, srcB, ncol, tile="cur", li=None):
            sl = wslot()
            ap4 = sl.ap[:, 0:16 * ncol].rearrange("p (a b c) -> p a b c", a=8, b=2)

            def ld(e):
                a_ = e.dma_start(out=ap4[:, :, 0, :], in_=srcA.rearrange("(kc p) c -> p kc c", p=128))
                b_ = e.dma_start(out=ap4[:, :, 1, :], in_=srcB.rearrange("(kc p) c -> p kc c", p=128))
                return [a_, b_]
            ring_load(sl, ld, 2, tile, li)
            return WView(sl, ap4[:, :, 0, :]), WView(sl, ap4[:, :, 1, :])

        def load_w(name, src2d, nchunk, ncol, chan=None):
            sl = wslot()
            ap = sl.ap[:, 0:nchunk * ncol].rearrange("p (a b) -> p a b", a=nchunk)
            ring_load(sl, lambda e: e.dma_start(out=ap, in_=src2d.rearrange("(kc p) c -> p kc c", p=128)), 1)
            return WView(sl, ap)

        junk = B("junk", [D], F32)

        g_pre = load_gain(0, "g_pre")
        hT = B("hT", [8, NT], BF16)
        hTh = B("hTh", [8, NT], BF16)
        hsT = B("hsT", [8, NS], BF16)
        hTh30 = B("hTh30", [8, 32], BF16)
        xin = [B("xin%d" % i, [D], F32) for i in range(3)]
        hbb = [B("hbb%d" % i, [D], BF16) for i in range(2)]

        hbb.append(B("hbb2", [D], BF16))
        p1 = []
        for blk in range(16):
            p1.append((xh[blk * 128:(blk + 1) * 128, :], 128, (lambda blk=blk: hTh.ap[:, :, blk * 128:(blk + 1) * 128]), [hTh.k(blk)]))
        p1.append((xs, NS, (lambda: hsT.ap), [hsT.k()]))
        for blk in range(16):
            p1.append((xo[blk * 128:(blk + 1) * 128, :], 128, (lambda blk=blk: hT.ap[:, :, blk * 128:(blk + 1) * 128]), [hT.k(blk)]))
        p1hb = {}

        def p1_a(ix):
            src_rows, M, dst_fn, dst_keys = p1[ix]
            i = rot("xin", len(xin))
            xb = xin[i]
            P.add("sp", lambda e: e.dma_start(out=xb.ap[0:M, :], in_=src_rows), writes=[xb.k()], dma=True, chan=("xin", i))
            hb = hbb[rot("hbb", len(hbb))]
            p1hb[ix] = hb
            rms_to_bf16(xb.ap[0:M, :], [xb.k()], M, g_pre, hb.ap[0:M, :], hb.k())

        def p1_b(ix):
            src_rows, M, dst_fn, dst_keys = p1[ix]
            hb = p1hb[ix]
            bank = rot("p1bank", 2)
            transpose_rows(hb.ap[0:M, :], hb.k(), M, 8, dst_fn, dst_keys, bank, evac=("act" if bank == 0 else "dve"))
            if ix == 15:
                P.add("dve", lambda e: e.tensor_copy(out=hTh30.ap[:, :, 0:30], in_=hTh.ap[:, :, NT - 30:NT]), reads=[hTh.k(15)], writes=[hTh30.k()])

        for ix in range(len(p1) + 1):
            if ix < len(p1):
                p1_a(ix)
            if ix > 0:
                p1_b(ix - 1)
        g_pre.free()
        while xin:
            xin.pop().free()
        hTh_all = [hTh.k(b) for b in range(16)]
        hT_all = [hT.k(b) for b in range(16)]

        def blk_cols(g, r, qb):
            dil = GROUPS[g][1]
            s0 = r + dil * 128 * qb
            return slice(s0, s0 + dil * 127 + 1, dil)

        add_wslot(); add_wslot()
        kTh = [B("kTh0", [2, 128], BF16), B("kTh1", [2, 512], BF16), B("kTh2", [2, NT], BF16)]
        vh = [B("vh0", [1, 256], BF16), B("vh1", [4, 256], BF16), B("vh2", [16, 256], BF16)]
        halo_lo = [NT - 128, NT - 512, 0]
        evt = {"n": 0}

        def evac_copy(out_ap, in_ap, reads, writes, scale=None):
            evt["n"] += 1
            if evt["n"] % 2 == 0:
                if scale is None:
                    P.add("act", lambda e: e.activation(out=out_ap, in_=in_ap, func=AF.Copy), reads=reads, writes=writes)
                else:
                    P.add("act", lambda e: e.activation(out=out_ap, in_=in_ap, func=AF.Copy, scale=scale), reads=reads, writes=writes)
            else:
                if scale is None:
                    P.add("dve", lambda e: e.tensor_copy(out=out_ap, in_=in_ap), reads=reads, writes=writes)
                else:
                    P.add("dve", lambda e: e.tensor_scalar(out=out_ap, in0=in_ap, scalar1=scale, scalar2=None, op0=ALU.mult), reads=reads, writes=writes)

        def proj_fm(wb, wcol0, src_ap_fn, src_keys, ncols, dst_ap, dst_keys, scale=None):
            bank = rot("pbank", 4)

            def f(e):
                r = None
                for kc in range(8):
                    r = e.matmul(out=ps(bank, ncols), lhsT=wb.ap[:, kc, wcol0:wcol0 + 128], rhs=src_ap_fn(kc), start=(kc == 0), stop=(kc == 7))
                return r
            P.add("pe", f, reads=[wb.k()] + src_keys, writes=psk(bank))
            evac_copy(dst_ap, ps(bank, ncols), [], psk(bank) + dst_keys, scale)

        for g in range(3):
            wq = load_w("wqkvh", w_in[:, 768 * g:768 * g + 768], 8, 768, ("w", rot("wslot", 3)))
            lo = halo_lo[g]
            nh = NT - lo
            for c2 in range(2):
                for t0 in range(0, nh, 512):
                    n = min(512, nh - t0)
                    proj_fm(wq, 256 + 128 * c2, (lambda kc, a=lo + t0, n=n: hTh.ap[:, kc, a:a + n]), hTh_all, n,
                            kTh[g].ap[:, c2, t0:t0 + n], [kTh[g].k(c2, t0)])
            nres = GROUPS[g][1]
            for r in range(nres):
                cols = blk_cols(g, r, {0: 15, 1: 3, 2: 0}[g])
                bank = rot("pbank", 4)

                def f(e, cols=cols, bank=bank, wq=wq):
                    rr_ = None
                    for kc in range(8):
                        rr_ = e.matmul(out=ps(bank, 256), lhsT=hTh.ap[:, kc, cols], rhs=wq.ap[:, kc, 512:768], start=(kc == 0), stop=(kc == 7))
                    return rr_
                P.add("pe", f, reads=[wq.k()] + hTh_all, writes=psk(bank))
                evac_copy(vh[g].ap[:, r, :], ps(bank, 256), [], psk(bank) + [vh[g].k(r)])
            wq.free()
        hTh.free()

        numacc = B("numacc", [2, NT], F32)
        denacc = B("denacc", [2, NT], F32)
        P.add("dve", lambda e: e.memset(numacc.ap, 0.0), writes=[numacc.k()])
        P.add("dve", lambda e: e.memset(denacc.ap, 0.0), writes=[denacc.k()])
        oT = B("oT", [2, NT], BF16)
        qs_s = B("qs_s", [3, 768], F32)
        ebuf = [B("ebuf%d" % i, [256], F32) for i in range(4)]
        pbuf = [B("pbuf%d" % i, [256], BF16) for i in range(4)]
        kvst = [B("kvst%d" % i, [512], F32) for i in range(2)]

        for g in range(3):
            win, dil = GROUPS[g]
            wq = load_w("wqkv", w_in[:, 768 * g:768 * g + 768], 8, 768, ("w", rot("wslot", 3)))
            ebs = B("ebs", [4, 256], F32)
            P.add("sp", lambda e, g=g: e.dma_start(out=ebs.ap.rearrange("p a b -> p (a b)"), in_=ebsrc[:, g * 1024:(g + 1) * 1024]),
                  writes=[ebs.k()], dma=True, chan="misc")
            EBn = B("EBn", [4, 256], F32)
            EBf = B("EBf", [4, 256], F32)
            P.add("act", lambda e: e.activation(out=EBn.ap, in_=ebs.ap, func=AF.Exp), reads=[ebs.k()], writes=[EBn.k()])
            P.add("dve", lambda e: e.tensor_copy(out=EBf.ap[:, :, 128:256], in_=EBn.ap[:, :, 128:256]), reads=[EBn.k()], writes=[EBf.k("c")])
            P.add("dve", lambda e: e.tensor_scalar(out=EBf.ap[:, :, 0:128], in0=EBn.ap[:, :, 0:128], scalar1=hvb.ap[:, 0:1], scalar2=None, op0=ALU.mult),
                  reads=[EBn.k(), hvb.k()], writes=[EBf.k("p")])
            qT = B("qT", [2, NT], BF16)
            kT = B("kT", [2, NT], BF16)
            vo = B("vo", [16, 256], BF16)
            for c2 in range(2):
                for t in range(4):
                    proj_fm(wq, 128 * c2, (lambda kc, t=t: hT.ap[:, kc, t * 512:(t + 1) * 512]), hT_all, 512,
                            qT.ap[:, c2, t * 512:(t + 1) * 512], [qT.k(c2, t)], scale=SCALE)
                    proj_fm(wq, 256 + 128 * c2, (lambda kc, t=t: hT.ap[:, kc, t * 512:(t + 1) * 512]), hT_all, 512,
                            kT.ap[:, c2, t * 512:(t + 1) * 512], [kT.k(c2, t)])
            qT_all = [qT.k(c2, t) for c2 in range(2) for t in range(4)]
            kT_all = [kT.k(c2, t) for c2 in range(2) for t in range(4)]
            nq = 16 // dil if dil < 16 else 1
            nres = dil
            for r in range(nres):
                for qb in range(nq):
                    bi = r * nq + qb
                    need_out = (qb == nq - 1)
                    cols = blk_cols(g, r, qb)
                    bank = rot("pbank", 4)
                    c0 = 256 if need_out else 512
                    ncol = 768 - c0

                    def f(e, cols=cols, bank=bank, c0=c0, ncol=ncol, wq=wq):
                        rr_ = None
                        for kc in range(8):
                            rr_ = e.matmul(out=ps(bank, ncol), lhsT=hT.ap[:, kc, cols], rhs=wq.ap[:, kc, c0:768], start=(kc == 0), stop=(kc == 7))
                        return rr_
                    P.add("pe", f, reads=[wq.k()] + hT_all, writes=psk(bank))
                    if need_out:
                        si = rot("kvst", 2)
                        sb_ = kvst[si]
                        P.add("act", lambda e, sb_=sb_, bank=bank: e.activation(out=sb_.ap, in_=ps(bank, 512), func=AF.Copy), reads=[], writes=psk(bank) + [sb_.k()])
                        P.add("dve", lambda e, bi=bi, bank=bank: e.tensor_copy(out=vo.ap[:, bi, :], in_=ps(bank, 256, 256)), reads=[], writes=psk(bank) + [vo.k(bi)])
                        P.add("sp", lambda e, sb_=sb_, r=r, g=g, dil=dil: e.dma_start(out=kvo[g][r::dil, :] if dil > 1 else kvo[g], in_=sb_.ap),
                              reads=[sb_.k()], writes=[("dram", "kvo", g, r)], dma=True, chan=("kvst", si))
                    else:
                        evac_copy(vo.ap[:, bi, :], ps(bank, 256), [], psk(bank) + [vo.k(bi)])
            bank = rot("pbank", 4)
            b2 = rot("pbank", 4)

            def fs(e, bank=bank, b2=b2, wq=wq):
                rr_ = None
                for kc in range(8):
                    e.matmul(out=psum[0:NS, 512 * bank:512 * bank + 512], lhsT=hsT.ap[:, kc, :], rhs=wq.ap[:, kc, 0:512], start=(kc == 0), stop=(kc == 7))
                for kc in range(8):
                    rr_ = e.matmul(out=psum[0:NS, 512 * b2:512 * b2 + 256], lhsT=hsT.ap[:, kc, :], rhs=wq.ap[:, kc, 512:768], start=(kc == 0), stop=(kc == 7))
                return rr_
            P.add("pe", fs, reads=[wq.k(), hsT.k()], writes=psk(bank, b2))
            P.add("act", lambda e, g=g, bank=bank: e.activation(out=qs_s.ap[0:NS, g, 0:512], in_=psum[0:NS, 512 * bank:512 * bank + 512], func=AF.Copy),
                  reads=[], writes=psk(bank) + [qs_s.k(g, 0)])
            P.add("act", lambda e, g=g, b2=b2: e.activation(out=qs_s.ap[0:NS, g, 512:768], in_=psum[0:NS, 512 * b2:512 * b2 + 256], func=AF.Copy),
                  reads=[], writes=psk(b2) + [qs_s.k(g, 1)])
            P.add("sp", lambda e, g=g, win=win: e.dma_start(out=kvs[g][:, win - 1, :], in_=qs_s.ap[0:NS, g, 256:768]),
                  reads=[qs_s.k(g, 0), qs_s.k(g, 1)], writes=[("dram", "kvs_new", g)], dma=True, chan="misc")
            wq.free()

            vo_all = [vo.k(b) for b in range(16)]
            for c2 in range(2):
                for quad in range(4):
                    if g == 0:
                        qblocks = [(0, 4 * quad + j) for j in range(4)]
                    elif g == 1:
                        qblocks = [(quad, j) for j in range(4)]
                    else:
                        qblocks = [(4 * quad + j, 0) for j in range(4)]
                    nb_, db_ = ((6, 7) if rot("ndbank", 2) == 0 else (2, 3))
                    units = [(hh, j, r, qb) for hh in range(2) for j, (r, qb) in enumerate(qblocks)]
                    ust = {}

                    def att_a(u):
                        hh, j, r, qb = units[u]
                        s_ = 2 * c2 + hh
                        pr = slice(64 * hh, 64 * hh + 64)
                        bi = r * nq + qb
                        qcols = blk_cols(g, r, qb)
                        sbank = (4, 5, 0, 1)[rot("sbank", 4)]
                        first = (qb == 0)
                        if first:
                            hcols = blk_cols(g, r, {0: 15, 1: 3, 2: 0}[g])
                            hc = slice(hcols.start - halo_lo[g], hcols.stop - halo_lo[g], hcols.step)
                            kprev = kTh[g].ap[pr, c2, hc]
                            vprev = vh[g].ap[:, r, 64 * s_:64 * s_ + 64]
                            kprev_keys = [kTh[g].k(c2, t0) for t0 in range(0, NT - halo_lo[g], 512)]
                            vprev_keys = [vh[g].k(r)]
                        else:
                            kprev = kT.ap[pr, c2, blk_cols(g, r, qb - 1)]
                            vprev = vo.ap[:, bi - 1, 64 * s_:64 * s_ + 64]
                            kprev_keys = kT_all
                            vprev_keys = [vo.k(bi - 1)]
                        kcur = kT.ap[pr, c2, qcols]
                        vcur = vo.ap[:, bi, 64 * s_:64 * s_ + 64]
                        qsl = qT.ap[pr, c2, qcols]

                        def fsc(e):
                            e.matmul(out=ps(sbank, 128, 0), lhsT=kprev, rhs=qsl, start=True, stop=True)
                            return e.matmul(out=ps(sbank, 128, 128), lhsT=kcur, rhs=qsl, start=True, stop=True)
                        P.add("pe", fsc, reads=qT_all + kT_all + kprev_keys, writes=psk(sbank))
                        eb = ebuf[rot("ebuf", 4)]
                        P.add("act", lambda e: e.activation(out=eb.ap, in_=ps(sbank, 256), func=AF.Exp), reads=[], writes=psk(sbank) + [eb.k()])
                        pb = pbuf[rot("pbuf", 4)]
                        EB = EBf if first else EBn
                        ebv = EB.ap[:, s_, :]
                        P.add("dve", lambda e: e.tensor_tensor(out=pb.ap, in0=eb.ap, in1=ebv, op=ALU.mult),
                              reads=[eb.k(), EB.k("c"), EB.k("p"), EB.k()], writes=[pb.k()])
                        ust[u] = (pb, vprev, vcur, pr, j, vprev_keys, bi)

                    def att_b(u):
                        pb, vprev, vcur, pr, j, vprev_keys, bi = ust[u]

                        def fpv(e):
                            on = psum[pr, 512 * nb_ + 128 * j: 512 * nb_ + 128 * j + 128]
                            od = psum[pr, 512 * db_ + 128 * j: 512 * db_ + 128 * j + 128]
                            e.matmul(out=on, lhsT=vprev, rhs=pb.ap[:, 0:128], start=True, stop=False)
                            e.matmul(out=on, lhsT=vcur, rhs=pb.ap[:, 128:256], start=False, stop=True)
                            e.matmul(out=od, lhsT=cst.ap[:, 256:320], rhs=pb.ap[:, 0:128], start=True, stop=False)
                            return e.matmul(out=od, lhsT=cst.ap[:, 256:320], rhs=pb.ap[:, 128:256], start=False, stop=True)
                        P.add("pe", fpv, reads=[pb.k(), cst.k("o")] + vprev_keys + [vo.k(bi)], writes=psk(nb_, db_))

                    for u in range(len(units) + 2):
                        if u < len(units):
                            att_a(u)
                        if u > 1:
                            att_b(u - 2)
                    if g == 0:
                        dn = numacc.ap[:, c2, 512 * quad:512 * quad + 512]
                        dd = denacc.ap[:, c2, 512 * quad:512 * quad + 512]
                        sn, sd_ = ps(nb_), ps(db_)
                    elif g == 1:
                        dn = numacc.ap[:, c2, quad::4]
                        dd = denacc.ap[:, c2, quad::4]
                        sn, sd_ = ps(nb_), ps(db_)
                    else:
                        dn = numacc.ap[:, c2, :].rearrange("p (i r) -> p r i", r=16)[:, 4 * quad:4 * quad + 4, :]
                        dd = denacc.ap[:, c2, :].rearrange("p (i r) -> p r i", r=16)[:, 4 * quad:4 * quad + 4, :]
                        sn = ps(nb_).rearrange("p (a b) -> p a b", a=4)
                        sd_ = ps(db_).rearrange("p (a b) -> p a b", a=4)
                    P.add("dve", lambda e, dn=dn, sn=sn: e.tensor_tensor(out=dn, in0=sn, in1=dn, op=ALU.add),
                          reads=[], writes=psk(nb_) + [numacc.k()])
                    P.add("dve", lambda e, dd=dd, sd_=sd_: e.tensor_tensor(out=dd, in0=sd_, in1=dd, op=ALU.add),
                          reads=[], writes=psk(db_) + [denacc.k()])
                    bulk_step(2)
            qT.free(); kT.free(); vo.free(); ebs.free(); EBn.free(); EBf.free()

        P.add("dve", lambda e: e.reciprocal(out=denacc.ap, in_=denacc.ap), reads=[], writes=[denacc.k()])
        P.add("dve", lambda e: e.tensor_tensor(out=oT.ap, in0=numacc.ap, in1=denacc.ap, op=ALU.mult),
              reads=[numacc.k(), denacc.k()], writes=[oT.k()])
        dbg("oT", oT.ap.rearrange("p a b -> p (a b)"), 128, 2 * NT, [oT.k()])
        numacc.free(); denacc.free()
        for b in ebuf + pbuf + kvst:
            b.free()
        for b in kTh + vh:
            b.free()

        selB = B("selB", [NS, 128], F32)
        selT = B("selT", [NS, NS], F32)
        sbs = B("sbs", [12], F32)
        sb0 = B("sb0", [12], F32)
        P.add("sp", lambda e: e.dma_start(out=selB.ap[0:NS].rearrange("p a b -> p (a b)"), in_=c_selB), writes=[selB.k()], dma=True, chan="misc")
        P.add("sp", lambda e: e.dma_start(out=selT.ap.rearrange("p a b -> p (a b)"), in_=c_selT), writes=[selT.k()], dma=True, chan="misc")
        P.add("sp", lambda e: e.dma_start(out=sbs.ap, in_=sbias), writes=[sbs.k()], dma=True, chan="misc")
        P.add("sp", lambda e: e.dma_start(out=sb0.ap[0:NS], in_=sbias0), writes=[sb0.k()], dma=True, chan="misc")
        sa_num = B("sa_num", [3, 260], F32)
        ctile = [B("ctile%d" % i, [512], F32) for i in range(4)]
        wv = [B("wvaug%d" % i, [260], F32) for i in range(3)]
        prodb = [B("prod%d" % i, [256], F32) for i in range(2)]
        sct = B("sct", [4, 4], F32)
        qs_keys = [qs_s.k(g, i) for g in range(3) for i in range(2)]
        its = [(g, b) for g in range(3) for b in range(NS)]
        sa_st = {}

        def sa_a1(i):
            g, b = its[i]
            dil = GROUPS[g][1]
            ci = i % 4
            ct = ctile[ci]
            P.add("sp", lambda e: e.dma_start(out=ct.ap, in_=cch[g][b, 0::dil, :] if dil > 1 else cch[g][b]),
                  writes=[ct.k()], dma=True, chan=("ctile", ci))
            bb = rot("pbank", 4)
            P.add("pe", lambda e: e.matmul(out=ps(bb, 256), lhsT=selB.ap[0:NS, b, :], rhs=qs_s.ap[0:NS, g, 0:256], start=True, stop=True),
                  reads=[selB.k()] + qs_keys, writes=psk(bb))
            pr_ = prodb[i % 2]
            P.add("dve", lambda e: e.tensor_tensor(out=pr_.ap, in0=ct.ap[:, 0:256], in1=ps(bb, 256), op=ALU.mult),
                  reads=[ct.k()], writes=psk(bb) + [pr_.k()])
            si = i % 4
            P.add("dve", lambda e: e.tensor_reduce(out=sct.ap[:, si, :], in_=pr_.ap.rearrange("p (a b) -> p a b", a=4), axis=AX.X, op=ALU.add),
                  reads=[pr_.k()], writes=[sct.k(si)])
            P.add("dve", lambda e: e.scalar_tensor_tensor(out=sct.ap[:, si, :], in0=sct.ap[:, si, :], scalar=SCALE, in1=sbs.ap[:, 4 * g:4 * g + 4], op0=ALU.mult, op1=ALU.add),
                  reads=[sbs.k()], writes=[sct.k(si)])
            w_ = wv[i % 3]
            P.add("act", lambda e: e.activation(out=w_.ap[:, 256:260], in_=sct.ap[:, si, :], func=AF.Exp),
                  reads=[sct.k(si)], writes=[w_.k("e")])
            sa_st[i] = (ct, w_)

        def sa_a2(i):
            ct, w_ = sa_st[i]
            P.add("dve", lambda e: e.tensor_tensor(out=w_.ap[:, 0:256].rearrange("p (a b) -> p a b", a=4),
                                                   in0=ct.ap[:, 256:512].rearrange("p (a b) -> p a b", a=4),
                                                   in1=w_.ap[:, 256:260].unsqueeze(2).broadcast_to([128, 4, 64]), op=ALU.mult),
                  reads=[ct.k(), w_.k("e")], writes=[w_.k("v")])

        def sa_b(i):
            g, b = its[i]
            ct, w_ = sa_st[i]
            P.add("pe", lambda e: e.matmul(out=psum[0:NS, 512 * (5 + g): 512 * (5 + g) + 260], lhsT=selT.ap[:, b, :], rhs=w_.ap,
                                           start=(b == 0), stop=(b == NS - 1)),
                  reads=[w_.k("e"), w_.k("v"), selT.k()], writes=psk(5 + g))
            if b == NS - 1:
                P.add("act", lambda e: e.activation(out=sa_num.ap[0:NS, g, :], in_=psum[0:NS, 512 * (5 + g): 512 * (5 + g) + 260], func=AF.Copy),
                      reads=[], writes=psk(5 + g) + [sa_num.k(g)])

        for i in range(len(its) + 2):
            if i < len(its):
                sa_a1(i)
            if 0 <= i - 1 < len(its):
                sa_a2(i - 1)
            if 0 <= i - 2 < len(its):
                sa_b(i - 2)
        sself = B("sself", [12 * 64 + 12 + 12], F32)
        sp_ = sself.ap[0:NS, 0:768].rearrange("p (a b) -> p a b", a=12)
        s0 = sself.ap[0:NS, 768:780]
        e0 = sself.ap[0:NS, 780:792]
        qv = qs_s.ap[0:NS]
        P.add("dve", lambda e: e.tensor_tensor(out=sself.ap[0:NS, 0:768].rearrange("p (g c) -> p g c", g=3), in0=qv[:, :, 0:256], in1=qv[:, :, 256:512], op=ALU.mult),
              reads=qs_keys, writes=[sself.k("p")])
        P.add("dve", lambda e: e.tensor_reduce(out=s0, in_=sp_, axis=AX.X, op=ALU.add), reads=[sself.k("p")], writes=[sself.k("s")])
        P.add("dve", lambda e: e.scalar_tensor_tensor(out=s0, in0=s0, scalar=SCALE, in1=sb0.ap[0:NS, :], op0=ALU.mult, op1=ALU.add),
              reads=[sb0.k()], writes=[sself.k("s")])
        P.add("act", lambda e: e.activation(out=e0, in_=s0, func=AF.Exp), reads=[sself.k("s")], writes=[sself.k("e")])
        P.add("dve", lambda e: e.tensor_tensor(out=sself.ap[0:NS, 0:768].rearrange("p (g s d) -> p g s d", g=3, s=4),
                                               in0=qv[:, :, 512:768].rearrange("p g (s d) -> p g s d", s=4),
                                               in1=sself.ap[0:NS, 780:792].rearrange("p (g s) -> p g s", g=3).unsqueeze(3).broadcast_to([NS, 3, 4, 64]), op=ALU.mult),
              reads=qs_keys + [sself.k("e")], writes=[sself.k("p")])
        san = sa_num.ap[0:NS]
        P.add("dve", lambda e: e.tensor_tensor(out=san[:, :, 0:256], in0=san[:, :, 0:256], in1=sself.ap[0:NS, 0:768].rearrange("p (g c) -> p g c", g=3), op=ALU.add),
              reads=[sself.k("p")], writes=[sa_num.k(0), sa_num.k(1), sa_num.k(2)])
        P.add("dve", lambda e: e.tensor_tensor(out=san[:, :, 256:260], in0=san[:, :, 256:260], in1=sself.ap[0:NS, 780:792].rearrange("p (g s) -> p g s", g=3), op=ALU.add),
              reads=[sself.k("e")], writes=[sa_num.k(0), sa_num.k(1), sa_num.k(2)])
        P.add("dve", lambda e: e.tensor_tensor(out=san[:, 0, :], in0=san[:, 0, :], in1=san[:, 1, :], op=ALU.add), reads=[], writes=[sa_num.k(0), sa_num.k(1), sa_num.k(2)])
        P.add("dve", lambda e: e.tensor_tensor(out=san[:, 0, :], in0=san[:, 0, :], in1=san[:, 2, :], op=ALU.add), reads=[], writes=[sa_num.k(0), sa_num.k(1), sa_num.k(2)])
        P.add("dve", lambda e: e.reciprocal(out=san[:, 0, 256:260], in_=san[:, 0, 256:260]), reads=[], writes=[sa_num.k(0), sa_num.k(1), sa_num.k(2)])
        P.add("dve", lambda e: e.tensor_tensor(out=san[:, 1, 0:256].rearrange("p (s d) -> p s d", s=4), in0=san[:, 0, 0:256].rearrange("p (s d) -> p s d", s=4),
                                               in1=san[:, 0, 256:260].unsqueeze(2).broadcast_to([NS, 4, 64]), op=ALU.mult),
              reads=[], writes=[sa_num.k(0), sa_num.k(1), sa_num.k(2)])
        osT = B("osT", [2, NS], BF16)

        def tr_os(e):
            r = None
            for kc in range(2):
                r = e.transpose(out=ps(0, NS, NS * kc), in_=sa_num.ap[0:NS, 1, kc * 128:(kc + 1) * 128], identity=identf.ap[0:NS, 0:NS])
            return r
        P.add("pe", tr_os, reads=[sa_num.k(1), identf.k()], writes=psk(0))
        P.add("act", lambda e: e.activation(out=osT.ap.rearrange("p a b -> p (a b)"), in_=ps(0, 2 * NS), func=AF.Copy), reads=[], writes=psk(0) + [osT.k()])
        dbg("osT", osT.ap.rearrange("p a b -> p (a b)"), 128, 2 * NS, [osT.k()])
        for b in [selB, selT, sbs, sb0, sa_num, sct, sself, qs_s] + ctile + wv + prodb:
            b.free()

        w_ao = load_w_own("w_ao", w_attn_out, 2, D, "wsmall")
        w_pp = load_w_own("w_pp", w_ple_proj, 2, D, "wsmall")
        hbb.pop().free()
        xin.extend([B("xin%d" % i, [D], F32) for i in range(2)])
        add_wslot(); add_wslot()
        zT = B("zT", [8, 30 + 512], BF16)
        zsT = B("zsT", [8, NS], F32)
        histT = B("histT", [8, NS * 30], F32)
        zlast = B("zlast", [8, 30], F32)

        for q4 in range(4):
            i = rot("xin", len(xin))
            xb = xin[i]
            P.add("sp", lambda e, xb=xb, q4=q4: e.dma_start(out=xb.ap[0:120, :], in_=sconv[4 * q4:4 * q4 + 4].rearrange("b i c -> (b i) c")),
                  writes=[xb.k()], dma=True, chan=("xin", i))
            for half in range(2):
                bank = rot("pbank", 4)

                def ft(e, xb=xb, bank=bank, half=half):
                    r = None
                    for k4 in range(4):
                        kc = 4 * half + k4
                        r = e.transpose(out=ps(bank, 120, 120 * k4), in_=xb.ap[0:120, kc * 128:(kc + 1) * 128], identity=identf.ap[0:120, 0:120])
                    return r
                P.add("pe", ft, reads=[xb.k(), identf.k()], writes=psk(bank))
                evac_copy(histT.ap[:, 4 * half:4 * half + 4, 120 * q4:120 * q4 + 120], ps(bank, 480).rearrange("p (a b) -> p a b", a=4),
                          [], psk(bank) + [histT.k(q4, half)])
        histT_all = [histT.k(q4, h) for q4 in range(4) for h in range(2)]
        while xin:
            xin.pop().free()

        def ln_silu(ysrc, ykeys, ncols, dstT, dkeys, tag):
            ybf = B("ybf", [8, ncols], BF16)
            ysq = B("ysq", [8, ncols], BF16)
            stt = B("stt", [3, ncols], F32)
            P.add("act", lambda e: e.activation(out=ybf.ap, in_=ysrc, func=AF.Copy), reads=ykeys, writes=[ybf.k()])
            P.add("act", lambda e: e.activation(out=ysq.ap, in_=ysrc, func=AF.Square), reads=ykeys, writes=[ysq.k()])

            def fst(e):
                r = None
                for kc in range(8):
                    e.matmul(out=ps(2, ncols), lhsT=ones_bf, rhs=ybf.ap[:, kc, :], start=(kc == 0), stop=(kc == 7))
                for kc in range(8):
                    r = e.matmul(out=ps(3, ncols), lhsT=ones_bf, rhs=ysq.ap[:, kc, :], start=(kc == 0), stop=(kc == 7))
                return r
            P.add("pe", fst, reads=[ybf.k(), ysq.k(), cst.k("o")], writes=psk(2, 3))
            mean = stt.ap[:, 0, :]; var = stt.ap[:, 1, :]; rstd = stt.ap[:, 2, :]
            P.add("act", lambda e: e.activation(out=mean, in_=ps(2, ncols), func=AF.Copy, scale=1.0 / D), reads=[], writes=psk(2) + [stt.k(0)])
            P.add("dve", lambda e: e.tensor_tensor(out=var, in0=mean, in1=mean, op=ALU.mult), reads=[stt.k(0)], writes=[stt.k(1)])
            P.add("dve", lambda e: e.scalar_tensor_tensor(out=var, in0=ps(3, ncols), scalar=1.0 / D, in1=var, op0=ALU.mult, op1=ALU.subtract),
                  reads=[], writes=psk(3) + [stt.k(1)])
            P.add("act", lambda e: e.activation(out=var, in_=var, func=AF.Sqrt, bias=epsb.ap[:, 0:1], scale=1.0), reads=[epsb.k()], writes=[stt.k(1)])
            P.add("dve", lambda e: e.reciprocal(out=rstd, in_=var), reads=[stt.k(1)], writes=[stt.k(2)])
            for kc in range(8):
                P.add("dve", lambda e, kc=kc: e.tensor_tensor(out=ysrc[:, kc, :], in0=ysrc[:, kc, :], in1=mean, op=ALU.subtract), reads=[stt.k(0)], writes=ykeys)
                P.add("dve", lambda e, kc=kc: e.tensor_tensor(out=ysrc[:, kc, :], in0=ysrc[:, kc, :], in1=rstd, op=ALU.mult), reads=[stt.k(2)], writes=ykeys)
                P.add("act", lambda e, kc=kc: e.activation(out=dstT[:, kc, :], in_=ysrc[:, kc, :], func=AF.Silu, bias=vcol.ap[:, 2, kc:kc + 1], scale=vcol.ap[:, 1, kc:kc + 1]),
                      reads=ykeys + [vcol.k()], writes=dkeys)
            ybf.free(); ysq.free(); stt.free()

        def make_4a(tt):
            T0_ = tt * 512
            hkeys_ = [hT.k(4 * tt + i) for i in range(4)]
            st4 = {}

            def hsrc_(kc):
                return hT.ap[:, kc, T0_:T0_ + 512]

            def step(q4, k4):
                def run():
                    if q4 == 0 and k4 == 0:
                        st4["sgw"] = [B("sgw%d" % i, [512], F32) for i in range(2)]
                    if k4 == 0:
                        st4["w"] = load_w_pair(w_in[:, 2304 + 512 * q4:2304 + 512 * q4 + 512], w_in[:, 3328 + 512 * q4:3328 + 512 * q4 + 512], 512, tile=tt, li=q4)
                    wa, wg = st4["w"]
                    sgw_ = st4["sgw"]
                    kc = 4 * q4 + k4
                    segs = [(512, hsrc_, hkeys_, "main")]
                    if tt == 0:
                        segs.append((30, (lambda k: hTh30.ap[:, k, 0:30]), [hTh30.k()], "halo"))
                        segs.append((NS, (lambda k: hsT.ap[:, k, :]), [hsT.k()], "samp"))
                    for (n, sfn, skeys, kind) in segs:
                        ba = rot("pbank", 4)
                        bg = rot("pbank", 4)

                        def fu(e, ba=ba, bg=bg, n=n, sfn=sfn):
                            r = None
                            for k in range(8):
                                e.matmul(out=ps(ba, n), lhsT=wa.ap[:, k, 128 * k4:128 * k4 + 128], rhs=sfn(k), start=(k == 0), stop=(k == 7))
                            for k in range(8):
                                r = e.matmul(out=ps(bg, n), lhsT=wg.ap[:, k, 128 * k4:128 * k4 + 128], rhs=sfn(k), start=(k == 0), stop=(k == 7))
                            return r
                        P.add("pe", fu, reads=[wa.k(), wg.k()] + skeys, writes=psk(ba, bg))
                        sg = sgw_[rot("sgw", 2)]
                        P.add("act", lambda e, sg=sg, bg=bg, n=n: e.activation(out=sg.ap[:, 0:n], in_=ps(bg, n), func=AF.Sigmoid), reads=[], writes=psk(bg) + [sg.k()])
                        if kind == "main":
                            P.add("dve", lambda e, sg=sg, ba=ba: e.tensor_tensor(out=zT.ap[:, kc, 30:542], in0=ps(ba, 512), in1=sg.ap, op=ALU.mult),
                                  reads=[sg.k()], writes=psk(ba) + [zT.k(kc, "m")])
                            if tt == 3:
                                P.add("dve", lambda e, sg=sg, ba=ba: e.tensor_tensor(out=zlast.ap[:, kc, :], in0=ps(ba, 30, 482), in1=sg.ap[:, 482:512], op=ALU.mult),
                                      reads=[sg.k()], writes=psk(ba) + [zlast.k(kc)])
                        elif kind == "halo":
                            P.add("dve", lambda e, sg=sg, ba=ba: e.tensor_tensor(out=zT.ap[:, kc, 0:30], in0=ps(ba, 30), in1=sg.ap[:, 0:30], op=ALU.mult),
                                  reads=[sg.k()], writes=psk(ba) + [zT.k(kc, "h")])
                        else:
                            P.add("dve", lambda e, sg=sg, ba=ba: e.tensor_tensor(out=zsT.ap[:, kc, :], in0=ps(ba, NS), in1=sg.ap[:, 0:NS], op=ALU.mult),
                                  reads=[sg.k()], writes=psk(ba) + [zsT.k(kc)])
                    if q4 == 1 and k4 == 3:
                        for b_ in sgw_:
                            b_.free()
                        if tt == 3:
                            zrow2 = B("zrow2", [D], F32)

                            def tzl(e):
                                r = None
                                for kc_ in range(8):
                                    r = e.transpose(out=psum[0:30, 128 * kc_:128 * kc_ + 128], in_=zlast.ap[:, kc_, :], identity=identf.ap)
                                return r
                            P.add("pe", tzl, reads=[zlast.k(kc_) for kc_ in range(8)] + [identf.k()], writes=psk(0, 1))
                            P.add("act", lambda e: e.activation(out=zrow2.ap[0:30, :], in_=psum[0:30, 0:1024], func=AF.Copy), reads=[], writes=psk(0, 1) + [zrow2.k()])
                            P.add("sp", lambda e: e.dma_start(out=convo, in_=zrow2.ap[0:30, :]), reads=[zrow2.k()], writes=[("dram", "convo")], dma=True, chan="misc")
                            zrow2.free()
                return run
            return [step(q4, k4) for q4 in range(2) for k4 in range(4)]

        for t in range(4):
            wsamp = (t == 0)
            T0 = t * 512
            wl["tile"] = t
            wl["li"] = 2
            def hsrc(kc, T0=T0):
                return hT.ap[:, kc, T0:T0 + 512]
            hkeys = [hT.k(4 * t + i) for i in range(4)]

            if t == 0:
                for st_ in make_4a(0):
                    st_()

            if wsamp:
                ysb = B("ysb", [8, NS], F32)
                hp = B("hp", [8, NS * 30], F32)
                sTs = B("sTs", [8, NS], BF16)
                P.add("dve", lambda e: e.tensor_tensor(out=hp.ap.rearrange("p k (b i) -> p k b i", i=30), in0=histT.ap.rearrange("p k (b i) -> p k b i", i=30),
                                                       in1=wT.ap[:, :, 0:30].unsqueeze(2).broadcast_to([128, 8, NS, 30]), op=ALU.mult),
                      reads=histT_all + [wT.k()], writes=[hp.k()])
                P.add("dve", lambda e: e.tensor_reduce(out=ysb.ap, in_=hp.ap.rearrange("p k (b i) -> p k b i", i=30), axis=AX.X, op=ALU.add),
                      reads=[hp.k()], writes=[ysb.k()])
                P.add("dve", lambda e: e.tensor_tensor(out=hp.ap[:, :, 0:NS], in0=zsT.ap, in1=wT.ap[:, :, 30:31].broadcast_to([128, 8, NS]), op=ALU.mult),
                      reads=[zsT.k(kc) for kc in range(8)] + [wT.k()], writes=[hp.k()])
                P.add("dve", lambda e: e.tensor_tensor(out=ysb.ap, in0=ysb.ap, in1=hp.ap[:, :, 0:NS], op=ALU.add), reads=[hp.k()], writes=[ysb.k()])
                P.add("dve", lambda e: e.tensor_tensor(out=ysb.ap, in0=ysb.ap, in1=vcol.ap[:, 0, :].unsqueeze(2).broadcast_to([128, 8, NS]), op=ALU.add),
                      reads=[vcol.k()], writes=[ysb.k()])
                ln_silu(ysb.ap, [ysb.k()], NS, sTs.ap, [sTs.k()], "s")
                dbg("sTs", sTs.ap.rearrange("p a b -> p (a b)"), 128, 8 * NS, [sTs.k()])
                ysb.free(); hp.free(); histT.free()
                zrow = B("zrow", [D], F32)

                def tz(e):
                    r = None
                    for kc in range(8):
                        r = e.transpose(out=psum[0:NS, 128 * kc:128 * kc + 128], in_=zsT.ap[:, kc, :], identity=identf.ap)
                    return r
                P.add("pe", tz, reads=[zsT.k(kc) for kc in range(8)] + [identf.k()], writes=psk(0, 1))
                P.add("act", lambda e: e.activation(out=zrow.ap[0:NS, :], in_=psum[0:NS, 0:1024], func=AF.Copy), reads=[], writes=psk(0, 1) + [zrow.k()])
                P.add("sp", lambda e: e.dma_start(out=convs[:, 29, :], in_=zrow.ap[0:NS, :]), reads=[zrow.k()], writes=[("dram", "convs_new")], dma=True, chan="misc")
                zrow.free()

            ybuf = B("ybuf", [8, 512], F32)
            sT = B("sT", [8, 512], BF16)
            diag = [B("diag%d" % i, [31, 128], BF16) for i in range(2)]
            ybfr = [B("ybf%d" % i, [512], BF16) for i in range(3)]
            ysqr = [B("ysq%d" % i, [512], BF16) for i in range(3)]
            zTs = B("zTs", [8, 542], BF16)
            for kc in range(8):
                P.add("dve", lambda e, kc=kc: e.tensor_copy(out=zTs.ap[:, kc, 0:541], in_=zT.ap[:, kc, 1:542]),
                      reads=[zT.k(kc, "m"), zT.k(kc, "h")], writes=[zTs.k(kc)])

            def conv_a(kc):
                dg = diag[kc % 2]
                for i in range(31):
                    P.add("dve", lambda e, i=i: e.tensor_scalar(out=dg.ap[:, i, :], in0=ident, scalar1=wT.ap[:, kc, i:i + 1], scalar2=None, op0=ALU.mult),
                          reads=[cst.k("i"), wT.k()], writes=[dg.k(i)])
                bank = rot("cbank", 2)

                def fc(e):
                    r = None
                    for i in range(31):
                        src = zT.ap[:, kc, i:i + 512] if i % 2 == 0 else zTs.ap[:, kc, i - 1:i - 1 + 512]
                        r = e.matmul(out=ps(bank), lhsT=dg.ap[:, i, :], rhs=src, start=(i == 0), stop=(i == 30))
                    return r
                P.add("pe", fc, reads=[dg.k(i) for i in range(31)] + [zT.k(kc, "m"), zT.k(kc, "h"), zTs.k(kc)], writes=psk(bank))
                P.add("act", lambda e: e.activation(out=ybuf.ap[:, kc, :], in_=ps(bank), func=AF.Identity, bias=vcol.ap[:, 0, kc:kc + 1], scale=1.0),
                      reads=[vcol.k()], writes=psk(bank) + [ybuf.k(kc)])
                ybf = ybfr[kc % 3]; ysq = ysqr[kc % 3]
                P.add("act", lambda e: e.activation(out=ybf.ap, in_=ybuf.ap[:, kc, :], func=AF.Copy), reads=[ybuf.k(kc)], writes=[ybf.k()])
                P.add("act", lambda e: e.activation(out=ysq.ap, in_=ybuf.ap[:, kc, :], func=AF.Square), reads=[ybuf.k(kc)], writes=[ysq.k()])
                if t < 3:
                    P.add("dve", lambda e: e.tensor_copy(out=zT.ap[:, kc, 0:30], in_=zT.ap[:, kc, 512:542]),
                          reads=[zT.k(kc, "m")], writes=[zT.k(kc, "h")])

            def conv_b(kc):
                ybf = ybfr[kc % 3]; ysq = ysqr[kc % 3]

                def fst(e):
                    e.matmul(out=ps(2), lhsT=ones_bf, rhs=ybf.ap, start=(kc == 0), stop=(kc == 7))
                    return e.matmul(out=ps(3), lhsT=ones_bf, rhs=ysq.ap, start=(kc == 0), stop=(kc == 7))
                P.add("pe", fst, reads=[ybf.k(), ysq.k(), cst.k("o")], writes=psk(2, 3))

            for kc in range(9):
                if kc < 8:
                    conv_a(kc)
                    if kc in (2, 5):
                        bulk_step(2)
                if kc > 0:
                    conv_b(kc - 1)
            for dg in diag:
                dg.free()
            zTs.free()
            ybk = [ybuf.k(kc) for kc in range(8)]
            stt = B("stt", [3, 512], F32)
            mean = stt.ap[:, 0, :]; var = stt.ap[:, 1, :]; rstd = stt.ap[:, 2, :]
            P.add("act", lambda e: e.activation(out=mean, in_=ps(2), func=AF.Copy, scale=1.0 / D), reads=[], writes=psk(2) + [stt.k(0)])
            P.add("dve", lambda e: e.tensor_tensor(out=var, in0=mean, in1=mean, op=ALU.mult), reads=[stt.k(0)], writes=[stt.k(1)])
            P.add("dve", lambda e: e.scalar_tensor_tensor(out=var, in0=ps(3), scalar=1.0 / D, in1=var, op0=ALU.mult, op1=ALU.subtract),
                  reads=[], writes=psk(3) + [stt.k(1)])
            P.add("act", lambda e: e.activation(out=var, in_=var, func=AF.Sqrt, bias=epsb.ap[:, 0:1], scale=1.0), reads=[epsb.k()], writes=[stt.k(1)])
            P.add("dve", lambda e: e.reciprocal(out=rstd, in_=var), reads=[stt.k(1)], writes=[stt.k(2)])
            for kc in range(8):
                P.add("dve", lambda e, kc=kc: e.tensor_tensor(out=ybuf.ap[:, kc, :], in0=ybuf.ap[:, kc, :], in1=mean, op=ALU.subtract), reads=[stt.k(0)], writes=[ybuf.k(kc)])
                P.add("dve", lambda e, kc=kc: e.tensor_tensor(out=ybuf.ap[:, kc, :], in0=ybuf.ap[:, kc, :], in1=rstd, op=ALU.mult), reads=[stt.k(2)], writes=[ybuf.k(kc)])
                P.add("act", lambda e, kc=kc: e.activation(out=sT.ap[:, kc, :], in_=ybuf.ap[:, kc, :], func=AF.Silu, bias=vcol.ap[:, 2, kc:kc + 1], scale=vcol.ap[:, 1, kc:kc + 1]),
                      reads=[ybuf.k(kc), vcol.k()], writes=[sT.k(kc)])
            sT_all = [sT.k(kc) for kc in range(8)]
            if t == 0:
                dbg("sT", sT.ap.rearrange("p a b -> p (a b)"), 128, 8 * 512, sT_all)
            ybuf.free(); stt.free()
            for b_ in ybfr + ysqr:
                b_.free()

            tiles = [(512, "p")] + ([(NS, "s")] if wsamp else [])
            blocks = [(128, i, "p") for i in range(4)] + ([(NS, 0, "s")] if wsamp else [])
            nb = len(blocks)
            x1 = B("x1", [nb, D], F32)
            for bi_, (M, i, kind) in enumerate(blocks):
                xsrc = xo[T0 + 128 * i:T0 + 128 * i + 128, :] if kind == "p" else xs
                P.add("sp", lambda e, bi_=bi_, M=M, xsrc=xsrc: e.dma_start(out=x1.ap[0:M, bi_, :], in_=xsrc), writes=[x1.k(bi_)], dma=True, chan=("xin", bi_ % 2))
            g_post = load_gain(1, "g_post"); g_fpre = load_gain(2, "g_fpre")
            mixT = B("mixT", [8, 512], BF16)
            mixTs = B("mixTs", [8, NS], BF16) if wsamp else None
            w_co = load_w("w_co", w_conv_out, 8, D, ("w", rot("wslot", 3)))
            sab = [B("sab%d" % i, [512], F32) for i in range(2)]
            for q2 in range(2):
                wga, wgb = load_w_pair(w_in[:, 4352 + 512 * q2:4352 + 512 * q2 + 512], w_in[:, 5376 + 512 * q2:5376 + 512 * q2 + 512], 512)
                bulk_step(1)
                for j4 in range(4):
                    j = 4 * q2 + j4
                    for (n, kind) in tiles:
                        if kind == "p":
                            o_src = lambda k: oT.ap[:, k, T0:T0 + 512]
                            s_src = lambda k: sT.ap[:, k, :]
                            h_src = hsrc
                            rk1 = [oT.k()] + hkeys
                            rk2 = sT_all
                            dst = mixT.ap[:, j, :]
                            dk = [mixT.k(j)]
                        else:
                            o_src = lambda k: osT.ap[:, k, :]
                            s_src = lambda k: sTs.ap[:, k, :]
                            h_src = lambda k: hsT.ap[:, k, :]
                            rk1 = [osT.k(), hsT.k()]
                            rk2 = [sTs.k()]
                            dst = mixTs.ap[:, j, :]
                            dk = [mixTs.k(j)]

                        def fm1(e, n=n, j=j, j4=j4, o_src=o_src, h_src=h_src, wga=wga, wgb=wgb):
                            r = None
                            for k in range(2):
                                e.matmul(out=ps(4, n), lhsT=w_ao.ap[:, k, 128 * j:128 * j + 128], rhs=o_src(k), start=(k == 0), stop=(k == 1))
                            for k in range(8):
                                e.matmul(out=ps(6, n), lhsT=wga.ap[:, k, 128 * j4:128 * j4 + 128], rhs=h_src(k), start=(k == 0), stop=(k == 7))
                            for k in range(8):
                                r = e.matmul(out=ps(7, n), lhsT=wgb.ap[:, k, 128 * j4:128 * j4 + 128], rhs=h_src(k), start=(k == 0), stop=(k == 7))
                            return r

                        def fm2(e, n=n, j=j, s_src=s_src):
                            r = None
                            for k in range(8):
                                r = e.matmul(out=ps(5, n), lhsT=w_co.ap[:, k, 128 * j:128 * j + 128], rhs=s_src(k), start=(k == 0), stop=(k == 7))
                            return r
                        P.add("pe", fm1, reads=[w_ao.k(), wga.k(), wgb.k()] + rk1, writes=psk(4, 6, 7))
                        P.add("pe", fm2, reads=[w_co.k()] + rk2, writes=psk(5))
                        sa, sb2 = sab[0], sab[1]
                        P.add("act", lambda e, n=n, sa=sa: e.activation(out=sa.ap[:, 0:n], in_=ps(6, n), func=AF.Sigmoid), reads=[], writes=psk(6) + [sa.k()])
                        P.add("act", lambda e, n=n, sb2=sb2: e.activation(out=sb2.ap[:, 0:n], in_=ps(7, n), func=AF.Sigmoid), reads=[], writes=psk(7) + [sb2.k()])
                        P.add("dve", lambda e, n=n, sa=sa: e.tensor_tensor(out=sa.ap[:, 0:n], in0=ps(4, n), in1=sa.ap[:, 0:n], op=ALU.mult), reads=[], writes=psk(4) + [sa.k()])
                        P.add("dve", lambda e, n=n, sb2=sb2: e.tensor_tensor(out=sb2.ap[:, 0:n], in0=ps(5, n), in1=sb2.ap[:, 0:n], op=ALU.mult), reads=[], writes=psk(5) + [sb2.k()])
                        P.add("dve", lambda e, n=n, sa=sa, sb2=sb2, dst=dst: e.tensor_tensor(out=dst, in0=sa.ap[:, 0:n], in1=sb2.ap[:, 0:n], op=ALU.add),
                              reads=[sb2.k()], writes=[sa.k()] + dk)
                wga.free(); wgb.free()
            if t == 0:
                dbg("mixT", mixT.ap.rearrange("p a b -> p (a b)"), 128, 8 * 512, [mixT.k(j) for j in range(8)])
                dbg("mixTs", mixTs.ap.rearrange("p a b -> p (a b)"), 128, 8 * NS, [mixTs.k(j) for j in range(8)])
            w_co.free()
            for b in sab:
                b.free()
            sT.free()
            if wsamp:
                sTs.free()

            h2T = B("h2T", [8, 512], BF16)
            h2Ts = B("h2Ts", [8, NS], BF16) if wsamp else None
            tmp = [B("tmp%d" % i, [D], F32) for i in range(2)]
            w_o = load_w("w_o", w_out, 8, D, ("w", rot("wslot", 3)))

            def tm_proj(wb, nk, lhs_fn, rkeys, M, banks):
                def f(e):
                    r = None
                    for hf in range(2):
                        for k in range(nk):
                            r = e.matmul(out=psum[0:M, 512 * banks[hf]:512 * banks[hf] + 512], lhsT=lhs_fn(k), rhs=wb.ap[:, k, 512 * hf:512 * hf + 512],
                                         start=(k == 0), stop=(k == nk - 1))
                    return r
                P.add("pe", f, reads=[wb.k()] + rkeys, writes=psk(*banks))

            def post_norm_residual(M, banks, gain, res_ap, res_keys, out_ap, out_keys):
                src = psum[0:M, 512 * banks[0]:512 * banks[0] + 1024]
                i = rot("small", 4)
                ms = small.ap[0:M, 0, i:i + 1]; sd = small.ap[0:M, 1, i:i + 1]; rs = small.ap[0:M, 2, i:i + 1]
                P.add("act", lambda e: e.activation(out=junk.ap[0:M, :], in_=src, func=AF.Square, scale=1.0 / 32.0, accum_out=ms), reads=[], writes=psk(*banks) + [small.k(0, i)])
                P.add("act", lambda e: e.activation(out=sd, in_=ms, func=AF.Sqrt, bias=epsb.ap[0:M, 0:1], scale=1.0), reads=[small.k(0, i), epsb.k()], writes=[small.k(1, i)])
                P.add("dve", lambda e: e.reciprocal(out=rs, in_=sd), reads=[small.k(1, i)], writes=[small.k(2, i)])
                ti = rot("tmp", 2)
                tb = tmp[ti]
                P.add("dve", lambda e: e.scalar_tensor_tensor(out=tb.ap[0:M, :], in0=src, scalar=rs, in1=gain.ap[0:M, :], op0=ALU.mult, op1=ALU.mult),
                      reads=[small.k(2, i), gain.k()], writes=psk(*banks) + [tb.k()])
                P.add("dve", lambda e: e.tensor_tensor(out=out_ap, in0=tb.ap[0:M, :], in1=res_ap, op=ALU.add), reads=[tb.k()] + res_keys, writes=out_keys)

            d_hb = {}

            def d_a(bi_):
                M, i, kind = blocks[bi_]
                banks = (0, 1) if bi_ % 2 == 0 else (2, 3)
                if kind == "p":
                    lhs = lambda k: mixT.ap[:, k, 128 * i:128 * i + 128]
                    rk = [mixT.k(j) for j in range(8)]
                    xsrc = xo[T0 + 128 * i:T0 + 128 * i + 128, :]
                else:
                    lhs = lambda k: mixTs.ap[:, k, :]
                    rk = [mixTs.k(j) for j in range(8)]
                    xsrc = xs
                tm_proj(w_o, 8, lhs, rk, M, banks)
                post_norm_residual(M, banks, g_post, x1.ap[0:M, bi_, :], [x1.k(bi_)], x1.ap[0:M, bi_, :], [x1.k(bi_)])
                hb = hbb[rot("hbb", len(hbb))]
                d_hb[bi_] = hb
                rms_to_bf16(x1.ap[0:M, bi_, :], [x1.k(bi_)], M, g_fpre, hb.ap[0:M, :], hb.k())

            def d_b(bi_):
                M, i, kind = blocks[bi_]
                hb = d_hb[bi_]
                tb_ = 4 + rot("tbank", 2)
                if kind == "p":
                    transpose_rows(hb.ap[0:M, :], hb.k(), M, 8, (lambda: h2T.ap[:, :, 128 * i:128 * i + 128]), [h2T.k(i)], tb_, evac=("act" if tb_ == 4 else "dve"))
                else:
                    transpose_rows(hb.ap[0:M, :], hb.k(), M, 8, (lambda: h2Ts.ap), [h2Ts.k()], tb_, evac="act")

            for bi_ in range(nb + 1):
                if bi_ < nb:
                    d_a(bi_)
                if bi_ > 0:
                    d_b(bi_ - 1)
            if t == 0:
                dbg("x1", x1.ap.rearrange("p a b -> p (a b)"), 128, nb * D, [x1.k(i_) for i_ in range(nb)])
                dbg("h2T", h2T.ap.rearrange("p a b -> p (a b)"), 128, 8 * 512, [h2T.k(i_) for i_ in range(4)])
            w_o.free(); g_post.free(); g_fpre.free()
            for b_ in tmp:
                b_.free()
            mixT.free()
            if wsamp:
                mixTs.free()

            g_fpost = load_gain(3, "g_fpost"); g_ple = load_gain(4, "g_ple")
            aTp = [B("aT%d" % i_, [8 if i_ < 2 else 6, 512], BF16) for i_ in range(3)]
            sgw = [B("sgw%d" % i, [512], F32) for i in range(2)]
            aTs = B("aTs", [NFF, NS], BF16) if wsamp else None
            h2keys = [h2T.k(i) for i in range(4)]
            for s6 in range(6):
                nch = 4 if s6 < 5 else 2
                sl_ = wslot()
                wf = WView(sl_, sl_.ap.rearrange("p (a b c) -> p a b c", a=8, b=2))

                def ldf(e, wf=wf, s6=s6, nch=nch):
                    a = e.dma_start(out=wf.ap[:, :, 0, 0:128 * nch], in_=w_ffn_in[:, 512 * s6:512 * s6 + 128 * nch].rearrange("(kc p) c -> p kc c", p=128))
                    b = e.dma_start(out=wf.ap[:, :, 1, 0:128 * nch], in_=w_ffn_in[:, DFF + 512 * s6:DFF + 512 * s6 + 128 * nch].rearrange("(kc p) c -> p kc c", p=128))
                    return [a, b]
                ring_load(sl_, ldf, 2)
                bulk_step(2 if s6 < 5 else 1)
                for j4 in range(nch):
                    j = 4 * s6 + j4
                    for (n, kind) in tiles:
                        if kind == "p":
                            src = lambda k: h2T.ap[:, k, :]
                            rk = h2keys
                            dst = aTp[j // 8].ap[:, j % 8, :]
                            dk = [aTp[j // 8].k(j % 8)]
                        else:
                            src = lambda k: h2Ts.ap[:, k, :]
                            rk = [h2Ts.k()]
                            dst = aTs.ap[:, j, :]
                            dk = [aTs.k(j)]
                        bg = rot("pbank", 4)
                        bu = rot("pbank", 4)

                        def ff(e, bg=bg, bu=bu, n=n, src=src, wf=wf, j4=j4):
                            r = None
                            for k in range(8):
                                e.matmul(out=ps(bg, n), lhsT=wf.ap[:, k, 0, 128 * j4:128 * j4 + 128], rhs=src(k), start=(k == 0), stop=(k == 7))
                            for k in range(8):
                                r = e.matmul(out=ps(bu, n), lhsT=wf.ap[:, k, 1, 128 * j4:128 * j4 + 128], rhs=src(k), start=(k == 0), stop=(k == 7))
                            return r
                        P.add("pe", ff, reads=[wf.k()] + rk, writes=psk(bg, bu))
                        si = rot("sgw", 2)
                        sg = sgw[si]
                        P.add("act", lambda e, sg=sg, bg=bg, n=n: e.activation(out=sg.ap[:, 0:n], in_=ps(bg, n), func=AF.Silu), reads=[], writes=psk(bg) + [sg.k()])
                        P.add("dve", lambda e, sg=sg, bu=bu, n=n, dst=dst: e.tensor_tensor(out=dst, in0=ps(bu, n), in1=sg.ap[:, 0:n], op=ALU.mult),
                              reads=[sg.k()], writes=psk(bu) + dk)
                wf.free()
            h2T.free()
            for b_ in sgw:
                b_.free()
            if wsamp:
                h2Ts.free()

            wfos = []
            for s3 in range(3):
                k0 = 8 * s3
                nk = 8 if s3 < 2 else 6
                wb_ = load_w("w_fo", w_ffn_out[128 * k0:128 * (k0 + nk), :], nk, D)
                wfos.append(wb_)
            wfo_keys = [w_.k() for w_ in wfos]
            tmp = [B("tmp%d" % i, [D], F32) for i in range(2)]
            pin = [B("pin%d" % i, [256], F32) for i in range(nb)]
            for bi_, (M, i, kind) in enumerate(blocks):
                psrc = po[T0 + 128 * i:T0 + 128 * i + 128, :] if kind == "p" else psm
                P.add("sp", lambda e, bi_=bi_, M=M, psrc=psrc: e.dma_start(out=pin[bi_].ap[0:M, :], in_=psrc), writes=[pin[bi_].k()], dma=True, chan=("pin", bi_ % 2))
            bulk_step(1)
            for bi_, (M, i, kind) in enumerate(blocks):
                banks = (0, 1) if bi_ % 2 == 0 else (2, 3)
                if kind == "p":
                    lhs = lambda k, i=i: aTp[k // 8].ap[:, k % 8, 128 * i:128 * i + 128]
                    rk = [aTp[j // 8].k(j % 8) for j in range(NFF)]
                else:
                    lhs = lambda k: aTs.ap[:, k, :]
                    rk = [aTs.k(j) for j in range(NFF)]

                for s3 in range(3):
                    def f(e, M=M, banks=banks, lhs=lhs, s3=s3):
                        r = None
                        for hf in range(2):
                            for k in range(8 * s3, min(NFF, 8 * s3 + 8)):
                                r = e.matmul(out=psum[0:M, 512 * banks[hf]:512 * banks[hf] + 512], lhsT=lhs(k), rhs=wfos[s3].ap[:, k % 8, 512 * hf:512 * hf + 512],
                                             start=(k == 0), stop=(k == NFF - 1))
                        return r
                    P.add("pe", f, reads=[wfo_keys[s3]] + rk, writes=psk(*banks))
                post_norm_residual(M, banks, g_fpost, x1.ap[0:M, bi_, :], [x1.k(bi_)], x1.ap[0:M, bi_, :], [x1.k(bi_)])
            if t == 0:
                dbg("x2", x1.ap.rearrange("p a b -> p (a b)"), 128, nb * D, [x1.k(i_) for i_ in range(nb)])
            for b_ in wfos + aTp + [g_fpost]:
                b_.free()
            if wsamp:
                aTs.free()
            w_pg = load_w("w_pg", w_ple_gate, 8, D, ("w", rot("wslot", 3)))
            h3b = [B("h3b%d" % i, [8, 128], BF16) for i in range(2)]
            pTb = [B("pTb%d" % i, [2, 128], BF16) for i in range(2)]
            pbf = [B("pbf%d" % i, [256], BF16) for i in range(nb)]
            sgg = B("sgg", [D], F32)
            e_hb = {}

            def e_a(bi_):
                M, i, kind = blocks[bi_]
                psrc = po[T0 + 128 * i:T0 + 128 * i + 128, :] if kind == "p" else psm
                hb = hb_all[bi_]
                e_hb[bi_] = hb
                rms_to_bf16(x1.ap[0:M, bi_, :], [x1.k(bi_)], M, g_ple, hb.ap[0:M, :], hb.k())
                pi_ = bi_
                P.add("act", lambda e: e.activation(out=pbf[pi_].ap[0:M, :], in_=pin[bi_].ap[0:M, :], func=AF.Copy), reads=[pin[bi_].k()], writes=[pbf[pi_].k()])

            def e_b(bi_):
                M, i, kind = blocks[bi_]
                ydst = y[T0 + 128 * i:T0 + 128 * i + 128, :] if kind == "p" else ysm
                hb = e_hb[bi_]
                h3 = h3b[bi_ % 2]
                tb_ = 0 + rot("tbank2", 2)
                transpose_rows(hb.ap[0:M, :], hb.k(), M, 8, (lambda: h3.ap[:, :, 0:M]), [h3.k()], tb_, evac=("act" if tb_ == 0 else "dve"))
                pi_ = bi_
                pT = pTb[bi_ % 2]
                tb2 = 2 + rot("tbank3", 2)
                transpose_rows(pbf[pi_].ap[0:M, :], pbf[pi_].k(), M, 2, (lambda: pT.ap[:, :, 0:M]), [pT.k()], tb2, evac="dve")
                tm_proj(w_pg, 8, (lambda k: h3.ap[:, k, 0:M]), [h3.k()], M, (4, 5))
                tm_proj(w_pp, 2, (lambda k: pT.ap[:, k, 0:M]), [pT.k()], M, (6, 7))
                P.add("act", lambda e: e.activation(out=sgg.ap[0:M, :], in_=psum[0:M, 512 * 4:512 * 4 + 1024], func=AF.Sigmoid), reads=[], writes=psk(4, 5) + [sgg.k()])
                P.add("dve", lambda e: e.tensor_tensor(out=sgg.ap[0:M, :], in0=psum[0:M, 512 * 6:512 * 6 + 1024], in1=sgg.ap[0:M, :], op=ALU.mult), reads=[], writes=psk(6, 7) + [sgg.k()])
                ti = rot("tmp", 2)
                tb = tmp[ti]
                P.add("dve", lambda e: e.tensor_tensor(out=tb.ap[0:M, :], in0=sgg.ap[0:M, :], in1=x1.ap[0:M, bi_, :], op=ALU.add),
                      reads=[sgg.k(), x1.k(bi_)], writes=[tb.k()])
                P.add("sp", lambda e: e.dma_start(out=ydst, in_=tb.ap[0:M, :]), reads=[tb.k()], writes=[("dram", "y", t, bi_)], dma=True, chan=("yout", ti))

            nxt4a = make_4a(t + 1) if t < 3 else []
            hbx = [B("hbx%d" % i, [D], BF16) for i in range(max(0, nb - len(hbb)))]
            hb_all = hbb + hbx
            for bi_ in range(nb):
                e_a(bi_)
            for bi_ in range(nb):
                e_b(bi_)
                for _ in range(2):
                    if nxt4a:
                        nxt4a.pop(0)()
            while nxt4a:
                nxt4a.pop(0)()
            for b in [w_pg, g_ple, sgg, x1] + h3b + pTb + pin + pbf + tmp + hbx:
                b.free()

        bulk_step(len(bulk))
        P.emit(st)
        print("arena peak bytes:", AR.peak, "ops:", {e: len(P.ops[e]) for e in ENGS}, "chans:", len(P.chan_cnt))
    return nc


_NC = None
_LAST = None


def _t5_bucket(dist):
    dist = np.asarray(dist).astype(np.int32)
    d = np.maximum(dist, 1).astype(np.float32)
    large = 16 + np.floor(np.log(d / 16) / np.log(2048 / 16) * 16).astype(np.int32)
    large = np.minimum(large, 31)
    return np.where(dist < 16, dist, large).astype(np.int32)


def kernel(**inp):
    global _NC
    if _NC is None:
        _NC = build_program()
    f32 = np.float32
    A = lambda k: np.ascontiguousarray(np.asarray(inp[k], dtype=f32))
    x_prompt = A("x_prompt"); x_sample = A("x_sample"); p_prompt = A("p_prompt")[0]; p_sample = A("p_sample")[0]
    caches = [A("cache_kv_w128")[0], A("cache_kv_w512")[0], A("cache_kv_w2048")[0]]
    state_conv = A("state_conv")[0]
    rel_bias = A("rel_bias")
    w_in = A("w_in")[0]
    perm = []
    for g in range(3):
        perm += list(range(256 * g, 256 * g + 256)) + list(range(768 + 256 * g, 768 + 256 * g + 256)) + list(range(1536 + 256 * g, 1536 + 256 * g + 256))
    perm += list(range(2304, 6400))
    w_in_r = np.ascontiguousarray(w_in[:, perm])
    vecs = np.ascontiguousarray(np.stack([A("norm_mix_pre")[0], A("norm_mix_post")[0], A("norm_ffn_pre")[0], A("norm_ffn_post")[0],
                                          A("ple_norm")[0], A("conv_b")[0], A("conv_ln_g")[0], A("conv_ln_b")[0]]))
    NEGC = f32(-200.0)
    kk = np.arange(128)[:, None]; qq = np.arange(128)[None, :]
    ebsrc = np.empty((128, 12, 2, 128), f32)
    sbias = np.empty((128, 12), f32)
    for h in range(12):
        dil = GROUPS[h // 4][1]
        jp = qq + 128 - kk
        jc = qq - kk
        ebsrc[:, h, 0, :] = np.where(kk >= qq, rel_bias[_t5_bucket(np.clip(jp, 0, 128) * dil), h], NEGC)
        ebsrc[:, h, 1, :] = np.where(kk <= qq, rel_bias[_t5_bucket(np.clip(jc, 0, 128) * dil), h], NEGC)
        sbias[:, h] = rel_bias[_t5_bucket((128 - np.arange(128)) * dil), h]
    ebsrc = np.ascontiguousarray(ebsrc.reshape(128, -1))
    sbias0 = np.ascontiguousarray(np.broadcast_to(rel_bias[0:1, :], (NS, 12))).astype(f32)
    c_selB = np.zeros((NS, NS, 128), f32)
    c_selT = np.zeros((128, NS, NS), f32)
    for b in range(NS):
        c_selB[b, b, :] = 1.0
        c_selT[:, b, b] = 1.0
    shared = {
        "ebsrc": ebsrc, "sbias": sbias, "sbias0": sbias0, "c_selB": c_selB.reshape(NS, -1), "c_selT": c_selT.reshape(128, -1),
        "w_in": w_in_r, "conv_w": A("conv_w")[0], "vecs": vecs, "w_conv_out": A("w_conv_out")[0], "w_attn_out": A("w_attn_out")[0],
        "w_out": A("w_out")[0], "w_ffn_in": A("w_ffn_in")[0], "w_ffn_out": A("w_ffn_out")[0], "w_ple_gate": A("w_ple_gate")[0],
        "w_ple_proj": A("w_ple_proj")[0],
    }
    in_maps = []
    for c in range(8):
        b, q = c // 4, c % 4
        s0 = NT * q
        m = dict(shared)
        m["xo"] = np.ascontiguousarray(x_prompt[b, s0:s0 + NT])
        m["xh"] = np.ascontiguousarray(x_prompt[b, s0 - NT:s0]) if q > 0 else np.zeros((NT, D), f32)
        m["hv"] = np.full((128, 1), 1.0 if q > 0 else 0.0, f32)
        m["po"] = np.ascontiguousarray(p_prompt[b, s0:s0 + NT])
        sl = slice(NS * c, NS * c + NS)
        m["xs"] = np.ascontiguousarray(x_sample[sl, 0])
        m["psm"] = np.ascontiguousarray(p_sample[sl, 0])
        m["c128"] = np.ascontiguousarray(caches[0][sl].reshape(NS, 128, 512))
        m["c512"] = np.ascontiguousarray(caches[1][sl].reshape(NS, 512, 512))
        m["c2048"] = np.ascontiguousarray(caches[2][sl].reshape(NS, 2048, 512))
        m["sconv"] = np.ascontiguousarray(state_conv[sl])
        in_maps.append(m)
    res = run_bass_kernel_spmd(_NC, in_maps, core_ids=list(range(8)))
    R = res.results
    if DEBUG:
        global _LAST
        _LAST = R
    y_prompt = np.stack([np.concatenate([R[4 * b + q]["y"] for q in range(4)], 0) for b in range(2)], 0).astype(f32)
    y_sample = np.concatenate([R[c]["ysm"] for c in range(8)], 0).reshape(128, 1, D).astype(f32)
    kvp = []
    for g, (win, _) in enumerate(GROUPS):
        name = ["kv128o", "kv512o", "kv2048o"][g]
        kvp.append(np.stack([R[4 * b + 3][name].reshape(win, 2, 4, 64) for b in range(2)], 0)[None].astype(f32))
    conv_p = np.stack([R[4 * b + 3]["convo"] for b in range(2)], 0)[None].astype(f32)
    kvsm = []
    for g, (win, _) in enumerate(GROUPS):
        name = ["kv128s", "kv512s", "kv2048s"][g]
        kvsm.append(np.concatenate([R[c][name] for c in range(8)], 0).reshape(128, win, 2, 4, 64)[None].astype(f32))
    conv_s = np.concatenate([R[c]["convs"] for c in range(8)], 0)[None].astype(f32)
    return (y_prompt, y_sample, kvp[0], kvp[1], kvp[2], conv_p, kvsm[0], kvsm[1], kvsm[2], conv_s)
```

```python
import numpy as np
import concourse.bass as bass
import concourse.mybir as mybir
from concourse.bass_utils import run_bass_kernel_spmd
from contextlib import ExitStack
import types

F32 = mybir.dt.float32
BF16 = mybir.dt.bfloat16
AF = mybir.ActivationFunctionType
ALU = mybir.AluOpType
AX = mybir.AxisListType

D = 1024
NT = 2048
NS = 16
DFF = 2816
NFF = 22
EPS = 1e-6
SCALE = 0.125
GROUPS = ((128, 1), (512, 4), (2048, 16))
ENGS = ("pe", "act", "dve", "pool", "sp")


class _Op:
    __slots__ = ("eng", "fn", "deps", "dma", "chan", "ndma", "cum", "cidx")

    def __init__(self, eng, fn, dma, chan):
        self.eng, self.fn, self.dma, self.chan = eng, fn, dma, chan
        self.deps = []
        self.ndma = 0
        self.cum = 0
        self.cidx = 0


def _freeze(fn, depth=0):
    if not isinstance(fn, types.FunctionType) or depth > 6:
        return fn
    dfl = fn.__defaults__
    if dfl:
        dfl = tuple(_freeze(d, depth + 1) if isinstance(d, types.FunctionType) else d for d in dfl)
    if fn.__closure__ is None:
        if dfl is fn.__defaults__:
            return fn
        return types.FunctionType(fn.__code__, fn.__globals__, fn.__name__, dfl, None)
    cells = []
    for c in fn.__closure__:
        try:
            v = c.cell_contents
        except ValueError:
            cells.append(c)
            continue
        if isinstance(v, types.FunctionType) and v is not fn:
            v = _freeze(v, depth + 1)
        cells.append(types.CellType(v))
    return types.FunctionType(fn.__code__, fn.__globals__, fn.__name__, dfl, tuple(cells))


def _summ(ops):
    best = {}
    for o in ops:
        if o.dma:
            k = ("c", o.chan)
            v = o.cum
        else:
            k = ("e", o.eng)
            v = o.cidx
        b = best.get(k)
        if b is None or v > b[0]:
            best[k] = (v, o)
    return [b[1] for b in best.values()]


class Prog:
    def __init__(self, nc):
        self.nc = nc
        self.ops = {e: [] for e in ENGS}
        self.state = {}
        self.by_name = {}
        self.ghost_ops = {}
        self.chan_cnt = {}
        self.chan_last = {}
        self.misc_rr = 0
        self.ccount = {e: 0 for e in ENGS}

    def _st(self, k):
        s = self.state.get(k)
        if s is None:
            s = [None, []]
            if isinstance(k, tuple):
                g = self.ghost_ops.get(k[0])
                if g:
                    s[1] = list(g)
                self.by_name.setdefault(k[0], []).append(k)
            self.state[k] = s
        return s

    def retire(self, name):
        ops = []
        for k in self.by_name.pop(name, []):
            s = self.state.pop(k)
            if s[0] is not None:
                ops.append(s[0])
            ops.extend(s[1])
        ops.extend(self.ghost_ops.pop(name, []))
        return _summ(ops)

    def add(self, eng, fn, reads=(), writes=(), dma=False, chan=None, ndma=1, extra=()):
        if dma and chan == "misc":
            self.misc_rr += 1
            chan = ("misc", self.misc_rr % 4)
        op = _Op(eng, _freeze(fn), dma, chan)
        deps = {}
        for o in extra:
            if o is not None:
                deps[id(o)] = o
        if dma:
            prev = self.chan_last.get(chan)
            if prev is not None:
                deps[id(prev)] = prev
            self.chan_last[chan] = op
        for k in reads:
            s = self._st(k)
            if s[0] is not None:
                deps[id(s[0])] = s[0]
        for k in writes:
            s = self._st(k)
            if s[0] is not None:
                deps[id(s[0])] = s[0]
            for r in s[1]:
                deps[id(r)] = r
        op.deps = _summ(deps.values())
        if dma:
            assert chan is not None
            op.ndma = ndma
            c = self.chan_cnt.get(chan, 0) + ndma
            self.chan_cnt[chan] = c
            op.cum = c
        else:
            self.ccount[eng] += 1
            op.cidx = self.ccount[eng]
        for k in reads:
            s = self._st(k)
            s[1].append(op)
            if len(s[1]) > 24:
                s[1] = _summ(s[1])
        for k in writes:
            s = self._st(k)
            s[0] = op
            s[1] = []
        self.ops[eng].append(op)
        return op

    def emit(self, stack):
        nc = self.nc
        sems = {e: stack.enter_context(nc.semaphore("s_" + e)) for e in ENGS}
        chan_sems = {}
        for i, c in enumerate(self.chan_cnt):
            chan_sems[c] = stack.enter_context(nc.semaphore("c%d" % i))
        block = stack.enter_context(nc.Block())
        engobj = {"pe": block.tensor, "act": block.scalar, "dve": block.vector,
                  "pool": block.gpsimd, "sp": block.sync}
        prog = self

        def make(ename):
            def body(eng):
                waited = {}
                for op in prog.ops[ename]:
                    for d in op.deps:
                        if d.dma:
                            sem, val, key = chan_sems[d.chan], 16 * d.cum, ("c", d.chan)
                        else:
                            if d.eng == ename and ename == "pe":
                                continue
                            sem, val, key = sems[d.eng], d.cidx, ("e", d.eng)
                        if waited.get(key, 0) >= val:
                            continue
                        waited[key] = val
                        eng.wait_ge(sem, val)
                    res = op.fn(eng)
                    if op.dma:
                        if not isinstance(res, (list, tuple)):
                            res = [res]
                        assert len(res) == op.ndma, (len(res), op.ndma)
                        for r in res:
                            r.then_inc(chan_sems[op.chan], 16)
                    else:
                        if isinstance(res, (list, tuple)):
                            res = res[-1]
                        res.then_inc(sems[ename], 1)
                if ename == "sp":
                    for c, n in prog.chan_cnt.items():
                        eng.wait_ge(chan_sems[c], 16 * n)
            return body

        for e in ENGS:
            engobj[e](make(e))


class Arena:
    def __init__(self, prog, tensor, nbytes):
        self.P, self.t, self.n = prog, tensor, nbytes
        self.free = [(0, nbytes)]
        self.bufs = {}
        self.ghosts = []
        self.uid = 0
        self.peak = 0

    def alloc(self, base, nbytes):
        nbytes = (nbytes + 63) // 64 * 64
        self.uid += 1
        name = "%s#%d" % (base, self.uid)
        for i, (o, s) in enumerate(self.free):
            if s >= nbytes:
                if s == nbytes:
                    self.free.pop(i)
                else:
                    self.free[i] = (o + nbytes, s - nbytes)
                off = o
                break
        else:
            raise RuntimeError("arena full: %s %d free=%s" % (base, nbytes, self.free))
        self.bufs[name] = (off, nbytes)
        self.peak = max(self.peak, off + nbytes)
        ops = []
        for (go, gs, gops) in self.ghosts:
            if go < off + nbytes and off < go + gs:
                ops.extend(gops)
        if ops:
            self.P.ghost_ops[name] = _summ(ops)
        return name, off

    def release(self, name):
        off, nbytes = self.bufs.pop(name)
        ops = self.P.retire(name)
        self.ghosts = [g for g in self.ghosts if not (g[0] >= off and g[0] + g[1] <= off + nbytes)]
        if ops:
            self.ghosts.append((off, nbytes, ops))
        self.free.append((off, nbytes))
        self.free.sort()
        m = []
        for o, s in self.free:
            if m and m[-1][0] + m[-1][1] == o:
                m[-1] = (m[-1][0], m[-1][1] + s)
            else:
                m.append((o, s))
        self.free = m

    def f32(self, off, n):
        return self.t[:, off // 4: off // 4 + n]

    def bf16(self, off, n):
        return self.t[:, off // 4: off // 4 + (n + 1) // 2].bitcast(BF16)


class Buf:
    def __init__(self, ar, base, shape, dt):
        self.ar = ar
        n = int(np.prod(shape))
        self.shape = shape
        self.dt = dt
        self.name, self.off = ar.alloc(base, n * (4 if dt == F32 else 2))
        flat = ar.f32(self.off, n) if dt == F32 else ar.bf16(self.off, n)
        if len(shape) == 1:
            self.ap = flat
        elif len(shape) == 2:
            self.ap = flat.rearrange("p (a b) -> p a b", a=shape[0])
        elif len(shape) == 3:
            self.ap = flat.rearrange("p (a b c) -> p a b c", a=shape[0], b=shape[1])
        else:
            self.ap = flat.rearrange("p (a b c d) -> p a b c d", a=shape[0], b=shape[1], c=shape[2])

    def k(self, *idx):
        return (self.name,) + idx

    def free(self):
        self.ar.release(self.name)


DEBUG = False


def build_program():
    nc = bass.Bass("TRN2", target_bir_lowering=False)

    def din(name, shape):
        return nc.dram_tensor(name, list(shape), F32, kind="ExternalInput").ap()

    def dout(name, shape):
        return nc.dram_tensor(name, list(shape), F32, kind="ExternalOutput").ap()

    xo = din("xo", [NT, D]); xh = din("xh", [NT, D]); po = din("po", [NT, 256])
    xs = din("xs", [NS, D]); psm = din("psm", [NS, 256])
    cch = [din("c128", [NS, 128, 512]), din("c512", [NS, 512, 512]), din("c2048", [NS, 2048, 512])]
    sconv = din("sconv", [NS, 30, D])
    hv = din("hv", [128, 1])
    ebsrc = din("ebsrc", [128, 12 * 2 * 128])
    sbias = din("sbias", [128, 12]); sbias0 = din("sbias0", [NS, 12])
    c_selB = din("c_selB", [NS, NS * 128]); c_selT = din("c_selT", [128, NS * NS])
    w_in = din("w_in", [D, 6400]); conv_w = din("conv_w", [31, D])
    vecs = din("vecs", [8, D])
    w_conv_out = din("w_conv_out", [D, D]); w_attn_out = din("w_attn_out", [256, D]); w_out = din("w_out", [D, D])
    w_ffn_in = din("w_ffn_in", [D, 2 * DFF]); w_ffn_out = din("w_ffn_out", [DFF, D])
    w_ple_gate = din("w_ple_gate", [D, D]); w_ple_proj = din("w_ple_proj", [256, D])

    y = dout("y", [NT, D]); ysm = dout("ysm", [NS, D])
    kvo = [dout("kv128o", [128, 512]), dout("kv512o", [512, 512]), dout("kv2048o", [2048, 512])]
    convo = dout("convo", [30, D])
    kvs = [dout("kv128s", [NS, 128, 512]), dout("kv512s", [NS, 512, 512]), dout("kv2048s", [NS, 2048, 512])]
    convs = dout("convs", [NS, 30, D])

    wscr = nc.dram_tensor("wscr", [16, 128, 8192], BF16, kind="Internal").ap()
    gscr = nc.dram_tensor("gscr", [5, 128, D], F32, kind="Internal").ap()

    st = ExitStack()
    with st:
        ARENA_BYTES = 206 * 1024
        arena_t = st.enter_context(nc.sbuf_tensor("arena", [128, ARENA_BYTES // 4], F32))
        psum = st.enter_context(nc.psum_tensor("psum", [128, 4096], F32))
        P = Prog(nc)
        AR = Arena(P, arena_t, ARENA_BYTES)

        def B(base, shape, dt=F32):
            return Buf(AR, base, shape, dt)

        def dbg(name, ap2d, rows, cols, keys):
            if not DEBUG:
                return
            tt = nc.dram_tensor("dbg_" + name, [rows, cols], F32, kind="ExternalOutput").ap()
            P.add("pool", lambda e: e.dma_start(out=tt, in_=ap2d), reads=keys, writes=[("dram", "dbg", name)], dma=True, chan="dbg")

        def ps(i, n=512, o=0):
            return psum[:, 512 * i + o: 512 * i + o + n]

        def psk(*banks):
            return [("ps", b) for b in banks]

        def psbf(i):
            return psum[:, 512 * i: 512 * (i + 1)].bitcast(BF16)

        rr = {}

        def rot(name, n):
            v = rr.get(name, 0)
            rr[name] = v + 1
            return v % n

        cst = B("cst", [128 + 128 + 64], BF16)
        ident = cst.ap[:, 0:128]
        ones_bf = cst.ap[:, 128:256]
        identf = B("identf", [128], F32)
        epsb = B("eps", [1], F32)
        hvb = B("hvb", [1], F32)
        vcol = B("vcol", [3, 8], F32)
        wT = B("wT", [8, 31], F32)

        P.add("pool", lambda e: e.memset(identf.ap, 0.0), writes=[identf.k()])
        P.add("pool", lambda e: e.affine_select(out=identf.ap, in_=identf.ap, pattern=[[-1, 128]], compare_op=ALU.not_equal,
                                                fill=1.0, base=0, channel_multiplier=1), writes=[identf.k()])
        P.add("dve", lambda e: e.tensor_copy(out=ident, in_=identf.ap), reads=[identf.k()], writes=[cst.k("i")])
        P.add("dve", lambda e: e.memset(cst.ap[:, 128:320], 1.0), writes=[cst.k("o")])
        P.add("dve", lambda e: e.memset(epsb.ap, EPS), writes=[epsb.k()])
        P.add("sp", lambda e: e.dma_start(out=hvb.ap, in_=hv), writes=[hvb.k()], dma=True, chan="misc")

        bulk = []
        def _rows(dst, src, b_, r0, r1):
            n = r1 - r0
            n8 = n - n % 8
            if n8:
                bulk.append((dst[b_, r0:r0 + n8, :].rearrange("(a l) c -> a (l c)", l=8), src[b_, r0 + 1:r0 + 1 + n8, :].rearrange("(a l) c -> a (l c)", l=8)))
            if n % 8:
                bulk.append((dst[b_, r0 + n8:r1, :], src[b_, r0 + 1 + n8:r1 + 1, :]))
        for b_ in range(NS):
            for (r0, r1) in ((0, 512), (512, 1024), (1024, 1536), (1536, 2047)):
                _rows(kvs[2], cch[2], b_, r0, r1)
        for b_ in range(NS):
            _rows(kvs[1], cch[1], b_, 0, 511)
        for j in range(4):
            bulk.append((kvs[0][4 * j:4 * j + 4, 0:127, :].rearrange("b l c -> b (l c)"), cch[0][4 * j:4 * j + 4, 1:128, :].rearrange("b l c -> b (l c)")))
        for j in range(2):
            bulk.append((convs[8 * j:8 * j + 8, 0:29, :].rearrange("b l c -> b (l c)"), sconv[8 * j:8 * j + 8, 1:30, :].rearrange("b l c -> b (l c)")))
        bulk_i = {"i": 0}

        def bulk_step(n=1):
            for _ in range(n):
                i = bulk_i["i"]
                if i >= len(bulk):
                    return
                bulk_i["i"] = i + 1
                o_, i_ = bulk[i]
                P.add("act", lambda e, o_=o_, i_=i_: e.dma_start(out=o_, in_=i_), reads=[], writes=[("dram", "bulk", i)],
                      dma=True, chan=("cpy", i % 2))

        vrow = B("vrow", [D], F32)
        P.add("sp", lambda e: e.dma_start(out=vrow.ap[0:3, :], in_=vecs[5:8, :]), writes=[vrow.k()], dma=True, chan="misc")

        def tr_vc(e):
            r = None
            for kc in range(8):
                r = e.transpose(out=ps(1, 3, 3 * kc), in_=vrow.ap[0:3, kc * 128:(kc + 1) * 128], identity=identf.ap[0:3, 0:3])
            return r
        P.add("pe", tr_vc, reads=[vrow.k(), identf.k()], writes=psk(1))
        P.add("act", lambda e: e.activation(out=vcol.ap.rearrange("p v k -> p k v"), in_=ps(1, 24).rearrange("p (k v) -> p k v", v=3), func=AF.Copy),
              reads=[], writes=psk(1) + [vcol.k()])
        vrow.free()

        cw = B("cw", [D], F32)
        P.add("sp", lambda e: e.dma_start(out=cw.ap[0:31, :], in_=conv_w), writes=[cw.k()], dma=True, chan="misc")

        def tr_cw(e):
            r = None
            for kc in range(8):
                r = e.transpose(out=ps(0, 31, 31 * kc), in_=cw.ap[0:31, kc * 128:(kc + 1) * 128], identity=identf.ap[0:31, 0:31])
            return r
        P.add("pe", tr_cw, reads=[cw.k(), identf.k()], writes=psk(0))
        P.add("act", lambda e: e.activation(out=wT.ap.rearrange("p a b -> p (a b)"), in_=ps(0, 248), func=AF.Copy),
              reads=[], writes=psk(0) + [wT.k()])
        cw.free()

        onesf = B("onesf", [128], F32)
        grow = B("grow", [5, D], F32)
        gtmp = B("gtmp", [D], F32)
        P.add("dve", lambda e: e.memset(onesf.ap, 1.0), writes=[onesf.k()])
        for r_ in range(5):
            P.add("sp", lambda e, r_=r_: e.dma_start(out=grow.ap[0:1, r_, :], in_=vecs[r_:r_ + 1, :]), writes=[grow.k(r_)], dma=True, chan="misc")

            def gb_mm(e, r_=r_):
                e.matmul(out=ps(2), lhsT=onesf.ap[0:1, :], rhs=grow.ap[0:1, r_, 0:512], start=True, stop=True)
                return e.matmul(out=ps(3), lhsT=onesf.ap[0:1, :], rhs=grow.ap[0:1, r_, 512:1024], start=True, stop=True)
            P.add("pe", gb_mm, reads=[onesf.k(), grow.k(r_)], writes=psk(2, 3))
            P.add("act", lambda e: e.activation(out=gtmp.ap, in_=psum[:, 1024:2048], func=AF.Copy), reads=[], writes=psk(2, 3) + [gtmp.k()])
            P.add("sp", lambda e, r_=r_: e.dma_start(out=gscr[r_], in_=gtmp.ap), reads=[gtmp.k()], writes=[("dram", "gscr", r_)], dma=True, chan="misc")
        onesf.free(); grow.free(); gtmp.free()

        def load_gain(row, name):
            gb = B(name, [D], F32)
            P.add("sp", lambda e: e.dma_start(out=gb.ap, in_=gscr[row]), reads=[("dram", "gscr", row)],
                  writes=[gb.k()], dma=True, chan="misc")
            return gb

        small = B("small", [3, 4], F32)

        def rms_to_bf16(src_ap, src_keys, M, gain, hb_ap, hb_key, src_is_psum=False):
            i = rot("small", 4)
            ms = small.ap[0:M, 0, i:i + 1]; sd = small.ap[0:M, 1, i:i + 1]; rs = small.ap[0:M, 2, i:i + 1]
            rk = src_keys if not src_is_psum else []
            wk = src_keys if src_is_psum else []
            P.add("act", lambda e: e.activation(out=junk.ap[0:M, :], in_=src_ap, func=AF.Square, scale=1.0 / 32.0, accum_out=ms),
                  reads=rk, writes=wk + [small.k(0, i)])
            P.add("act", lambda e: e.activation(out=sd, in_=ms, func=AF.Sqrt, bias=epsb.ap[0:M, 0:1], scale=1.0),
                  reads=[small.k(0, i), epsb.k()], writes=[small.k(1, i)])
            P.add("dve", lambda e: e.reciprocal(out=rs, in_=sd), reads=[small.k(1, i)], writes=[small.k(2, i)])
            P.add("dve", lambda e: e.scalar_tensor_tensor(out=hb_ap, in0=src_ap, scalar=rs, in1=gain.ap[0:M, :], op0=ALU.mult, op1=ALU.mult),
                  reads=rk + [small.k(2, i), gain.k()], writes=wk + [hb_key])

        def transpose_rows(hb_ap, hb_key, M, nchunk, dst_fn, dst_keys, bank, evac="act"):
            pb = psbf(bank)

            def f(e):
                r = None
                for kc in range(nchunk):
                    r = e.transpose(out=pb[:, kc * M:(kc + 1) * M], in_=hb_ap[:, kc * 128:(kc + 1) * 128], identity=ident[0:M, 0:M])
                return r
            P.add("pe", f, reads=[hb_key, cst.k("i")], writes=psk(bank))
            src = pb[:, 0:nchunk * M].rearrange("p (a b) -> p a b", a=nchunk)
            if evac == "act":
                P.add("act", lambda e: e.activation(out=dst_fn(), in_=src, func=AF.Copy), reads=[], writes=psk(bank) + dst_keys)
            else:
                P.add("dve", lambda e: e.tensor_copy(out=dst_fn(), in_=src), reads=[], writes=psk(bank) + dst_keys)

        def load_w_own(name, src2d, nchunk, ncol, chan):
            wb = B(name, [nchunk, ncol], BF16)
            P.add("pool", lambda e: e.dma_start(out=wb.ap, in_=src2d.rearrange("(kc p) c -> p kc c", p=128)),
                  writes=[wb.k()], dma=True, chan=chan)
            return wb

        wring = []
        wr_i = {"i": 0}

        class WView:
            def __init__(self, slot, ap):
                self.slot, self.ap = slot, ap

            def k(self, *a):
                return self.slot.k()

            def free(self):
                pass

        def wslot():
            sl = wring[wr_i["i"] % len(wring)]
            wr_i["i"] += 1
            return sl

        def add_wslot():
            sl = B("wr%d" % len(wring), [8192], BF16)
            sl.idx = len(wring)
            wring.append(sl)

        wl = {"tile": None, "li": 0}

        def ring_load(sl, issue_fn, ndma, tile="cur", li=None):
            if tile == "cur":
                tile = wl["tile"]
            if tile is not None and li is None:
                li = wl["li"]
                wl["li"] += 1
            if tile is None or tile == 0:
                P.add("pool", issue_fn, writes=[sl.k()], dma=True, chan=("w", sl.idx), ndma=ndma)
                if tile == 0:
                    P.add("act", lambda e: e.dma_start(out=wscr[li], in_=sl.ap), reads=[sl.k()], writes=[("dram", "wscr", li)],
                          dma=True, chan=("wst", li % 2))
            else:
                P.add("pool", lambda e: e.dma_start(out=sl.ap, in_=wscr[li]), reads=[("dram", "wscr", li)], writes=[sl.k()],
                      dma=True, chan=("w", sl.idx))

        def load_w_pair(srcA, srcB, ncol, tile="cur", li=None):
            sl = wslot()
            ap4 = sl.ap[:, 0:16 * ncol].rearrange("p (a b c) -> p a b c", a=8, b=2)

            def ld(e):
                a_ = e.dma_start(out=ap4[:, :, 0, :], in_=srcA.rearrange("(kc p) c -> p kc c", p=128))
                b_ = e.dma_start(out=ap4[:, :, 1, :], in_=srcB.rearrange("(kc p) c -> p kc c", p=128))
                return [a_, b_]
            ring_load(sl, ld, 2, tile, li)
            return WView(sl, ap4[:, :, 0, :]), WView(sl, ap4[:, :, 1, :])

        def load_w(name, src2d, nchunk, ncol, chan=None):
            sl = wslot()
            ap = sl.ap[:, 0:nchunk * ncol].rearrange("p (a b) -> p a b", a=nchunk)
            ring_load(sl, lambda e: e.dma_start(out=ap, in_=src2d.rearrange("(kc p) c -> p kc c", p=128)), 1)
            return WView(sl, ap)

        junk = B("junk", [D], F32)

        g_pre = load_gain(0, "g_pre")
        hT = B("hT", [8, NT], BF16)
        hTh = B("hTh", [8, NT], BF16)
        hsT = B("hsT", [8, NS], BF16)
        hTh30 = B("hTh30", [8, 32], BF16)
        xin = [B("xin%d" % i, [D], F32) for i in range(3)]
        hbb = [B("hbb%d" % i, [D], BF16) for i in range(2)]

        hbb.append(B("hbb2", [D], BF16))
        p1 = []
        for blk in range(16):
            p1.append((xh[blk * 128:(blk + 1) * 128, :], 128, (lambda blk=blk: hTh.ap[:, :, blk * 128:(blk + 1) * 128]), [hTh.k(blk)]))
        p1.append((xs, NS, (lambda: hsT.ap), [hsT.k()]))
        for blk in range(16):
            p1.append((xo[blk * 128:(blk + 1) * 128, :], 128, (lambda blk=blk: hT.ap[:, :, blk * 128:(blk + 1) * 128]), [hT.k(blk)]))
        p1hb = {}

        def p1_a(ix):
            src_rows, M, dst_fn, dst_keys = p1[ix]
            i = rot("xin", len(xin))
            xb = xin[i]
            P.add("sp", lambda e: e.dma_start(out=xb.ap[0:M, :], in_=src_rows), writes=[xb.k()], dma=True, chan=("xin", i))
            hb = hbb[rot("hbb", len(hbb))]
            p1hb[ix] = hb
            rms_to_bf16(xb.ap[0:M, :], [xb.k()], M, g_pre, hb.ap[0:M, :], hb.k())

        def p1_b(ix):
            src_rows, M, dst_fn, dst_keys = p1[ix]
            hb = p1hb[ix]
            bank = rot("p1bank", 2)
            transpose_rows(hb.ap[0:M, :], hb.k(), M, 8, dst_fn, dst_keys, bank, evac=("act" if bank == 0 else "dve"))
            if ix == 15:
                P.add("dve", lambda e: e.tensor_copy(out=hTh30.ap[:, :, 0:30], in_=hTh.ap[:, :, NT - 30:NT]), reads=[hTh.k(15)], writes=[hTh30.k()])

        for ix in range(len(p1) + 1):
            if ix < len(p1):
                p1_a(ix)
            if ix > 0:
                p1_b(ix - 1)
        g_pre.free()
        while xin:
            xin.pop().free()
        hTh_all = [hTh.k(b) for b in range(16)]
        hT_all = [hT.k(b) for b in range(16)]

        def blk_cols(g, r, qb):
            dil = GROUPS[g][1]
            s0 = r + dil * 128 * qb
            return slice(s0, s0 + dil * 127 + 1, dil)

        add_wslot(); add_wslot()
        kTh = [B("kTh0", [2, 128], BF16), B("kTh1", [2, 512], BF16), B("kTh2", [2, NT], BF16)]
        vh = [B("vh0", [1, 256], BF16), B("vh1", [4, 256], BF16), B("vh2", [16, 256], BF16)]
        halo_lo = [NT - 128, NT - 512, 0]
        evt = {"n": 0}

        def evac_copy(out_ap, in_ap, reads, writes, scale=None):
            evt["n"] += 1
            if evt["n"] % 2 == 0:
                if scale is None:
                    P.add("act", lambda e: e.activation(out=out_ap, in_=in_ap, func=AF.Copy), reads=reads, writes=writes)
                else:
                    P.add("act", lambda e: e.activation(out=out_ap, in_=in_ap, func=AF.Copy, scale=scale), reads=reads, writes=writes)
            else:
                if scale is None:
                    P.add("dve", lambda e: e.tensor_copy(out=out_ap, in_=in_ap), reads=reads, writes=writes)
                else:
                    P.add("dve", lambda e: e.tensor_scalar(out=out_ap, in0=in_ap, scalar1=scale, scalar2=None, op0=ALU.mult), reads=reads, writes=writes)

        def proj_fm(wb, wcol0, src_ap_fn, src_keys, ncols, dst_ap, dst_keys, scale=None):
            bank = rot("pbank", 4)

            def f(e):
                r = None
                for kc in range(8):
                    r = e.matmul(out=ps(bank, ncols), lhsT=wb.ap[:, kc, wcol0:wcol0 + 128], rhs=src_ap_fn(kc), start=(kc == 0), stop=(kc == 7))
                return r
            P.add("pe", f, reads=[wb.k()] + src_keys, writes=psk(bank))
            evac_copy(dst_ap, ps(bank, ncols), [], psk(bank) + dst_keys, scale)

        for g in range(3):
            wq = load_w("wqkvh", w_in[:, 768 * g:768 * g + 768], 8, 768, ("w", rot("wslot", 3)))
            lo = halo_lo[g]
            nh = NT - lo
            for c2 in range(2):
                for t0 in range(0, nh, 512):
                    n = min(512, nh - t0)
                    proj_fm(wq, 256 + 128 * c2, (lambda kc, a=lo + t0, n=n: hTh.ap[:, kc, a:a + n]), hTh_all, n,
                            kTh[g].ap[:, c2, t0:t0 + n], [kTh[g].k(c2, t0)])
            nres = GROUPS[g][1]
            for r in range(nres):
                cols = blk_cols(g, r, {0: 15, 1: 3, 2: 0}[g])
                bank = rot("pbank", 4)

                def f(e, cols=cols, bank=bank, wq=wq):
                    rr_ = None
                    for kc in range(8):
                        rr_ = e.matmul(out=ps(bank, 256), lhsT=hTh.ap[:, kc, cols], rhs=wq.ap[:, kc, 512:768], start=(kc == 0), stop=(kc == 7))
                    return rr_
                P.add("pe", f, reads=[wq.k()] + hTh_all, writes=psk(bank))
                evac_copy(vh[g].ap[:, r, :], ps(bank, 256), [], psk(bank) + [vh[g].k(r)])
            wq.free()
        hTh.free()

        numacc = B("numacc", [2, NT], F32)
        denacc = B("denacc", [2, NT], F32)
        P.add("dve", lambda e: e.memset(numacc.ap, 0.0), writes=[numacc.k()])
        P.add("dve", lambda e: e.memset(denacc.ap, 0.0), writes=[denacc.k()])
        oT = B("oT", [2, NT], BF16)
        qs_s = B("qs_s", [3, 768], F32)
        ebuf = [B("ebuf%d" % i, [256], F32) for i in range(4)]
        pbuf = [B("pbuf%d" % i, [256], BF16) for i in range(4)]
        kvst = [B("kvst%d" % i, [512], F32) for i in range(2)]

        for g in range(3):
            win, dil = GROUPS[g]
            wq = load_w("wqkv", w_in[:, 768 * g:768 * g + 768], 8, 768, ("w", rot("wslot", 3)))
            ebs = B("ebs", [4, 256], F32)
            P.add("sp", lambda e, g=g: e.dma_start(out=ebs.ap.rearrange("p a b -> p (a b)"), in_=ebsrc[:, g * 1024:(g + 1) * 1024]),
                  writes=[ebs.k()], dma=True, chan="misc")
            EBn = B("EBn", [4, 256], F32)
            EBf = B("EBf", [4, 256], F32)
            P.add("act", lambda e: e.activation(out=EBn.ap, in_=ebs.ap, func=AF.Exp), reads=[ebs.k()], writes=[EBn.k()])
            P.add("dve", lambda e: e.tensor_copy(out=EBf.ap[:, :, 128:256], in_=EBn.ap[:, :, 128:256]), reads=[EBn.k()], writes=[EBf.k("c")])
            P.add("dve", lambda e: e.tensor_scalar(out=EBf.ap[:, :, 0:128], in0=EBn.ap[:, :, 0:128], scalar1=hvb.ap[:, 0:1], scalar2=None, op0=ALU.mult),
                  reads=[EBn.k(), hvb.k()], writes=[EBf.k("p")])
            qT = B("qT", [2, NT], BF16)
            kT = B("kT", [2, NT], BF16)
            vo = B("vo", [16, 256], BF16)
            for c2 in range(2):
                for t in range(4):
                    proj_fm(wq, 128 * c2, (lambda kc, t=t: hT.ap[:, kc, t * 512:(t + 1) * 512]), hT_all, 512,
                            qT.ap[:, c2, t * 512:(t + 1) * 512], [qT.k(c2, t)], scale=SCALE)
                    proj_fm(wq, 256 + 128 * c2, (lambda kc, t=t: hT.ap[:, kc, t * 512:(t + 1) * 512]), hT_all, 512,
                            kT.ap[:, c2, t * 512:(t + 1) * 512], [kT.k(c2, t)])
            qT_all = [qT.k(c2, t) for c2 in range(2) for t in range(4)]
            kT_all = [kT.k(c2, t) for c2 in range(2) for t in range(4)]
            nq = 16 // dil if dil < 16 else 1
            nres = dil
            for r in range(nres):
                for qb in range(nq):
                    bi = r * nq + qb
                    need_out = (qb == nq - 1)
                    cols = blk_cols(g, r, qb)
                    bank = rot("pbank", 4)
                    c0 = 256 if need_out else 512
                    ncol = 768 - c0

                    def f(e, cols=cols, bank=bank, c0=c0, ncol=ncol, wq=wq):
                        rr_ = None
                        for kc in range(8):
                            rr_ = e.matmul(out=ps(bank, ncol), lhsT=hT.ap[:, kc, cols], rhs=wq.ap[:, kc, c0:768], start=(kc == 0), stop=(kc == 7))
                        return rr_
                    P.add("pe", f, reads=[wq.k()] + hT_all, writes=psk(bank))
                    if need_out:
                        si = rot("kvst", 2)
                        sb_ = kvst[si]
                        P.add("act", lambda e, sb_=sb_, bank=bank: e.activation(out=sb_.ap, in_=ps(bank, 512), func=AF.Copy), reads=[], writes=psk(bank) + [sb_.k()])
                        P.add("dve", lambda e, bi=bi, bank=bank: e.tensor_copy(out=vo.ap[:, bi, :], in_=ps(bank, 256, 256)), reads=[], writes=psk(bank) + [vo.k(bi)])
                        P.add("sp", lambda e, sb_=sb_, r=r, g=g, dil=dil: e.dma_start(out=kvo[g][r::dil, :] if dil > 1 else kvo[g], in_=sb_.ap),
                              reads=[sb_.k()], writes=[("dram", "kvo", g, r)], dma=True, chan=("kvst", si))
                    else:
                        evac_copy(vo.ap[:, bi, :], ps(bank, 256), [], psk(bank) + [vo.k(bi)])
            bank = rot("pbank", 4)
            b2 = rot("pbank", 4)

            def fs(e, bank=bank, b2=b2, wq=wq):
                rr_ = None
                for kc in range(8):
                    e.matmul(out=psum[0:NS, 512 * bank:512 * bank + 512], lhsT=hsT.ap[:, kc, :], rhs=wq.ap[:, kc, 0:512], start=(kc == 0), stop=(kc == 7))
                for kc in range(8):
                    rr_ = e.matmul(out=psum[0:NS, 512 * b2:512 * b2 + 256], lhsT=hsT.ap[:, kc, :], rhs=wq.ap[:, kc, 512:768], start=(kc == 0), stop=(kc == 7))
                return rr_
            P.add("pe", fs, reads=[wq.k(), hsT.k()], writes=psk(bank, b2))
            P.add("act", lambda e, g=g, bank=bank: e.activation(out=qs_s.ap[0:NS, g, 0:512], in_=psum[0:NS, 512 * bank:512 * bank + 512], func=AF.Copy),
                  reads=[], writes=psk(bank) + [qs_s.k(g, 0)])
            P.add("act", lambda e, g=g, b2=b2: e.activation(out=qs_s.ap[0:NS, g, 512:768], in_=psum[0:NS, 512 * b2:512 * b2 + 256], func=AF.Copy),
                  reads=[], writes=psk(b2) + [qs_s.k(g, 1)])
            P.add("sp", lambda e, g=g, win=win: e.dma_start(out=kvs[g][:, win - 1, :], in_=qs_s.ap[0:NS, g, 256:768]),
                  reads=[qs_s.k(g, 0), qs_s.k(g, 1)], writes=[("dram", "kvs_new", g)], dma=True, chan="misc")
            wq.free()

            vo_all = [vo.k(b) for b in range(16)]
            for c2 in range(2):
                for quad in range(4):
                    if g == 0:
                        qblocks = [(0, 4 * quad + j) for j in range(4)]
                    elif g == 1:
                        qblocks = [(quad, j) for j in range(4)]
                    else:
                        qblocks = [(4 * quad + j, 0) for j in range(4)]
                    nb_, db_ = ((6, 7) if rot("ndbank", 2) == 0 else (2, 3))
                    units = [(hh, j, r, qb) for hh in range(2) for j, (r, qb) in enumerate(qblocks)]
                    ust = {}

                    def att_a(u):
                        hh, j, r, qb = units[u]
                        s_ = 2 * c2 + hh
                        pr = slice(64 * hh, 64 * hh + 64)
                        bi = r * nq + qb
                        qcols = blk_cols(g, r, qb)
                        sbank = (4, 5, 0, 1)[rot("sbank", 4)]
                        first = (qb == 0)
                        if first:
                            hcols = blk_cols(g, r, {0: 15, 1: 3, 2: 0}[g])
                            hc = slice(hcols.start - halo_lo[g], hcols.stop - halo_lo[g], hcols.step)
                            kprev = kTh[g].ap[pr, c2, hc]
                            vprev = vh[g].ap[:, r, 64 * s_:64 * s_ + 64]
                            kprev_keys = [kTh[g].k(c2, t0) for t0 in range(0, NT - halo_lo[g], 512)]
                            vprev_keys = [vh[g].k(r)]
                        else:
                            kprev = kT.ap[pr, c2, blk_cols(g, r, qb - 1)]
                            vprev = vo.ap[:, bi - 1, 64 * s_:64 * s_ + 64]
                            kprev_keys = kT_all
                            vprev_keys = [vo.k(bi - 1)]
                        kcur = kT.ap[pr, c2, qcols]
                        vcur = vo.ap[:, bi, 64 * s_:64 * s_ + 64]
                        qsl = qT.ap[pr, c2, qcols]

                        def fsc(e):
                            e.matmul(out=ps(sbank, 128, 0), lhsT=kprev, rhs=qsl, start=True, stop=True)
                            return e.matmul(out=ps(sbank, 128, 128), lhsT=kcur, rhs=qsl, start=True, stop=True)
                        P.add("pe", fsc, reads=qT_all + kT_all + kprev_keys, writes=psk(sbank))
                        eb = ebuf[rot("ebuf", 4)]
                        P.add("act", lambda e: e.activation(out=eb.ap, in_=ps(sbank, 256), func=AF.Exp), reads=[], writes=psk(sbank) + [eb.k()])
                        pb = pbuf[rot("pbuf", 4)]
                        EB = EBf if first else EBn
                        ebv = EB.ap[:, s_, :]
                        P.add("dve", lambda e: e.tensor_tensor(out=pb.ap, in0=eb.ap, in1=ebv, op=ALU.mult),
                              reads=[eb.k(), EB.k("c"), EB.k("p"), EB.k()], writes=[pb.k()])
                        ust[u] = (pb, vprev, vcur, pr, j, vprev_keys, bi)

                    def att_b(u):
                        pb, vprev, vcur, pr, j, vprev_keys, bi = ust[u]

                        def fpv(e):
                            on = psum[pr, 512 * nb_ + 128 * j: 512 * nb_ + 128 * j + 128]
                            od = psum[pr, 512 * db_ + 128 * j: 512 * db_ + 128 * j + 128]
                            e.matmul(out=on, lhsT=vprev, rhs=pb.ap[:, 0:128], start=True, stop=False)
                            e.matmul(out=on, lhsT=vcur, rhs=pb.ap[:, 128:256], start=False, stop=True)
                            e.matmul(out=od, lhsT=cst.ap[:, 256:320], rhs=pb.ap[:, 0:128], start=True, stop=False)
                            return e.matmul(out=od, lhsT=cst.ap[:, 256:320], rhs=pb.ap[:, 128:256], start=False, stop=True)
                        P.add("pe", fpv, reads=[pb.k(), cst.k("o")] + vprev_keys + [vo.k(bi)], writes=psk(nb_, db_))

                    for u in range(len(units) + 2):
                        if u < len(units):
                            att_a(u)
                        if u > 1:
                            att_b(u - 2)
                    if g == 0:
                        dn = numacc.ap[:, c2, 512 * quad:512 * quad + 512]
                        dd = denacc.ap[:, c2, 512 * quad:512 * quad + 512]
                        sn, sd_ = ps(nb_), ps(db_)
                    elif g == 1:
                        dn = numacc.ap[:, c2, quad::4]
                        dd = denacc.ap[:, c2, quad::4]
                        sn, sd_ = ps(nb_), ps(db_)
                    else:
                        dn = numacc.ap[:, c2, :].rearrange("p (i r) -> p r i", r=16)[:, 4 * quad:4 * quad + 4, :]
                        dd = denacc.ap[:, c2, :].rearrange("p (i r) -> p r i", r=16)[:, 4 * quad:4 * quad + 4, :]
                        sn = ps(nb_).rearrange("p (a b) -> p a b", a=4)
                        sd_ = ps(db_).rearrange("p (a b) -> p a b", a=4)
                    P.add("dve", lambda e, dn=dn, sn=sn: e.tensor_tensor(out=dn, in0=sn, in1=dn, op=ALU.add),
                          reads=[], writes=psk(nb_) + [numacc.k()])
                    P.add("dve", lambda e, dd=dd, sd_=sd_: e.tensor_tensor(out=dd, in0=sd_, in1=dd, op=ALU.add),
                          reads=[], writes=psk(db_) + [denacc.k()])
                    bulk_step(2)
            qT.free(); kT.free(); vo.free(); ebs.free(); EBn.free(); EBf.free()

        P.add("dve", lambda e: e.reciprocal(out=denacc.ap, in_=denacc.ap), reads=[], writes=[denacc.k()])
        P.add("dve", lambda e: e.tensor_tensor(out=oT.ap, in0=numacc.ap, in1=denacc.ap, op=ALU.mult),
              reads=[numacc.k(), denacc.k()], writes=[oT.k()])
        dbg("oT", oT.ap.rearrange("p a b -> p (a b)"), 128, 2 * NT, [oT.k()])
        numacc.free(); denacc.free()
        for b in ebuf + pbuf + kvst:
            b.free()
        for b in kTh + vh:
            b.free()

        selB = B("selB", [NS, 128], F32)
        selT = B("selT", [NS, NS], F32)
        sbs = B("sbs", [12], F32)
        sb0 = B("sb0", [12], F32)
        P.add("sp", lambda e: e.dma_start(out=selB.ap[0:NS].rearrange("p a b -> p (a b)"), in_=c_selB), writes=[selB.k()], dma=True, chan="misc")
        P.add("sp", lambda e: e.dma_start(out=selT.ap.rearrange("p a b -> p (a b)"), in_=c_selT), writes=[selT.k()], dma=True, chan="misc")
        P.add("sp", lambda e: e.dma_start(out=sbs.ap, in_=sbias), writes=[sbs.k()], dma=True, chan="misc")
        P.add("sp", lambda e: e.dma_start(out=sb0.ap[0:NS], in_=sbias0), writes=[sb0.k()], dma=True, chan="misc")
        sa_num = B("sa_num", [3, 260], F32)
        ctile = [B("ctile%d" % i, [512], F32) for i in range(4)]
        wv = [B("wvaug%d" % i, [260], F32) for i in range(3)]
        prodb = [B("prod%d" % i, [256], F32) for i in range(2)]
        sct = B("sct", [4, 4], F32)
        qs_keys = [qs_s.k(g, i) for g in range(3) for i in range(2)]
        its = [(g, b) for g in range(3) for b in range(NS)]
        sa_st = {}

        def sa_a1(i):
            g, b = its[i]
            dil = GROUPS[g][1]
            ci = i % 4
            ct = ctile[ci]
            P.add("sp", lambda e: e.dma_start(out=ct.ap, in_=cch[g][b, 0::dil, :] if dil > 1 else cch[g][b]),
                  writes=[ct.k()], dma=True, chan=("ctile", ci))
            bb = rot("pbank", 4)
            P.add("pe", lambda e: e.matmul(out=ps(bb, 256), lhsT=selB.ap[0:NS, b, :], rhs=qs_s.ap[0:NS, g, 0:256], start=True, stop=True),
                  reads=[selB.k()] + qs_keys, writes=psk(bb))
            pr_ = prodb[i % 2]
            P.add("dve", lambda e: e.tensor_tensor(out=pr_.ap, in0=ct.ap[:, 0:256], in1=ps(bb, 256), op=ALU.mult),
                  reads=[ct.k()], writes=psk(bb) + [pr_.k()])
            si = i % 4
            P.add("dve", lambda e: e.tensor_reduce(out=sct.ap[:, si, :], in_=pr_.ap.rearrange("p (a b) -> p a b", a=4), axis=AX.X, op=ALU.add),
                  reads=[pr_.k()], writes=[sct.k(si)])
            P.add("dve", lambda e: e.scalar_tensor_tensor(out=sct.ap[:, si, :], in0=sct.ap[:, si, :], scalar=SCALE, in1=sbs.ap[:, 4 * g:4 * g + 4], op0=ALU.mult, op1=ALU.add),
                  reads=[sbs.k()], writes=[sct.k(si)])
            w_ = wv[i % 3]
            P.add("act", lambda e: e.activation(out=w_.ap[:, 256:260], in_=sct.ap[:, si, :], func=AF.Exp),
                  reads=[sct.k(si)], writes=[w_.k("e")])
            sa_st[i] = (ct, w_)

        def sa_a2(i):
            ct, w_ = sa_st[i]
            P.add("dve", lambda e: e.tensor_tensor(out=w_.ap[:, 0:256].rearrange("p (a b) -> p a b", a=4),
                                                   in0=ct.ap[:, 256:512].rearrange("p (a b) -> p a b", a=4),
                                                   in1=w_.ap[:, 256:260].unsqueeze(2).broadcast_to([128, 4, 64]), op=ALU.mult),
                  reads=[ct.k(), w_.k("e")], writes=[w_.k("v")])

        def sa_b(i):
            g, b = its[i]
            ct, w_ = sa_st[i]
            P.add("pe", lambda e: e.matmul(out=psum[0:NS, 512 * (5 + g): 512 * (5 + g) + 260], lhsT=selT.ap[:, b, :], rhs=w_.ap,
                                           start=(b == 0), stop=(b == NS - 1)),
                  reads=[w_.k("e"), w_.k("v"), selT.k()], writes=psk(5 + g))
            if b == NS - 1:
                P.add("act", lambda e: e.activation(out=sa_num.ap[0:NS, g, :], in_=psum[0:NS, 512 * (5 + g): 512 * (5 + g) + 260], func=AF.Copy),
                      reads=[], writes=psk(5 + g) + [sa_num.k(g)])

        for i in range(len(its) + 2):
            if i < len(its):
                sa_a1(i)
            if 0 <= i - 1 < len(its):
                sa_a2(i - 1)
            if 0 <= i - 2 < len(its):
                sa_b(i - 2)
        sself = B("sself", [12 * 64 + 12 + 12], F32)
        sp_ = sself.ap[0:NS, 0:768].rearrange("p (a b) -> p a b", a=12)
        s0 = sself.ap[0:NS, 768:780]
        e0 = sself.ap[0:NS, 780:792]
        qv = qs_s.ap[0:NS]
        P.add("dve", lambda e: e.tensor_tensor(out=sself.ap[0:NS, 0:768].rearrange("p (g c) -> p g c", g=3), in0=qv[:, :, 0:256], in1=qv[:, :, 256:512], op=ALU.mult),
              reads=qs_keys, writes=[sself.k("p")])
        P.add("dve", lambda e: e.tensor_reduce(out=s0, in_=sp_, axis=AX.X, op=ALU.add), reads=[sself.k("p")], writes=[sself.k("s")])
        P.add("dve", lambda e: e.scalar_tensor_tensor(out=s0, in0=s0, scalar=SCALE, in1=sb0.ap[0:NS, :], op0=ALU.mult, op1=ALU.add),
              reads=[sb0.k()], writes=[sself.k("s")])
        P.add("act", lambda e: e.activation(out=e0, in_=s0, func=AF.Exp), reads=[sself.k("s")], writes=[sself.k("e")])
        P.add("dve", lambda e: e.tensor_tensor(out=sself.ap[0:NS, 0:768].rearrange("p (g s d) -> p g s d", g=3, s=4),
                                               in0=qv[:, :, 512:768].rearrange("p g (s d) -> p g s d", s=4),
                                               in1=sself.ap[0:NS, 780:792].rearrange("p (g s) -> p g s", g=3).unsqueeze(3).broadcast_to([NS, 3, 4, 64]), op=ALU.mult),
              reads=qs_keys + [sself.k("e")], writes=[sself.k("p")])
        san = sa_num.ap[0:NS]
        P.add("dve", lambda e: e.tensor_tensor(out=san[:, :, 0:256], in0=san[:, :, 0:256], in1=sself.ap[0:NS, 0:768].rearrange("p (g c) -> p g c", g=3), op=ALU.add),
              reads=[sself.k("p")], writes=[sa_num.k(0), sa_num.k(1), sa_num.k(2)])
        P.add("dve", lambda e: e.tensor_tensor(out=san[:, :, 256:260], in0=san[:, :, 256:260], in1=sself.ap[0:NS, 780:792].rearrange("p (g s) -> p g s", g=3), op=ALU.add),
              reads=[sself.k("e")], writes=[sa_num.k(0), sa_num.k(1), sa_num.k(2)])
        P.add("dve", lambda e: e.tensor_tensor(out=san[:, 0, :], in0=san[:, 0, :], in1=san[:, 1, :], op=ALU.add), reads=[], writes=[sa_num.k(0), sa_num.k(1), sa_num.k(2)])
        P.add("dve", lambda e: e.tensor_tensor(out=san[:, 0, :], in0=san[:, 0, :], in1=san[:, 2, :], op=ALU.add), reads=[], writes=[sa_num.k(0), sa_num.k(1), sa_num.k(2)])
        P.add("dve", lambda e: e.reciprocal(out=san[:, 0, 256:260], in_=san[:, 0, 256:260]), reads=[], writes=[sa_num.k(0), sa_num.k(1), sa_num.k(2)])
        P.add("dve", lambda e: e.tensor_tensor(out=san[:, 1, 0:256].rearrange("p (s d) -> p s d", s=4), in0=san[:, 0, 0:256].rearrange("p (s d) -> p s d", s=4),
                                               in1=san[:, 0, 256:260].unsqueeze(2).broadcast_to([NS, 4, 64]), op=ALU.mult),
              reads=[], writes=[sa_num.k(0), sa_num.k(1), sa_num.k(2)])
        osT = B("osT", [2, NS], BF16)

        def tr_os(e):
            r = None
            for kc in range(2):
                r = e.transpose(out=ps(0, NS, NS * kc), in_=sa_num.ap[0:NS, 1, kc * 128:(kc + 1) * 128], identity=identf.ap[0:NS, 0:NS])
            return r
        P.add("pe", tr_os, reads=[sa_num.k(1), identf.k()], writes=psk(0))
        P.add("act", lambda e: e.activation(out=osT.ap.rearrange("p a b -> p (a b)"), in_=ps(0, 2 * NS), func=AF.Copy), reads=[], writes=psk(0) + [osT.k()])
        dbg("osT", osT.ap.rearrange("p a b -> p (a b)"), 128, 2 * NS, [osT.k()])
        for b in [selB, selT, sbs, sb0, sa_num, sct, sself, qs_s] + ctile + wv + prodb:
            b.free()

        w_ao = load_w_own("w_ao", w_attn_out, 2, D, "wsmall")
        w_pp = load_w_own("w_pp", w_ple_proj, 2, D, "wsmall")
        hbb.pop().free()
        xin.extend([B("xin%d" % i, [D], F32) for i in range(2)])
        add_wslot(); add_wslot()
        zT = B("zT", [8, 30 + 512], BF16)
        zsT = B("zsT", [8, NS], F32)
        histT = B("histT", [8, NS * 30], F32)
        zlast = B("zlast", [8, 30], F32)

        for q4 in range(4):
            i = rot("xin", len(xin))
            xb = xin[i]
            P.add("sp", lambda e, xb=xb, q4=q4: e.dma_start(out=xb.ap[0:120, :], in_=sconv[4 * q4:4 * q4 + 4].rearrange("b i c -> (b i) c")),
                  writes=[xb.k()], dma=True, chan=("xin", i))
            for half in range(2):
                bank = rot("pbank", 4)

                def ft(e, xb=xb, bank=bank, half=half):
                    r = None
                    for k4 in range(4):
                        kc = 4 * half + k4
                        r = e.transpose(out=ps(bank, 120, 120 * k4), in_=xb.ap[0:120, kc * 128:(kc + 1) * 128], identity=identf.ap[0:120, 0:120])
                    return r
                P.add("pe", ft, reads=[xb.k(), identf.k()], writes=psk(bank))
                evac_copy(histT.ap[:, 4 * half:4 * half + 4, 120 * q4:120 * q4 + 120], ps(bank, 480).rearrange("p (a b) -> p a b", a=4),
                          [], psk(bank) + [histT.k(q4, half)])
        histT_all = [histT.k(q4, h) for q4 in range(4) for h in range(2)]
        while xin:
            xin.pop().free()

        def ln_silu(ysrc, ykeys, ncols, dstT, dkeys, tag):
            ybf = B("ybf", [8, ncols], BF16)
            ysq = B("ysq", [8, ncols], BF16)
            stt = B("stt", [3, ncols], F32)
            P.add("act", lambda e: e.activation(out=ybf.ap, in_=ysrc, func=AF.Copy), reads=ykeys, writes=[ybf.k()])
            P.add("act", lambda e: e.activation(out=ysq.ap, in_=ysrc, func=AF.Square), reads=ykeys, writes=[ysq.k()])

            def fst(e):
                r = None
                for kc in range(8):
                    e.matmul(out=ps(2, ncols), lhsT=ones_bf, rhs=ybf.ap[:, kc, :], start=(kc == 0), stop=(kc == 7))
                for kc in range(8):
                    r = e.matmul(out=ps(3, ncols), lhsT=ones_bf, rhs=ysq.ap[:, kc, :], start=(kc == 0), stop=(kc == 7))
                return r
            P.add("pe", fst, reads=[ybf.k(), ysq.k(), cst.k("o")], writes=psk(2, 3))
            mean = stt.ap[:, 0, :]; var = stt.ap[:, 1, :]; rstd = stt.ap[:, 2, :]
            P.add("act", lambda e: e.activation(out=mean, in_=ps(2, ncols), func=AF.Copy, scale=1.0 / D), reads=[], writes=psk(2) + [stt.k(0)])
            P.add("dve", lambda e: e.tensor_tensor(out=var, in0=mean, in1=mean, op=ALU.mult), reads=[stt.k(0)], writes=[stt.k(1)])
            P.add("dve", lambda e: e.scalar_tensor_tensor(out=var, in0=ps(3, ncols), scalar=1.0 / D, in1=var, op0=ALU.mult, op1=ALU.subtract),
                  reads=[], writes=psk(3) + [stt.k(1)])
            P.add("act", lambda e: e.activation(out=var, in_=var, func=AF.Sqrt, bias=epsb.ap[:, 0:1], scale=1.0), reads=[epsb.k()], writes=[stt.k(1)])
            P.add("dve", lambda e: e.reciprocal(out=rstd, in_=var), reads=[stt.k(1)], writes=[stt.k(2)])
            for kc in range(8):
                P.add("dve", lambda e, kc=kc: e.tensor_tensor(out=ysrc[:, kc, :], in0=ysrc[:, kc, :], in1=mean, op=ALU.subtract), reads=[stt.k(0)], writes=ykeys)
                P.add("dve", lambda e, kc=kc: e.tensor_tensor(out=ysrc[:, kc, :], in0=ysrc[:, kc, :], in1=rstd, op=ALU.mult), reads=[stt.k(2)], writes=ykeys)
                P.add("act", lambda e, kc=kc: e.activation(out=dstT[:, kc, :], in_=ysrc[:, kc, :], func=AF.Silu, bias=vcol.ap[:, 2, kc:kc + 1], scale=vcol.ap[:, 1, kc:kc + 1]),
                      reads=ykeys + [vcol.k()], writes=dkeys)
            ybf.free(); ysq.free(); stt.free()

        def make_4a(tt):
            T0_ = tt * 512
            hkeys_ = [hT.k(4 * tt + i) for i in range(4)]
            st4 = {}

            def hsrc_(kc):
                return hT.ap[:, kc, T0_:T0_ + 512]

            def step(q4, k4):
                def run():
                    if q4 == 0 and k4 == 0:
                        st4["sgw"] = [B("sgw%d" % i, [512], F32) for i in range(2)]
                    if k4 == 0:
                        st4["w"] = load_w_pair(w_in[:, 2304 + 512 * q4:2304 + 512 * q4 + 512], w_in[:, 3328 + 512 * q4:3328 + 512 * q4 + 512], 512, tile=tt, li=q4)
                    wa, wg = st4["w"]
                    sgw_ = st4["sgw"]
                    kc = 4 * q4 + k4
                    segs = [(512, hsrc_, hkeys_, "main")]
                    if tt == 0:
                        segs.append((30, (lambda k: hTh30.ap[:, k, 0:30]), [hTh30.k()], "halo"))
                        segs.append((NS, (lambda k: hsT.ap[:, k, :]), [hsT.k()], "samp"))
                    for (n, sfn, skeys, kind) in segs:
                        ba = rot("pbank", 4)
                        bg = rot("pbank", 4)

                        def fu(e, ba=ba, bg=bg, n=n, sfn=sfn):
                            r = None
                            for k in range(8):
                                e.matmul(out=ps(ba, n), lhsT=wa.ap[:, k, 128 * k4:128 * k4 + 128], rhs=sfn(k), start=(k == 0), stop=(k == 7))
                            for k in range(8):
                                r = e.matmul(out=ps(bg, n), lhsT=wg.ap[:, k, 128 * k4:128 * k4 + 128], rhs=sfn(k), start=(k == 0), stop=(k == 7))
                            return r
                        P.add("pe", fu, reads=[wa.k(), wg.k()] + skeys, writes=psk(ba, bg))
                        sg = sgw_[rot("sgw", 2)]
                        P.add("act", lambda e, sg=sg, bg=bg, n=n: e.activation(out=sg.ap[:, 0:n], in_=ps(bg, n), func=AF.Sigmoid), reads=[], writes=psk(bg) + [sg.k()])
                        if kind == "main":
                            P.add("dve", lambda e, sg=sg, ba=ba: e.tensor_tensor(out=zT.ap[:, kc, 30:542], in0=ps(ba, 512), in1=sg.ap, op=ALU.mult),
                                  reads=[sg.k()], writes=psk(ba) + [zT.k(kc, "m")])
                            if tt == 3:
                                P.add("dve", lambda e, sg=sg, ba=ba: e.tensor_tensor(out=zlast.ap[:, kc, :], in0=ps(ba, 30, 482), in1=sg.ap[:, 482:512], op=ALU.mult),
                                      reads=[sg.k()], writes=psk(ba) + [zlast.k(kc)])
                        elif kind == "halo":
                            P.add("dve", lambda e, sg=sg, ba=ba: e.tensor_tensor(out=zT.ap[:, kc, 0:30], in0=ps(ba, 30), in1=sg.ap[:, 0:30], op=ALU.mult),
                                  reads=[sg.k()], writes=psk(ba) + [zT.k(kc, "h")])
                        else:
                            P.add("dve", lambda e, sg=sg, ba=ba: e.tensor_tensor(out=zsT.ap[:, kc, :], in0=ps(ba, NS), in1=sg.ap[:, 0:NS], op=ALU.mult),
                                  reads=[sg.k()], writes=psk(ba) + [zsT.k(kc)])
                    if q4 == 1 and k4 == 3:
                        for b_ in sgw_:
                            b_.free()
                        if tt == 3:
                            zrow2 = B("zrow2", [D], F32)

                            def tzl(e):
                                r = None
                                for kc_ in range(8):
                                    r = e.transpose(out=psum[0:30, 128 * kc_:128 * kc_ + 128], in_=zlast.ap[:, kc_, :], identity=identf.ap)
                                return r
                            P.add("pe", tzl, reads=[zlast.k(kc_) for kc_ in range(8)] + [identf.k()], writes=psk(0, 1))
                            P.add("act", lambda e: e.activation(out=zrow2.ap[0:30, :], in_=psum[0:30, 0:1024], func=AF.Copy), reads=[], writes=psk(0, 1) + [zrow2.k()])
                            P.add("sp", lambda e: e.dma_start(out=convo, in_=zrow2.ap[0:30, :]), reads=[zrow2.k()], writes=[("dram", "convo")], dma=True, chan="misc")
                            zrow2.free()
                return run
            return [step(q4, k4) for q4 in range(2) for k4 in range(4)]

        for t in range(4):
            wsamp = (t == 0)
            T0 = t * 512
            wl["tile"] = t
            wl["li"] = 2
            def hsrc(kc, T0=T0):
                return hT.ap[:, kc, T0:T0 + 512]
            hkeys = [hT.k(4 * t + i) for i in range(4)]

            if t == 0:
                for st_ in make_4a(0):
                    st_()

            if wsamp:
                ysb = B("ysb", [8, NS], F32)
                hp = B("hp", [8, NS * 30], F32)
                sTs = B("sTs", [8, NS], BF16)
                P.add("dve", lambda e: e.tensor_tensor(out=hp.ap.rearrange("p k (b i) -> p k b i", i=30), in0=histT.ap.rearrange("p k (b i) -> p k b i", i=30),
                                                       in1=wT.ap[:, :, 0:30].unsqueeze(2).broadcast_to([128, 8, NS, 30]), op=ALU.mult),
                      reads=histT_all + [wT.k()], writes=[hp.k()])
                P.add("dve", lambda e: e.tensor_reduce(out=ysb.ap, in_=hp.ap.rearrange("p k (b i) -> p k b i", i=30), axis=AX.X, op=ALU.add),
                      reads=[hp.k()], writes=[ysb.k()])
                P.add("dve", lambda e: e.tensor_tensor(out=hp.ap[:, :, 0:NS], in0=zsT.ap, in1=wT.ap[:, :, 30:31].broadcast_to([128, 8, NS]), op=ALU.mult),
                      reads=[zsT.k(kc) for kc in range(8)] + [wT.k()], writes=[hp.k()])
                P.add("dve", lambda e: e.tensor_tensor(out=ysb.ap, in0=ysb.ap, in1=hp.ap[:, :, 0:NS], op=ALU.add), reads=[hp.k()], writes=[ysb.k()])
                P.add("dve", lambda e: e.tensor_tensor(out=ysb.ap, in0=ysb.ap, in1=vcol.ap[:, 0, :].unsqueeze(2).broadcast_to([128, 8, NS]), op=ALU.add),
                      reads=[vcol.k()], writes=[ysb.k()])
                ln_silu(ysb.ap, [ysb.k()], NS, sTs.ap, [sTs.k()], "s")
                dbg("sTs", sTs.ap.rearrange("p a b -> p (a b)"), 128, 8 * NS, [sTs.k()])
                ysb.free(); hp.free(); histT.free()
                zrow = B("zrow", [D], F32)

                def tz(e):
                    r = None
                    for kc in range(8):
                        r = e.transpose(out=psum[0:NS, 128 * kc:128 * kc + 128], in_=zsT.ap[:, kc, :], identity=identf.ap)
                    return r
                P.add("pe", tz, reads=[zsT.k(kc) for kc in range(8)] + [identf.k()], writes=psk(0, 1))
                P.add("act", lambda e: e.activation(out=zrow.ap[0:NS, :], in_=psum[0:NS, 0:1024], func=AF.Copy), reads=[], writes=psk(0, 1) + [zrow.k()])
                P.add("sp", lambda e: e.dma_start(out=convs[:, 29, :], in_=zrow.ap[0:NS, :]), reads=[zrow.k()], writes=[("dram", "convs_new")], dma=True, chan="misc")
                zrow.free()

            ybuf = B("ybuf", [8, 512], F32)
            sT = B("sT", [8, 512], BF16)
            diag = [B("diag%d" % i, [31, 128], BF16) for i in range(2)]
            ybfr = [B("ybf%d" % i, [512], BF16) for i in range(3)]
            ysqr = [B("ysq%d" % i, [512], BF16) for i in range(3)]
            zTs = B("zTs", [8, 542], BF16)
            for kc in range(8):
                P.add("dve", lambda e, kc=kc: e.tensor_copy(out=zTs.ap[:, kc, 0:541], in_=zT.ap[:, kc, 1:542]),
                      reads=[zT.k(kc, "m"), zT.k(kc, "h")], writes=[zTs.k(kc)])

            def conv_a(kc):
                dg = diag[kc % 2]
                for i in range(31):
                    P.add("dve", lambda e, i=i: e.tensor_scalar(out=dg.ap[:, i, :], in0=ident, scalar1=wT.ap[:, kc, i:i + 1], scalar2=None, op0=ALU.mult),
                          reads=[cst.k("i"), wT.k()], writes=[dg.k(i)])
                bank = rot("cbank", 2)

                def fc(e):
                    r = None
                    for i in range(31):
                        src = zT.ap[:, kc, i:i + 512] if i % 2 == 0 else zTs.ap[:, kc, i - 1:i - 1 + 512]
                        r = e.matmul(out=ps(bank), lhsT=dg.ap[:, i, :], rhs=src, start=(i == 0), stop=(i == 30))
                    return r
                P.add("pe", fc, reads=[dg.k(i) for i in range(31)] + [zT.k(kc, "m"), zT.k(kc, "h"), zTs.k(kc)], writes=psk(bank))
                P.add("act", lambda e: e.activation(out=ybuf.ap[:, kc, :], in_=ps(bank), func=AF.Identity, bias=vcol.ap[:, 0, kc:kc + 1], scale=1.0),
                      reads=[vcol.k()], writes=psk(bank) + [ybuf.k(kc)])
                ybf = ybfr[kc % 3]; ysq = ysqr[kc % 3]
                P.add("act", lambda e: e.activation(out=ybf.ap, in_=ybuf.ap[:, kc, :], func=AF.Copy), reads=[ybuf.k(kc)], writes=[ybf.k()])
                P.add("act", lambda e: e.activation(out=ysq.ap, in_=ybuf.ap[:, kc, :], func=AF.Square), reads=[ybuf.k(kc)], writes=[ysq.k()])
                if t < 3:
                    P.add("dve", lambda e: e.tensor_copy(out=zT.ap[:, kc, 0:30], in_=zT.ap[:, kc, 512:542]),
                          reads=[zT.k(kc, "m")], writes=[zT.k(kc, "h")])

            def conv_b(kc):
                ybf = ybfr[kc % 3]; ysq = ysqr[kc % 3]

                def fst(e):
                    e.matmul(out=ps(2), lhsT=ones_bf, rhs=ybf.ap, start=(kc == 0), stop=(kc == 7))
                    return e.matmul(out=ps(3), lhsT=ones_bf, rhs=ysq.ap, start=(kc == 0), stop=(kc == 7))
                P.add("pe", fst, reads=[ybf.k(), ysq.k(), cst.k("o")], writes=psk(2, 3))

            for kc in range(9):
                if kc < 8:
                    conv_a(kc)
                    if kc in (2, 5):
                        bulk_step(2)
                if kc > 0:
                    conv_b(kc - 1)
            for dg in diag:
                dg.free()
            zTs.free()
            ybk = [ybuf.k(kc) for kc in range(8)]
            stt = B("stt", [3, 512], F32)
            mean = stt.ap[:, 0, :]; var = stt.ap[:, 1, :]; rstd = stt.ap[:, 2, :]
            P.add("act", lambda e: e.activation(out=mean, in_=ps(2), func=AF.Copy, scale=1.0 / D), reads=[], writes=psk(2) + [stt.k(0)])
            P.add("dve", lambda e: e.tensor_tensor(out=var, in0=mean, in1=mean, op=ALU.mult), reads=[stt.k(0)], writes=[stt.k(1)])
            P.add("dve", lambda e: e.scalar_tensor_tensor(out=var, in0=ps(3), scalar=1.0 / D, in1=var, op0=ALU.mult, op1=ALU.subtract),
                  reads=[], writes=psk(3) + [stt.k(1)])
            P.add("act", lambda e: e.activation(out=var, in_=var, func=AF.Sqrt, bias=epsb.ap[:, 0:1], scale=1.0), reads=[epsb.k()], writes=[stt.k(1)])
            P.add("dve", lambda e: e.reciprocal(out=rstd, in_=var), reads=[stt.k(1)], writes=[stt.k(2)])
            for kc in range(8):
                P.add("dve", lambda e, kc=kc: e.tensor_tensor(out=ybuf.ap[:, kc, :], in0=ybuf.ap[:, kc, :], in1=mean, op=ALU.subtract), reads=[stt.k(0)], writes=[ybuf.k(kc)])
                P.add("dve", lambda e, kc=kc: e.tensor_tensor(out=ybuf.ap[:, kc, :], in0=ybuf.ap[:, kc, :], in1=rstd, op=ALU.mult), reads=[stt.k(2)], writes=[ybuf.k(kc)])
                P.add("act", lambda e, kc=kc: e.activation(out=sT.ap[:, kc, :], in_=ybuf.ap[:, kc, :], func=AF.Silu, bias=vcol.ap[:, 2, kc:kc + 1], scale=vcol.ap[:, 1, kc:kc + 1]),
                      reads=[ybuf.k(kc), vcol.k()], writes=[sT.k(kc)])
            sT_all = [sT.k(kc) for kc in range(8)]
            if t == 0:
                dbg("sT", sT.ap.rearrange("p a b -> p (a b)"), 128, 8 * 512, sT_all)
            ybuf.free(); stt.free()
            for b_ in ybfr + ysqr:
                b_.free()

            tiles = [(512, "p")] + ([(NS, "s")] if wsamp else [])
            blocks = [(128, i, "p") for i in range(4)] + ([(NS, 0, "s")] if wsamp else [])
            nb = len(blocks)
            x1 = B("x1", [nb, D], F32)
            for bi_, (M, i, kind) in enumerate(blocks):
                xsrc = xo[T0 + 128 * i:T0 + 128 * i + 128, :] if kind == "p" else xs
                P.add("sp", lambda e, bi_=bi_, M=M, xsrc=xsrc: e.dma_start(out=x1.ap[0:M, bi_, :], in_=xsrc), writes=[x1.k(bi_)], dma=True, chan=("xin", bi_ % 2))
            g_post = load_gain(1, "g_post"); g_fpre = load_gain(2, "g_fpre")
            mixT = B("mixT", [8, 512], BF16)
            mixTs = B("mixTs", [8, NS], BF16) if wsamp else None
            w_co = load_w("w_co", w_conv_out, 8, D, ("w", rot("wslot", 3)))
            sab = [B("sab%d" % i, [512], F32) for i in range(2)]
            for q2 in range(2):
                wga, wgb = load_w_pair(w_in[:, 4352 + 512 * q2:4352 + 512 * q2 + 512], w_in[:, 5376 + 512 * q2:5376 + 512 * q2 + 512], 512)
                bulk_step(1)
                for j4 in range(4):
                    j = 4 * q2 + j4
                    for (n, kind) in tiles:
                        if kind == "p":
                            o_src = lambda k: oT.ap[:, k, T0:T0 + 512]
                            s_src = lambda k: sT.ap[:, k, :]
                            h_src = hsrc
                            rk1 = [oT.k()] + hkeys
                            rk2 = sT_all
                            dst = mixT.ap[:, j, :]
                            dk = [mixT.k(j)]
                        else:
                            o_src = lambda k: osT.ap[:, k, :]
                            s_src = lambda k: sTs.ap[:, k, :]
                            h_src = lambda k: hsT.ap[:, k, :]
                            rk1 = [osT.k(), hsT.k()]
                            rk2 = [sTs.k()]
                            dst = mixTs.ap[:, j, :]
                            dk = [mixTs.k(j)]

                        def fm1(e, n=n, j=j, j4=j4, o_src=o_src, h_src=h_src, wga=wga, wgb=wgb):
                            r = None
                            for k in range(2):
                                e.matmul(out=ps(4, n), lhsT=w_ao.ap[:, k, 128 * j:128 * j + 128], rhs=o_src(k), start=(k == 0), stop=(k == 1))
                            for k in range(8):
                                e.matmul(out=ps(6, n), lhsT=wga.ap[:, k, 128 * j4:128 * j4 + 128], rhs=h_src(k), start=(k == 0), stop=(k == 7))
                            for k in range(8):
                                r = e.matmul(out=ps(7, n), lhsT=wgb.ap[:, k, 128 * j4:128 * j4 + 128], rhs=h_src(k), start=(k == 0), stop=(k == 7))
                            return r

                        def fm2(e, n=n, j=j, s_src=s_src):
                            r = None
                            for k in range(8):
                                r = e.matmul(out=ps(5, n), lhsT=w_co.ap[:, k, 128 * j:128 * j + 128], rhs=s_src(k), start=(k == 0), stop=(k == 7))
                            return r
                        P.add("pe", fm1, reads=[w_ao.k(), wga.k(), wgb.k()] + rk1, writes=psk(4, 6, 7))
                        P.add("pe", fm2, reads=[w_co.k()] + rk2, writes=psk(5))
                        sa, sb2 = sab[0], sab[1]
                        P.add("act", lambda e, n=n, sa=sa: e.activation(out=sa.ap[:, 0:n], in_=ps(6, n), func=AF.Sigmoid), reads=[], writes=psk(6) + [sa.k()])
                        P.add("act", lambda e, n=n, sb2=sb2: e.activation(out=sb2.ap[:, 0:n], in_=ps(7, n), func=AF.Sigmoid), reads=[], writes=psk(7) + [sb2.k()])
                        P.add("dve", lambda e, n=n, sa=sa: e.tensor_tensor(out=sa.ap[:, 0:n], in0=ps(4, n), in1=sa.ap[:, 0:n], op=ALU.mult), reads=[], writes=psk(4) + [sa.k()])
                        P.add("dve", lambda e, n=n, sb2=sb2: e.tensor_tensor(out=sb2.ap[:, 0:n], in0=ps(5, n), in1=sb2.ap[:, 0:n], op=ALU.mult), reads=[], writes=psk(5) + [sb2.k()])
                        P.add("dve", lambda e, n=n, sa=sa, sb2=sb2, dst=dst: e.tensor_tensor(out=dst, in0=sa.ap[:, 0:n], in1=sb2.ap[:, 0:n], op=ALU.add),
                              reads=[sb2.k()], writes=[sa.k()] + dk)
                wga.free(); wgb.free()
            if t == 0:
                dbg("mixT", mixT.ap.rearrange("p a b -> p (a b)"), 128, 8 * 512, [mixT.k(j) for j in range(8)])
                dbg("mixTs", mixTs.ap.rearrange("p a b -> p (a b)"), 128, 8 * NS, [mixTs.k(j) for j in range(8)])
            w_co.free()
            for b in sab:
                b.free()
            sT.free()
            if wsamp:
                sTs.free()

            h2T = B("h2T", [8, 512], BF16)
            h2Ts = B("h2Ts", [8, NS], BF16) if wsamp else None
            tmp = [B("tmp%d" % i, [D], F32) for i in range(2)]
            w_o = load_w("w_o", w_out, 8, D, ("w", rot("wslot", 3)))

            def tm_proj(wb, nk, lhs_fn, rkeys, M, banks):
                def f(e):
                    r = None
                    for hf in range(2):
                        for k in range(nk):
                            r = e.matmul(out=psum[0:M, 512 * banks[hf]:512 * banks[hf] + 512], lhsT=lhs_fn(k), rhs=wb.ap[:, k, 512 * hf:512 * hf + 512],
                                         start=(k == 0), stop=(k == nk - 1))
                    return r
                P.add("pe", f, reads=[wb.k()] + rkeys, writes=psk(*banks))

            def post_norm_residual(M, banks, gain, res_ap, res_keys, out_ap, out_keys):
                src = psum[0:M, 512 * banks[0]:512 * banks[0] + 1024]
                i = rot("small", 4)
                ms = small.ap[0:M, 0, i:i + 1]; sd = small.ap[0:M, 1, i:i + 1]; rs = small.ap[0:M, 2, i:i + 1]
                P.add("act", lambda e: e.activation(out=junk.ap[0:M, :], in_=src, func=AF.Square, scale=1.0 / 32.0, accum_out=ms), reads=[], writes=psk(*banks) + [small.k(0, i)])
                P.add("act", lambda e: e.activation(out=sd, in_=ms, func=AF.Sqrt, bias=epsb.ap[0:M, 0:1], scale=1.0), reads=[small.k(0, i), epsb.k()], writes=[small.k(1, i)])
                P.add("dve", lambda e: e.reciprocal(out=rs, in_=sd), reads=[small.k(1, i)], writes=[small.k(2, i)])
                ti = rot("tmp", 2)
                tb = tmp[ti]
                P.add("dve", lambda e: e.scalar_tensor_tensor(out=tb.ap[0:M, :], in0=src, scalar=rs, in1=gain.ap[0:M, :], op0=ALU.mult, op1=ALU.mult),
                      reads=[small.k(2, i), gain.k()], writes=psk(*banks) + [tb.k()])
                P.add("dve", lambda e: e.tensor_tensor(out=out_ap, in0=tb.ap[0:M, :], in1=res_ap, op=ALU.add), reads=[tb.k()] + res_keys, writes=out_keys)

            d_hb = {}

            def d_a(bi_):
                M, i, kind = blocks[bi_]
                banks = ((0, 1), (2, 3), (6, 7))[bi_ % 3]
                if kind == "p":
                    lhs = lambda k: mixT.ap[:, k, 128 * i:128 * i + 128]
                    rk = [mixT.k(j) for j in range(8)]
                    xsrc = xo[T0 + 128 * i:T0 + 128 * i + 128, :]
                else:
                    lhs = lambda k: mixTs.ap[:, k, :]
                    rk = [mixTs.k(j) for j in range(8)]
                    xsrc = xs
                tm_proj(w_o, 8, lhs, rk, M, banks)
                post_norm_residual(M, banks, g_post, x1.ap[0:M, bi_, :], [x1.k(bi_)], x1.ap[0:M, bi_, :], [x1.k(bi_)])
                hb = hb_all4[bi_]
                d_hb[bi_] = hb
                rms_to_bf16(x1.ap[0:M, bi_, :], [x1.k(bi_)], M, g_fpre, hb.ap[0:M, :], hb.k())

            def d_b(bi_):
                M, i, kind = blocks[bi_]
                hb = d_hb[bi_]
                tb_ = 4 + rot("tbank", 2)
                if kind == "p":
                    transpose_rows(hb.ap[0:M, :], hb.k(), M, 8, (lambda: h2T.ap[:, :, 128 * i:128 * i + 128]), [h2T.k(i)], tb_, evac=("act" if tb_ == 4 else "dve"))
                else:
                    transpose_rows(hb.ap[0:M, :], hb.k(), M, 8, (lambda: h2Ts.ap), [h2Ts.k()], tb_, evac="act")

            hbx4 = [B("hbx%d" % i, [D], BF16) for i in range(max(0, nb - len(hbb)))]
            hb_all4 = hbb + hbx4
            for bi_ in range(nb):
                d_a(bi_)
            for bi_ in range(nb):
                d_b(bi_)
            for b_ in hbx4:
                b_.free()
            if t == 0:
                dbg("x1", x1.ap.rearrange("p a b -> p (a b)"), 128, nb * D, [x1.k(i_) for i_ in range(nb)])
                dbg("h2T", h2T.ap.rearrange("p a b -> p (a b)"), 128, 8 * 512, [h2T.k(i_) for i_ in range(4)])
            w_o.free(); g_post.free(); g_fpre.free()
            for b_ in tmp:
                b_.free()
            mixT.free()
            if wsamp:
                mixTs.free()

            g_fpost = load_gain(3, "g_fpost"); g_ple = load_gain(4, "g_ple")
            aTp = [B("aT%d" % i_, [8 if i_ < 2 else 6, 512], BF16) for i_ in range(3)]
            sgw = [B("sgw%d" % i, [512], F32) for i in range(2)]
            aTs = B("aTs", [NFF, NS], BF16) if wsamp else None
            h2keys = [h2T.k(i) for i in range(4)]
            for s6 in range(6):
                nch = 4 if s6 < 5 else 2
                sl_ = wslot()
                wf = WView(sl_, sl_.ap.rearrange("p (a b c) -> p a b c", a=8, b=2))

                def ldf(e, wf=wf, s6=s6, nch=nch):
                    a = e.dma_start(out=wf.ap[:, :, 0, 0:128 * nch], in_=w_ffn_in[:, 512 * s6:512 * s6 + 128 * nch].rearrange("(kc p) c -> p kc c", p=128))
                    b = e.dma_start(out=wf.ap[:, :, 1, 0:128 * nch], in_=w_ffn_in[:, DFF + 512 * s6:DFF + 512 * s6 + 128 * nch].rearrange("(kc p) c -> p kc c", p=128))
                    return [a, b]
                ring_load(sl_, ldf, 2)
                bulk_step(2 if s6 < 5 else 1)
                for j4 in range(nch):
                    j = 4 * s6 + j4
                    for (n, kind) in tiles:
                        if kind == "p":
                            src = lambda k: h2T.ap[:, k, :]
                            rk = h2keys
                            dst = aTp[j // 8].ap[:, j % 8, :]
                            dk = [aTp[j // 8].k(j % 8)]
                        else:
                            src = lambda k: h2Ts.ap[:, k, :]
                            rk = [h2Ts.k()]
                            dst = aTs.ap[:, j, :]
                            dk = [aTs.k(j)]
                        bg = rot("pbank", 4)
                        bu = rot("pbank", 4)

                        def ff(e, bg=bg, bu=bu, n=n, src=src, wf=wf, j4=j4):
                            r = None
                            for k in range(8):
                                e.matmul(out=ps(bg, n), lhsT=wf.ap[:, k, 0, 128 * j4:128 * j4 + 128], rhs=src(k), start=(k == 0), stop=(k == 7))
                            for k in range(8):
                                r = e.matmul(out=ps(bu, n), lhsT=wf.ap[:, k, 1, 128 * j4:128 * j4 + 128], rhs=src(k), start=(k == 0), stop=(k == 7))
                            return r
                        P.add("pe", ff, reads=[wf.k()] + rk, writes=psk(bg, bu))
                        si = rot("sgw", 2)
                        sg = sgw[si]
                        P.add("act", lambda e, sg=sg, bg=bg, n=n: e.activation(out=sg.ap[:, 0:n], in_=ps(bg, n), func=AF.Silu), reads=[], writes=psk(bg) + [sg.k()])
                        P.add("dve", lambda e, sg=sg, bu=bu, n=n, dst=dst: e.tensor_tensor(out=dst, in0=ps(bu, n), in1=sg.ap[:, 0:n], op=ALU.mult),
                              reads=[sg.k()], writes=psk(bu) + dk)
                wf.free()
            h2T.free()
            for b_ in sgw:
                b_.free()
            if wsamp:
                h2Ts.free()

            wfos = []
            for s3 in range(3):
                k0 = 8 * s3
                nk = 8 if s3 < 2 else 6
                wb_ = load_w("w_fo", w_ffn_out[128 * k0:128 * (k0 + nk), :], nk, D)
                wfos.append(wb_)
            wfo_keys = [w_.k() for w_ in wfos]
            tmp = [B("tmp%d" % i, [D], F32) for i in range(2)]
            pin = [B("pin%d" % i, [256], F32) for i in range(nb)]
            for bi_, (M, i, kind) in enumerate(blocks):
                psrc = po[T0 + 128 * i:T0 + 128 * i + 128, :] if kind == "p" else psm
                P.add("sp", lambda e, bi_=bi_, M=M, psrc=psrc: e.dma_start(out=pin[bi_].ap[0:M, :], in_=psrc), writes=[pin[bi_].k()], dma=True, chan=("pin", bi_ % 2))
            bulk_step(1)
            for bi_, (M, i, kind) in enumerate(blocks):
                banks = (0, 1) if bi_ % 2 == 0 else (2, 3)
                if kind == "p":
                    lhs = lambda k, i=i: aTp[k // 8].ap[:, k % 8, 128 * i:128 * i + 128]
                    rk = [aTp[j // 8].k(j % 8) for j in range(NFF)]
                else:
                    lhs = lambda k: aTs.ap[:, k, :]
                    rk = [aTs.k(j) for j in range(NFF)]

                for s3 in range(3):
                    def f(e, M=M, banks=banks, lhs=lhs, s3=s3):
                        r = None
                        for hf in range(2):
                            for k in range(8 * s3, min(NFF, 8 * s3 + 8)):
                                r = e.matmul(out=psum[0:M, 512 * banks[hf]:512 * banks[hf] + 512], lhsT=lhs(k), rhs=wfos[s3].ap[:, k % 8, 512 * hf:512 * hf + 512],
                                             start=(k == 0), stop=(k == NFF - 1))
                        return r
                    P.add("pe", f, reads=[wfo_keys[s3]] + rk, writes=psk(*banks))
                post_norm_residual(M, banks, g_fpost, x1.ap[0:M, bi_, :], [x1.k(bi_)], x1.ap[0:M, bi_, :], [x1.k(bi_)])
            if t == 0:
                dbg("x2", x1.ap.rearrange("p a b -> p (a b)"), 128, nb * D, [x1.k(i_) for i_ in range(nb)])
            for b_ in wfos + aTp + [g_fpost]:
                b_.free()
            if wsamp:
                aTs.free()
            w_pg = load_w("w_pg", w_ple_gate, 8, D, ("w", rot("wslot", 3)))
            h3b = [B("h3b%d" % i, [8, 128], BF16) for i in range(2)]
            pTb = [B("pTb%d" % i, [2, 128], BF16) for i in range(2)]
            pbf = [B("pbf%d" % i, [256], BF16) for i in range(nb)]
            sgg = B("sgg", [D], F32)
            e_hb = {}

            def e_a(bi_):
                M, i, kind = blocks[bi_]
                psrc = po[T0 + 128 * i:T0 + 128 * i + 128, :] if kind == "p" else psm
                hb = hb_all[bi_]
                e_hb[bi_] = hb
                rms_to_bf16(x1.ap[0:M, bi_, :], [x1.k(bi_)], M, g_ple, hb.ap[0:M, :], hb.k())
                pi_ = bi_
                P.add("act", lambda e: e.activation(out=pbf[pi_].ap[0:M, :], in_=pin[bi_].ap[0:M, :], func=AF.Copy), reads=[pin[bi_].k()], writes=[pbf[pi_].k()])

            def e_b(bi_):
                M, i, kind = blocks[bi_]
                ydst = y[T0 + 128 * i:T0 + 128 * i + 128, :] if kind == "p" else ysm
                hb = e_hb[bi_]
                h3 = h3b[bi_ % 2]
                tb_ = 0 + rot("tbank2", 2)
                transpose_rows(hb.ap[0:M, :], hb.k(), M, 8, (lambda: h3.ap[:, :, 0:M]), [h3.k()], tb_, evac=("act" if tb_ == 0 else "dve"))
                pi_ = bi_
                pT = pTb[bi_ % 2]
                tb2 = 2 + rot("tbank3", 2)
                transpose_rows(pbf[pi_].ap[0:M, :], pbf[pi_].k(), M, 2, (lambda: pT.ap[:, :, 0:M]), [pT.k()], tb2, evac="dve")
                tm_proj(w_pg, 8, (lambda k: h3.ap[:, k, 0:M]), [h3.k()], M, (4, 5))
                tm_proj(w_pp, 2, (lambda k: pT.ap[:, k, 0:M]), [pT.k()], M, (6, 7))
                P.add("act", lambda e: e.activation(out=sgg.ap[0:M, :], in_=psum[0:M, 512 * 4:512 * 4 + 1024], func=AF.Sigmoid), reads=[], writes=psk(4, 5) + [sgg.k()])
                P.add("dve", lambda e: e.tensor_tensor(out=sgg.ap[0:M, :], in0=psum[0:M, 512 * 6:512 * 6 + 1024], in1=sgg.ap[0:M, :], op=ALU.mult), reads=[], writes=psk(6, 7) + [sgg.k()])
                ti = rot("tmp", 2)
                tb = tmp[ti]
                P.add("dve", lambda e: e.tensor_tensor(out=tb.ap[0:M, :], in0=sgg.ap[0:M, :], in1=x1.ap[0:M, bi_, :], op=ALU.add),
                      reads=[sgg.k(), x1.k(bi_)], writes=[tb.k()])
                P.add("sp", lambda e: e.dma_start(out=ydst, in_=tb.ap[0:M, :]), reads=[tb.k()], writes=[("dram", "y", t, bi_)], dma=True, chan=("yout", ti))

            nxt4a = make_4a(t + 1) if t < 3 else []
            hbx = [B("hbx%d" % i, [D], BF16) for i in range(max(0, nb - len(hbb)))]
            hb_all = hbb + hbx
            for bi_ in range(nb):
                e_a(bi_)
            for bi_ in range(nb):
                e_b(bi_)
                for _ in range(2):
                    if nxt4a:
                        nxt4a.pop(0)()
            while nxt4a:
                nxt4a.pop(0)()
            for b in [w_pg, g_ple, sgg, x1] + h3b + pTb + pin + pbf + tmp + hbx:
                b.free()

        bulk_step(len(bulk))
        P.emit(st)
        print("arena peak bytes:", AR.peak, "ops:", {e: len(P.ops[e]) for e in ENGS}, "chans:", len(P.chan_cnt))
    return nc


_NC = None
_LAST = None


def _t5_bucket(dist):
    dist = np.asarray(dist).astype(np.int32)
    d = np.maximum(dist, 1).astype(np.float32)
    large = 16 + np.floor(np.log(d / 16) / np.log(2048 / 16) * 16).astype(np.int32)
    large = np.minimum(large, 31)
    return np.where(dist < 16, dist, large).astype(np.int32)


def kernel(**inp):
    global _NC
    if _NC is None:
        _NC = build_program()
    f32 = np.float32
    A = lambda k: np.ascontiguousarray(np.asarray(inp[k], dtype=f32))
    x_prompt = A("x_prompt"); x_sample = A("x_sample"); p_prompt = A("p_prompt")[0]; p_sample = A("p_sample")[0]
    caches = [A("cache_kv_w128")[0], A("cache_kv_w512")[0], A("cache_kv_w2048")[0]]
    state_conv = A("state_conv")[0]
    rel_bias = A("rel_bias")
    w_in = A("w_in")[0]
    perm = []
    for g in range(3):
        perm += list(range(256 * g, 256 * g + 256)) + list(range(768 + 256 * g, 768 + 256 * g + 256)) + list(range(1536 + 256 * g, 1536 + 256 * g + 256))
    perm += list(range(2304, 6400))
    w_in_r = np.ascontiguousarray(w_in[:, perm])
    vecs = np.ascontiguousarray(np.stack([A("norm_mix_pre")[0], A("norm_mix_post")[0], A("norm_ffn_pre")[0], A("norm_ffn_post")[0],
                                          A("ple_norm")[0], A("conv_b")[0], A("conv_ln_g")[0], A("conv_ln_b")[0]]))
    NEGC = f32(-200.0)
    kk = np.arange(128)[:, None]; qq = np.arange(128)[None, :]
    ebsrc = np.empty((128, 12, 2, 128), f32)
    sbias = np.empty((128, 12), f32)
    for h in range(12):
        dil = GROUPS[h // 4][1]
        jp = qq + 128 - kk
        jc = qq - kk
        ebsrc[:, h, 0, :] = np.where(kk >= qq, rel_bias[_t5_bucket(np.clip(jp, 0, 128) * dil), h], NEGC)
        ebsrc[:, h, 1, :] = np.where(kk <= qq, rel_bias[_t5_bucket(np.clip(jc, 0, 128) * dil), h], NEGC)
        sbias[:, h] = rel_bias[_t5_bucket((128 - np.arange(128)) * dil), h]
    ebsrc = np.ascontiguousarray(ebsrc.reshape(128, -1))
    sbias0 = np.ascontiguousarray(np.broadcast_to(rel_bias[0:1, :], (NS, 12))).astype(f32)
    c_selB = np.zeros((NS, NS, 128), f32)
    c_selT = np.zeros((128, NS, NS), f32)
    for b in range(NS):
        c_selB[b, b, :] = 1.0
        c_selT[:, b, b] = 1.0
    shared = {
        "ebsrc": ebsrc, "sbias": sbias, "sbias0": sbias0, "c_selB": c_selB.reshape(NS, -1), "c_selT": c_selT.reshape(128, -1),
        "w_in": w_in_r, "conv_w": A("conv_w")[0], "vecs": vecs, "w_conv_out": A("w_conv_out")[0], "w_attn_out": A("w_attn_out")[0],
        "w_out": A("w_out")[0], "w_ffn_in": A("w_ffn_in")[0], "w_ffn_out": A("w_ffn_out")[0], "w_ple_gate": A("w_ple_gate")[0],
        "w_ple_proj": A("w_ple_proj")[0],
    }
    in_maps = []
    for c in range(8):
        b, q = c // 4, c % 4
        s0 = NT * q
        m = dict(shared)
        m["xo"] = np.ascontiguousarray(x_prompt[b, s0:s0 + NT])
        m["xh"] = np.ascontiguousarray(x_prompt[b, s0 - NT:s0]) if q > 0 else np.zeros((NT, D), f32)
        m["hv"] = np.full((128, 1), 1.0 if q > 0 else 0.0, f32)
        m["po"] = np.ascontiguousarray(p_prompt[b, s0:s0 + NT])
        sl = slice(NS * c, NS * c + NS)
        m["xs"] = np.ascontiguousarray(x_sample[sl, 0])
        m["psm"] = np.ascontiguousarray(p_sample[sl, 0])
        m["c128"] = np.ascontiguousarray(caches[0][sl].reshape(NS, 128, 512))
        m["c512"] = np.ascontiguousarray(caches[1][sl].reshape(NS, 512, 512))
        m["c2048"] = np.ascontiguousarray(caches[2][sl].reshape(NS, 2048, 512))
        m["sconv"] = np.ascontiguousarray(state_conv[sl])
        in_maps.append(m)
    res = run_bass_kernel_spmd(_NC, in_maps, core_ids=list(range(8)))
    R = res.results
    if DEBUG:
        global _LAST
        _LAST = R
    y_prompt = np.stack([np.concatenate([R[4 * b + q]["y"] for q in range(4)], 0) for b in range(2)], 0).astype(f32)
    y_sample = np.concatenate([R[c]["ysm"] for c in range(8)], 0).reshape(128, 1, D).astype(f32)
    kvp = []
    for g, (win, _) in enumerate(GROUPS):
        name = ["kv128o", "kv512o", "kv2048o"][g]
        kvp.append(np.stack([R[4 * b + 3][name].reshape(win, 2, 4, 64) for b in range(2)], 0)[None].astype(f32))
    conv_p = np.stack([R[4 * b + 3]["convo"] for b in range(2)], 0)[None].astype(f32)
    kvsm = []
    for g, (win, _) in enumerate(GROUPS):
        name = ["kv128s", "kv512s", "kv2048s"][g]
        kvsm.append(np.concatenate([R[c][name] for c in range(8)], 0).reshape(128, win, 2, 4, 64)[None].astype(f32))
    conv_s = np.concatenate([R[c]["convs"] for c in range(8)], 0)[None].astype(f32)
    return (y_prompt, y_sample, kvp[0], kvp[1], kvp[2], conv_p, kvsm[0], kvsm[1], kvsm[2], conv_s)
```

```python
import numpy as np
import concourse.bass as bass
import concourse.mybir as mybir
from concourse.bass_utils import run_bass_kernel_spmd
from contextlib import ExitStack
import types

F32 = mybir.dt.float32
BF16 = mybir.dt.bfloat16
AF = mybir.ActivationFunctionType
ALU = mybir.AluOpType
AX = mybir.AxisListType

D = 1024
NT = 2048
NS = 16
DFF = 2816
NFF = 22
EPS = 1e-6
SCALE = 0.125
GROUPS = ((128, 1), (512, 4), (2048, 16))
ENGS = ("pe", "act", "dve", "pool", "sp")


class _Op:
    __slots__ = ("eng", "fn", "deps", "dma", "chan", "ndma", "cum", "cidx")

    def __init__(self, eng, fn, dma, chan):
        self.eng, self.fn, self.dma, self.chan = eng, fn, dma, chan
        self.deps = []
        self.ndma = 0
        self.cum = 0
        self.cidx = 0


def _freeze(fn, depth=0):
    if not isinstance(fn, types.FunctionType) or depth > 6:
        return fn
    dfl = fn.__defaults__
    if dfl:
        dfl = tuple(_freeze(d, depth + 1) if isinstance(d, types.FunctionType) else d for d in dfl)
    if fn.__closure__ is None:
        if dfl is fn.__defaults__:
            return fn
        return types.FunctionType(fn.__code__, fn.__globals__, fn.__name__, dfl, None)
    cells = []
    for c in fn.__closure__:
        try:
            v = c.cell_contents
        except ValueError:
            cells.append(c)
            continue
        if isinstance(v, types.FunctionType) and v is not fn:
            v = _freeze(v, depth + 1)
        cells.append(types.CellType(v))
    return types.FunctionType(fn.__code__, fn.__globals__, fn.__name__, dfl, tuple(cells))


def _summ(ops):
    best = {}
    for o in ops:
        if o.dma:
            k = ("c", o.chan)
            v = o.cum
        else:
            k = ("e", o.eng)
            v = o.cidx
        b = best.get(k)
        if b is None or v > b[0]:
            best[k] = (v, o)
    return [b[1] for b in best.values()]


class Prog:
    def __init__(self, nc):
        self.nc = nc
        self.ops = {e: [] for e in ENGS}
        self.state = {}
        self.by_name = {}
        self.ghost_ops = {}
        self.chan_cnt = {}
        self.chan_last = {}
        self.misc_rr = 0
        self.ccount = {e: 0 for e in ENGS}

    def _st(self, k):
        s = self.state.get(k)
        if s is None:
            s = [None, []]
            if isinstance(k, tuple):
                g = self.ghost_ops.get(k[0])
                if g:
                    s[1] = list(g)
                self.by_name.setdefault(k[0], []).append(k)
            self.state[k] = s
        return s

    def retire(self, name):
        ops = []
        for k in self.by_name.pop(name, []):
            s = self.state.pop(k)
            if s[0] is not None:
                ops.append(s[0])
            ops.extend(s[1])
        ops.extend(self.ghost_ops.pop(name, []))
        return _summ(ops)

    def add(self, eng, fn, reads=(), writes=(), dma=False, chan=None, ndma=1, extra=()):
        if dma and chan == "misc":
            self.misc_rr += 1
            chan = ("misc", self.misc_rr % 4)
        op = _Op(eng, _freeze(fn), dma, chan)
        deps = {}
        for o in extra:
            if o is not None:
                deps[id(o)] = o
        if dma:
            prev = self.chan_last.get(chan)
            if prev is not None:
                deps[id(prev)] = prev
            self.chan_last[chan] = op
        for k in reads:
            s = self._st(k)
            if s[0] is not None:
                deps[id(s[0])] = s[0]
        for k in writes:
            s = self._st(k)
            if s[0] is not None:
                deps[id(s[0])] = s[0]
            for r in s[1]:
                deps[id(r)] = r
        op.deps = _summ(deps.values())
        if dma:
            assert chan is not None
            op.ndma = ndma
            c = self.chan_cnt.get(chan, 0) + ndma
            self.chan_cnt[chan] = c
            op.cum = c
        else:
            self.ccount[eng] += 1
            op.cidx = self.ccount[eng]
        for k in reads:
            s = self._st(k)
            s[1].append(op)
            if len(s[1]) > 24:
                s[1] = _summ(s[1])
        for k in writes:
            s = self._st(k)
            s[0] = op
            s[1] = []
        self.ops[eng].append(op)
        return op

    def emit(self, stack):
        nc = self.nc
        sems = {e: stack.enter_context(nc.semaphore("s_" + e)) for e in ENGS}
        chan_sems = {}
        for i, c in enumerate(self.chan_cnt):
            chan_sems[c] = stack.enter_context(nc.semaphore("c%d" % i))
        block = stack.enter_context(nc.Block())
        engobj = {"pe": block.tensor, "act": block.scalar, "dve": block.vector,
                  "pool": block.gpsimd, "sp": block.sync}
        prog = self

        def make(ename):
            def body(eng):
                waited = {}
                for op in prog.ops[ename]:
                    for d in op.deps:
                        if d.dma:
                            sem, val, key = chan_sems[d.chan], 16 * d.cum, ("c", d.chan)
                        else:
                            if d.eng == ename and ename == "pe":
                                continue
                            sem, val, key = sems[d.eng], d.cidx, ("e", d.eng)
                        if waited.get(key, 0) >= val:
                            continue
                        waited[key] = val
                        eng.wait_ge(sem, val)
                    res = op.fn(eng)
                    if op.dma:
                        if not isinstance(res, (list, tuple)):
                            res = [res]
                        assert len(res) == op.ndma, (len(res), op.ndma)
                        for r in res:
                            r.then_inc(chan_sems[op.chan], 16)
                    else:
                        if isinstance(res, (list, tuple)):
                            res = res[-1]
                        res.then_inc(sems[ename], 1)
                if ename == "sp":
                    for c, n in prog.chan_cnt.items():
                        eng.wait_ge(chan_sems[c], 16 * n)
            return body

        for e in ENGS:
            engobj[e](make(e))


class Arena:
    def __init__(self, prog, tensor, nbytes):
        self.P, self.t, self.n = prog, tensor, nbytes
        self.free = [(0, nbytes)]
        self.bufs = {}
        self.ghosts = []
        self.uid = 0
        self.peak = 0

    def alloc(self, base, nbytes):
        nbytes = (nbytes + 63) // 64 * 64
        self.uid += 1
        name = "%s#%d" % (base, self.uid)
        for i, (o, s) in enumerate(self.free):
            if s >= nbytes:
                if s == nbytes:
                    self.free.pop(i)
                else:
                    self.free[i] = (o + nbytes, s - nbytes)
                off = o
                break
        else:
            raise RuntimeError("arena full: %s %d free=%s" % (base, nbytes, self.free))
        self.bufs[name] = (off, nbytes)
        self.peak = max(self.peak, off + nbytes)
        ops = []
        for (go, gs, gops) in self.ghosts:
            if go < off + nbytes and off < go + gs:
                ops.extend(gops)
        if ops:
            self.P.ghost_ops[name] = _summ(ops)
        return name, off

    def release(self, name):
        off, nbytes = self.bufs.pop(name)
        ops = self.P.retire(name)
        self.ghosts = [g for g in self.ghosts if not (g[0] >= off and g[0] + g[1] <= off + nbytes)]
        if ops:
            self.ghosts.append((off, nbytes, ops))
        self.free.append((off, nbytes))
        self.free.sort()
        m = []
        for o, s in self.free:
            if m and m[-1][0] + m[-1][1] == o:
                m[-1] = (m[-1][0], m[-1][1] + s)
            else:
                m.append((o, s))
        self.free = m

    def f32(self, off, n):
        return self.t[:, off // 4: off // 4 + n]

    def bf16(self, off, n):
        return self.t[:, off // 4: off // 4 + (n + 1) // 2].bitcast(BF16)


class Buf:
    def __init__(self, ar, base, shape, dt):
        self.ar = ar
        n = int(np.prod(shape))
        self.shape = shape
        self.dt = dt
        self.name, self.off = ar.alloc(base, n * (4 if dt == F32 else 2))
        flat = ar.f32(self.off, n) if dt == F32 else ar.bf16(self.off, n)
        if len(shape) == 1:
            self.ap = flat
        elif len(shape) == 2:
            self.ap = flat.rearrange("p (a b) -> p a b", a=shape[0])
        elif len(shape) == 3:
            self.ap = flat.rearrange("p (a b c) -> p a b c", a=shape[0], b=shape[1])
        else:
            self.ap = flat.rearrange("p (a b c d) -> p a b c d", a=shape[0], b=shape[1], c=shape[2])

    def k(self, *idx):
        return (self.name,) + idx

    def free(self):
        self.ar.release(self.name)


DEBUG = False


def build_program():
    nc = bass.Bass("TRN2", target_bir_lowering=False)

    def din(name, shape):
        return nc.dram_tensor(name, list(shape), F32, kind="ExternalInput").ap()

    def dout(name, shape):
        return nc.dram_tensor(name, list(shape), F32, kind="ExternalOutput").ap()

    xo = din("xo", [NT, D]); xh = din("xh", [NT, D]); po = din("po", [NT, 256])
    xs = din("xs", [NS, D]); psm = din("psm", [NS, 256])
    cch = [din("c128", [NS, 128, 512]), din("c512", [NS, 512, 512]), din("c2048", [NS, 2048, 512])]
    sconv = din("sconv", [NS, 30, D])
    hv = din("hv", [128, 1])
    ebsrc = din("ebsrc", [128, 12 * 2 * 128])
    sbias = din("sbias", [128, 12]); sbias0 = din("sbias0", [NS, 12])
    c_selB = din("c_selB", [NS, NS * 128]); c_selT = din("c_selT", [128, NS * NS])
    w_in = din("w_in", [D, 6400]); conv_w = din("conv_w", [31, D])
    vecs = din("vecs", [8, D])
    w_conv_out = din("w_conv_out", [D, D]); w_attn_out = din("w_attn_out", [256, D]); w_out = din("w_out", [D, D])
    w_ffn_in = din("w_ffn_in", [D, 2 * DFF]); w_ffn_out = din("w_ffn_out", [DFF, D])
    w_ple_gate = din("w_ple_gate", [D, D]); w_ple_proj = din("w_ple_proj", [256, D])

    y = dout("y", [NT, D]); ysm = dout("ysm", [NS, D])
    kvo = [dout("kv128o", [128, 512]), dout("kv512o", [512, 512]), dout("kv2048o", [2048, 512])]
    convo = dout("convo", [30, D])
    kvs = [dout("kv128s", [NS, 128, 512]), dout("kv512s", [NS, 512, 512]), dout("kv2048s", [NS, 2048, 512])]
    convs = dout("convs", [NS, 30, D])

    wscr = nc.dram_tensor("wscr", [16, 128, 8192], BF16, kind="Internal").ap()
    gscr = nc.dram_tensor("gscr", [5, 128, D], F32, kind="Internal").ap()

    st = ExitStack()
    with st:
        ARENA_BYTES = 206 * 1024
        arena_t = st.enter_context(nc.sbuf_tensor("arena", [128, ARENA_BYTES // 4], F32))
        psum = st.enter_context(nc.psum_tensor("psum", [128, 4096], F32))
        P = Prog(nc)
        AR = Arena(P, arena_t, ARENA_BYTES)

        def B(base, shape, dt=F32):
            return Buf(AR, base, shape, dt)

        def dbg(name, ap2d, rows, cols, keys):
            if not DEBUG:
                return
            tt = nc.dram_tensor("dbg_" + name, [rows, cols], F32, kind="ExternalOutput").ap()
            P.add("pool", lambda e: e.dma_start(out=tt, in_=ap2d), reads=keys, writes=[("dram", "dbg", name)], dma=True, chan="dbg")

        def ps(i, n=512, o=0):
            return psum[:, 512 * i + o: 512 * i + o + n]

        def psk(*banks):
            return [("ps", b) for b in banks]

        def psbf(i):
            return psum[:, 512 * i: 512 * (i + 1)].bitcast(BF16)

        rr = {}

        def rot(name, n):
            v = rr.get(name, 0)
            rr[name] = v + 1
            return v % n

        cst = B("cst", [128 + 128 + 64], BF16)
        ident = cst.ap[:, 0:128]
        ones_bf = cst.ap[:, 128:256]
        identf = B("identf", [128], F32)
        epsb = B("eps", [1], F32)
        hvb = B("hvb", [1], F32)
        vcol = B("vcol", [3, 8], F32)
        wT = B("wT", [8, 31], F32)

        P.add("pool", lambda e: e.memset(identf.ap, 0.0), writes=[identf.k()])
        P.add("pool", lambda e: e.affine_select(out=identf.ap, in_=identf.ap, pattern=[[-1, 128]], compare_op=ALU.not_equal,
                                                fill=1.0, base=0, channel_multiplier=1), writes=[identf.k()])
        P.add("dve", lambda e: e.tensor_copy(out=ident, in_=identf.ap), reads=[identf.k()], writes=[cst.k("i")])
        P.add("dve", lambda e: e.memset(cst.ap[:, 128:320], 1.0), writes=[cst.k("o")])
        P.add("dve", lambda e: e.memset(epsb.ap, EPS), writes=[epsb.k()])
        P.add("sp", lambda e: e.dma_start(out=hvb.ap, in_=hv), writes=[hvb.k()], dma=True, chan="misc")

        bulk = []
        def _rows(dst, src, b_, r0, r1):
            n = r1 - r0
            n8 = n - n % 8
            if n8:
                bulk.append((dst[b_, r0:r0 + n8, :].rearrange("(a l) c -> a (l c)", l=8), src[b_, r0 + 1:r0 + 1 + n8, :].rearrange("(a l) c -> a (l c)", l=8)))
            if n % 8:
                bulk.append((dst[b_, r0 + n8:r1, :], src[b_, r0 + 1 + n8:r1 + 1, :]))
        for b_ in range(NS):
            for (r0, r1) in ((0, 512), (512, 1024), (1024, 1536), (1536, 2047)):
                _rows(kvs[2], cch[2], b_, r0, r1)
        for b_ in range(NS):
            _rows(kvs[1], cch[1], b_, 0, 511)
        for j in range(4):
            bulk.append((kvs[0][4 * j:4 * j + 4, 0:127, :].rearrange("b l c -> b (l c)"), cch[0][4 * j:4 * j + 4, 1:128, :].rearrange("b l c -> b (l c)")))
        for j in range(2):
            bulk.append((convs[8 * j:8 * j + 8, 0:29, :].rearrange("b l c -> b (l c)"), sconv[8 * j:8 * j + 8, 1:30, :].rearrange("b l c -> b (l c)")))
        bulk_i = {"i": 0}

        def bulk_step(n=1):
            for _ in range(n):
                i = bulk_i["i"]
                if i >= len(bulk):
                    return
                bulk_i["i"] = i + 1
                o_, i_ = bulk[i]
                P.add("act", lambda e, o_=o_, i_=i_: e.dma_start(out=o_, in_=i_), reads=[], writes=[("dram", "bulk", i)],
                      dma=True, chan=("cpy", i % 2))

        vrow = B("vrow", [D], F32)
        P.add("sp", lambda e: e.dma_start(out=vrow.ap[0:3, :], in_=vecs[5:8, :]), writes=[vrow.k()], dma=True, chan="misc")

        def tr_vc(e):
            r = None
            for kc in range(8):
                r = e.transpose(out=ps(1, 3, 3 * kc), in_=vrow.ap[0:3, kc * 128:(kc + 1) * 128], identity=identf.ap[0:3, 0:3])
            return r
        P.add("pe", tr_vc, reads=[vrow.k(), identf.k()], writes=psk(1))
        P.add("act", lambda e: e.activation(out=vcol.ap.rearrange("p v k -> p k v"), in_=ps(1, 24).rearrange("p (k v) -> p k v", v=3), func=AF.Copy),
              reads=[], writes=psk(1) + [vcol.k()])
        vrow.free()

        cw = B("cw", [D], F32)
        P.add("sp", lambda e: e.dma_start(out=cw.ap[0:31, :], in_=conv_w), writes=[cw.k()], dma=True, chan="misc")

        def tr_cw(e):
            r = None
            for kc in range(8):
                r = e.transpose(out=ps(0, 31, 31 * kc), in_=cw.ap[0:31, kc * 128:(kc + 1) * 128], identity=identf.ap[0:31, 0:31])
            return r
        P.add("pe", tr_cw, reads=[cw.k(), identf.k()], writes=psk(0))
        P.add("act", lambda e: e.activation(out=wT.ap.rearrange("p a b -> p (a b)"), in_=ps(0, 248), func=AF.Copy),
              reads=[], writes=psk(0) + [wT.k()])
        cw.free()

        onesf = B("onesf", [128], F32)
        grow = B("grow", [5, D], F32)
        gtmp = B("gtmp", [D], F32)
        P.add("dve", lambda e: e.memset(onesf.ap, 1.0), writes=[onesf.k()])
        for r_ in range(5):
            P.add("sp", lambda e, r_=r_: e.dma_start(out=grow.ap[0:1, r_, :], in_=vecs[r_:r_ + 1, :]), writes=[grow.k(r_)], dma=True, chan="misc")

            def gb_mm(e, r_=r_):
                e.matmul(out=ps(2), lhsT=onesf.ap[0:1, :], rhs=grow.ap[0:1, r_, 0:512], start=True, stop=True)
                return e.matmul(out=ps(3), lhsT=onesf.ap[0:1, :], rhs=grow.ap[0:1, r_, 512:1024], start=True, stop=True)
            P.add("pe", gb_mm, reads=[onesf.k(), grow.k(r_)], writes=psk(2, 3))
            P.add("act", lambda e: e.activation(out=gtmp.ap, in_=psum[:, 1024:2048], func=AF.Copy), reads=[], writes=psk(2, 3) + [gtmp.k()])
            P.add("sp", lambda e, r_=r_: e.dma_start(out=gscr[r_], in_=gtmp.ap), reads=[gtmp.k()], writes=[("dram", "gscr", r_)], dma=True, chan="misc")
        onesf.free(); grow.free(); gtmp.free()

        def load_gain(row, name):
            gb = B(name, [D], F32)
            P.add("sp", lambda e: e.dma_start(out=gb.ap, in_=gscr[row]), reads=[("dram", "gscr", row)],
                  writes=[gb.k()], dma=True, chan="misc")
            return gb

        small = B("small", [3, 4], F32)

        def rms_to_bf16(src_ap, src_keys, M, gain, hb_ap, hb_key, src_is_psum=False):
            i = rot("small", 4)
            ms = small.ap[0:M, 0, i:i + 1]; sd = small.ap[0:M, 1, i:i + 1]; rs = small.ap[0:M, 2, i:i + 1]
            rk = src_keys if not src_is_psum else []
            wk = src_keys if src_is_psum else []
            P.add("act", lambda e: e.activation(out=junk.ap[0:M, :], in_=src_ap, func=AF.Square, scale=1.0 / 32.0, accum_out=ms),
                  reads=rk, writes=wk + [small.k(0, i)])
            P.add("act", lambda e: e.activation(out=sd, in_=ms, func=AF.Sqrt, bias=epsb.ap[0:M, 0:1], scale=1.0),
                  reads=[small.k(0, i), epsb.k()], writes=[small.k(1, i)])
            P.add("dve", lambda e: e.reciprocal(out=rs, in_=sd), reads=[small.k(1, i)], writes=[small.k(2, i)])
            P.add("dve", lambda e: e.scalar_tensor_tensor(out=hb_ap, in0=src_ap, scalar=rs, in1=gain.ap[0:M, :], op0=ALU.mult, op1=ALU.mult),
                  reads=rk + [small.k(2, i), gain.k()], writes=wk + [hb_key])

        def transpose_rows(hb_ap, hb_key, M, nchunk, dst_fn, dst_keys, bank, evac="act"):
            pb = psbf(bank)

            def f(e):
                r = None
                for kc in range(nchunk):
                    r = e.transpose(out=pb[:, kc * M:(kc + 1) * M], in_=hb_ap[:, kc * 128:(kc + 1) * 128], identity=ident[0:M, 0:M])
                return r
            P.add("pe", f, reads=[hb_key, cst.k("i")], writes=psk(bank))
            src = pb[:, 0:nchunk * M].rearrange("p (a b) -> p a b", a=nchunk)
            if evac == "act":
                P.add("act", lambda e: e.activation(out=dst_fn(), in_=src, func=AF.Copy), reads=[], writes=psk(bank) + dst_keys)
            else:
                P.add("dve", lambda e: e.tensor_copy(out=dst_fn(), in_=src), reads=[], writes=psk(bank) + dst_keys)

        def load_w_own(name, src2d, nchunk, ncol, chan):
            wb = B(name, [nchunk, ncol], BF16)
            P.add("pool", lambda e: e.dma_start(out=wb.ap, in_=src2d.rearrange("(kc p) c -> p kc c", p=128)),
                  writes=[wb.k()], dma=True, chan=chan)
            return wb

        wring = []
        wr_i = {"i": 0}

        class WView:
            def __init__(self, slot, ap):
                self.slot, self.ap = slot, ap

            def k(self, *a):
                return self.slot.k()

            def free(self):
                pass

        def wslot():
            sl = wring[wr_i["i"] % len(wring)]
            wr_i["i"] += 1
            return sl

        def add_wslot():
            sl = B("wr%d" % len(wring), [8192], BF16)
            sl.idx = len(wring)
            wring.append(sl)

        wl = {"tile": None, "li": 0}

        def ring_load(sl, issue_fn, ndma, tile="cur", li=None):
            if tile == "cur":
                tile = wl["tile"]
            if tile is not None and li is None:
                li = wl["li"]
                wl["li"] += 1
            if tile is None or tile == 0:
                P.add("pool", issue_fn, writes=[sl.k()], dma=True, chan=("w", sl.idx), ndma=ndma)
                if tile == 0:
                    P.add("act", lambda e: e.dma_start(out=wscr[li], in_=sl.ap), reads=[sl.k()], writes=[("dram", "wscr", li)],
                          dma=True, chan=("wst", li % 2))
            else:
                P.add("pool", lambda e: e.dma_start(out=sl.ap, in_=wscr[li]), reads=[("dram", "wscr", li)], writes=[sl.k()],
                      dma=True, chan=("w", sl.idx))

        def load_w_pair(srcA, srcB, ncol, tile="cur", li=None):
            sl = wslot()
            ap4 = sl.ap[:, 0:16 * ncol].rearrange("p (a b c) -> p a b c", a=8, b=2)

            def ld(e):
                a_ = e.dma_start(out=ap4[:, :, 0, :], in_=srcA.rearrange("(kc p) c -> p kc c", p=128))
                b_ = e.dma_start(out=ap4[:, :, 1, :], in_=srcB.rearrange("(kc p) c -> p kc c", p=128))
                return [a_, b_]
            ring_load(sl, ld, 2, tile, li)
            return WView(sl, ap4[:, :, 0, :]), WView(sl, ap4[:, :, 1, :])

        def load_w(name, src2d, nchunk, ncol, chan=None):
            sl = wslot()
            ap = sl.ap[:, 0:nchunk * ncol].rearrange("p (a b) -> p a b", a=nchunk)
            ring_load(sl, lambda e: e.dma_start(out=ap, in_=src2d.rearrange("(kc p) c -> p kc c", p=128)), 1)
            return WView(sl, ap)

        junk = B("junk", [D], F32)

        g_pre = load_gain(0, "g_pre")
        hT = B("hT", [8, NT], BF16)
        hTh = B("hTh", [8, NT], BF16)
        hsT = B("hsT", [8, NS], BF16)
        hTh30 = B("hTh30", [8, 32], BF16)
        xin = [B("xin%d" % i, [D], F32) for i in range(3)]
        hbb = [B("hbb%d" % i, [D], BF16) for i in range(2)]

        hbb.append(B("hbb2", [D], BF16))
        p1 = []
        for blk in range(16):
            p1.append((xh[blk * 128:(blk + 1) * 128, :], 128, (lambda blk=blk: hTh.ap[:, :, blk * 128:(blk + 1) * 128]), [hTh.k(blk)]))
        p1.append((xs, NS, (lambda: hsT.ap), [hsT.k()]))
        for blk in range(16):
            p1.append((xo[blk * 128:(blk + 1) * 128, :], 128, (lambda blk=blk: hT.ap[:, :, blk * 128:(blk + 1) * 128]), [hT.k(blk)]))
        p1hb = {}

        def p1_a(ix):
            src_rows, M, dst_fn, dst_keys = p1[ix]
            i = rot("xin", len(xin))
            xb = xin[i]
            P.add("sp", lambda e: e.dma_start(out=xb.ap[0:M, :], in_=src_rows), writes=[xb.k()], dma=True, chan=("xin", i))
            hb = hbb[rot("hbb", len(hbb))]
            p1hb[ix] = hb
            rms_to_bf16(xb.ap[0:M, :], [xb.k()], M, g_pre, hb.ap[0:M, :], hb.k())

        def p1_b(ix):
            src_rows, M, dst_fn, dst_keys = p1[ix]
            hb = p1hb[ix]
            bank = rot("p1bank", 2)
            transpose_rows(hb.ap[0:M, :], hb.k(), M, 8, dst_fn, dst_keys, bank, evac=("act" if bank == 0 else "dve"))
            if ix == 15:
                P.add("dve", lambda e: e.tensor_copy(out=hTh30.ap[:, :, 0:30], in_=hTh.ap[:, :, NT - 30:NT]), reads=[hTh.k(15)], writes=[hTh30.k()])

        for ix in range(len(p1) + 2):
            if ix < len(p1):
                p1_a(ix)
            if ix > 1:
                p1_b(ix - 2)
        g_pre.free()
        while xin:
            xin.pop().free()
        hTh_all = [hTh.k(b) for b in range(16)]
        hT_all = [hT.k(b) for b in range(16)]

        def blk_cols(g, r, qb):
            dil = GROUPS[g][1]
            s0 = r + dil * 128 * qb
            return slice(s0, s0 + dil * 127 + 1, dil)

        add_wslot(); add_wslot()
        kTh = [B("kTh0", [2, 128], BF16), B("kTh1", [2, 512], BF16), B("kTh2", [2, NT], BF16)]
        vh = [B("vh0", [1, 256], BF16), B("vh1", [4, 256], BF16), B("vh2", [16, 256], BF16)]
        halo_lo = [NT - 128, NT - 512, 0]
        evt = {"n": 0}

        def evac_copy(out_ap, in_ap, reads, writes, scale=None):
            evt["n"] += 1
            if evt["n"] % 2 == 0:
                if scale is None:
                    P.add("act", lambda e: e.activation(out=out_ap, in_=in_ap, func=AF.Copy), reads=reads, writes=writes)
                else:
                    P.add("act", lambda e: e.activation(out=out_ap, in_=in_ap, func=AF.Copy, scale=scale), reads=reads, writes=writes)
            else:
                if scale is None:
                    P.add("dve", lambda e: e.tensor_copy(out=out_ap, in_=in_ap), reads=reads, writes=writes)
                else:
                    P.add("dve", lambda e: e.tensor_scalar(out=out_ap, in0=in_ap, scalar1=scale, scalar2=None, op0=ALU.mult), reads=reads, writes=writes)

        def proj_fm(wb, wcol0, src_ap_fn, src_keys, ncols, dst_ap, dst_keys, scale=None):
            bank = rot("pbank", 4)

            def f(e):
                r = None
                for kc in range(8):
                    r = e.matmul(out=ps(bank, ncols), lhsT=wb.ap[:, kc, wcol0:wcol0 + 128], rhs=src_ap_fn(kc), start=(kc == 0), stop=(kc == 7))
                return r
            P.add("pe", f, reads=[wb.k()] + src_keys, writes=psk(bank))
            evac_copy(dst_ap, ps(bank, ncols), [], psk(bank) + dst_keys, scale)

        for g in range(3):
            wq = load_w("wqkvh", w_in[:, 768 * g:768 * g + 768], 8, 768, ("w", rot("wslot", 3)))
            lo = halo_lo[g]
            nh = NT - lo
            for c2 in range(2):
                for t0 in range(0, nh, 512):
                    n = min(512, nh - t0)
                    proj_fm(wq, 256 + 128 * c2, (lambda kc, a=lo + t0, n=n: hTh.ap[:, kc, a:a + n]), hTh_all, n,
                            kTh[g].ap[:, c2, t0:t0 + n], [kTh[g].k(c2, t0)])
            nres = GROUPS[g][1]
            for r in range(nres):
                cols = blk_cols(g, r, {0: 15, 1: 3, 2: 0}[g])
                bank = rot("pbank", 4)

                def f(e, cols=cols, bank=bank, wq=wq):
                    rr_ = None
                    for kc in range(8):
                        rr_ = e.matmul(out=ps(bank, 256), lhsT=hTh.ap[:, kc, cols], rhs=wq.ap[:, kc, 512:768], start=(kc == 0), stop=(kc == 7))
                    return rr_
                P.add("pe", f, reads=[wq.k()] + hTh_all, writes=psk(bank))
                evac_copy(vh[g].ap[:, r, :], ps(bank, 256), [], psk(bank) + [vh[g].k(r)])
            wq.free()
        hTh.free()

        numacc = B("numacc", [2, NT], F32)
        denacc = B("denacc", [2, NT], F32)
        P.add("dve", lambda e: e.memset(numacc.ap, 0.0), writes=[numacc.k()])
        P.add("dve", lambda e: e.memset(denacc.ap, 0.0), writes=[denacc.k()])
        oT = B("oT", [2, NT], BF16)
        qs_s = B("qs_s", [3, 768], F32)
        ebuf = [B("ebuf%d" % i, [256], F32) for i in range(4)]
        pbuf = [B("pbuf%d" % i, [256], BF16) for i in range(4)]
        kvst = [B("kvst%d" % i, [512], F32) for i in range(2)]

        for g in range(3):
            win, dil = GROUPS[g]
            wq = load_w("wqkv", w_in[:, 768 * g:768 * g + 768], 8, 768, ("w", rot("wslot", 3)))
            ebs = B("ebs", [4, 256], F32)
            P.add("sp", lambda e, g=g: e.dma_start(out=ebs.ap.rearrange("p a b -> p (a b)"), in_=ebsrc[:, g * 1024:(g + 1) * 1024]),
                  writes=[ebs.k()], dma=True, chan="misc")
            EBn = B("EBn", [4, 256], F32)
            EBf = B("EBf", [4, 256], F32)
            P.add("act", lambda e: e.activation(out=EBn.ap, in_=ebs.ap, func=AF.Exp), reads=[ebs.k()], writes=[EBn.k()])
            P.add("dve", lambda e: e.tensor_copy(out=EBf.ap[:, :, 128:256], in_=EBn.ap[:, :, 128:256]), reads=[EBn.k()], writes=[EBf.k("c")])
            P.add("dve", lambda e: e.tensor_scalar(out=EBf.ap[:, :, 0:128], in0=EBn.ap[:, :, 0:128], scalar1=hvb.ap[:, 0:1], scalar2=None, op0=ALU.mult),
                  reads=[EBn.k(), hvb.k()], writes=[EBf.k("p")])
            qT = B("qT", [2, NT], BF16)
            kT = B("kT", [2, NT], BF16)
            vo = B("vo", [16, 256], BF16)
            for c2 in range(2):
                for t in range(4):
                    proj_fm(wq, 128 * c2, (lambda kc, t=t: hT.ap[:, kc, t * 512:(t + 1) * 512]), hT_all, 512,
                            qT.ap[:, c2, t * 512:(t + 1) * 512], [qT.k(c2, t)], scale=SCALE)
                    proj_fm(wq, 256 + 128 * c2, (lambda kc, t=t: hT.ap[:, kc, t * 512:(t + 1) * 512]), hT_all, 512,
                            kT.ap[:, c2, t * 512:(t + 1) * 512], [kT.k(c2, t)])
            qT_all = [qT.k(c2, t) for c2 in range(2) for t in range(4)]
            kT_all = [kT.k(c2, t) for c2 in range(2) for t in range(4)]
            nq = 16 // dil if dil < 16 else 1
            nres = dil
            for r in range(nres):
                for qb in range(nq):
                    bi = r * nq + qb
                    need_out = (qb == nq - 1)
                    cols = blk_cols(g, r, qb)
                    bank = rot("pbank", 4)
                    c0 = 256 if need_out else 512
                    ncol = 768 - c0

                    def f(e, cols=cols, bank=bank, c0=c0, ncol=ncol, wq=wq):
                        rr_ = None
                        for kc in range(8):
                            rr_ = e.matmul(out=ps(bank, ncol), lhsT=hT.ap[:, kc, cols], rhs=wq.ap[:, kc, c0:768], start=(kc == 0), stop=(kc == 7))
                        return rr_
                    P.add("pe", f, reads=[wq.k()] + hT_all, writes=psk(bank))
                    if need_out:
                        si = rot("kvst", 2)
                        sb_ = kvst[si]
                        P.add("act", lambda e, sb_=sb_, bank=bank: e.activation(out=sb_.ap, in_=ps(bank, 512), func=AF.Copy), reads=[], writes=psk(bank) + [sb_.k()])
                        P.add("dve", lambda e, bi=bi, bank=bank: e.tensor_copy(out=vo.ap[:, bi, :], in_=ps(bank, 256, 256)), reads=[], writes=psk(bank) + [vo.k(bi)])
                        P.add("sp", lambda e, sb_=sb_, r=r, g=g, dil=dil: e.dma_start(out=kvo[g][r::dil, :] if dil > 1 else kvo[g], in_=sb_.ap),
                              reads=[sb_.k()], writes=[("dram", "kvo", g, r)], dma=True, chan=("kvst", si))
                    else:
                        evac_copy(vo.ap[:, bi, :], ps(bank, 256), [], psk(bank) + [vo.k(bi)])
            bank = rot("pbank", 4)
            b2 = rot("pbank", 4)

            def fs(e, bank=bank, b2=b2, wq=wq):
                rr_ = None
                for kc in range(8):
                    e.matmul(out=psum[0:NS, 512 * bank:512 * bank + 512], lhsT=hsT.ap[:, kc, :], rhs=wq.ap[:, kc, 0:512], start=(kc == 0), stop=(kc == 7))
                for kc in range(8):
                    rr_ = e.matmul(out=psum[0:NS, 512 * b2:512 * b2 + 256], lhsT=hsT.ap[:, kc, :], rhs=wq.ap[:, kc, 512:768], start=(kc == 0), stop=(kc == 7))
                return rr_
            P.add("pe", fs, reads=[wq.k(), hsT.k()], writes=psk(bank, b2))
            P.add("act", lambda e, g=g, bank=bank: e.activation(out=qs_s.ap[0:NS, g, 0:512], in_=psum[0:NS, 512 * bank:512 * bank + 512], func=AF.Copy),
                  reads=[], writes=psk(bank) + [qs_s.k(g, 0)])
            P.add("act", lambda e, g=g, b2=b2: e.activation(out=qs_s.ap[0:NS, g, 512:768], in_=psum[0:NS, 512 * b2:512 * b2 + 256], func=AF.Copy),
                  reads=[], writes=psk(b2) + [qs_s.k(g, 1)])
            P.add("sp", lambda e, g=g, win=win: e.dma_start(out=kvs[g][:, win - 1, :], in_=qs_s.ap[0:NS, g, 256:768]),
                  reads=[qs_s.k(g, 0), qs_s.k(g, 1)], writes=[("dram", "kvs_new", g)], dma=True, chan="misc")
            wq.free()

            vo_all = [vo.k(b) for b in range(16)]
            for c2 in range(2):
                for quad in range(4):
                    if g == 0:
                        qblocks = [(0, 4 * quad + j) for j in range(4)]
                    elif g == 1:
                        qblocks = [(quad, j) for j in range(4)]
                    else:
                        qblocks = [(4 * quad + j, 0) for j in range(4)]
                    nb_, db_ = ((6, 7) if rot("ndbank", 2) == 0 else (2, 3))
                    units = [(hh, j, r, qb) for hh in range(2) for j, (r, qb) in enumerate(qblocks)]
                    ust = {}

                    def att_a(u):
                        hh, j, r, qb = units[u]
                        s_ = 2 * c2 + hh
                        pr = slice(64 * hh, 64 * hh + 64)
                        bi = r * nq + qb
                        qcols = blk_cols(g, r, qb)
                        sbank = (4, 5, 0, 1)[rot("sbank", 4)]
                        first = (qb == 0)
                        if first:
                            hcols = blk_cols(g, r, {0: 15, 1: 3, 2: 0}[g])
                            hc = slice(hcols.start - halo_lo[g], hcols.stop - halo_lo[g], hcols.step)
                            kprev = kTh[g].ap[pr, c2, hc]
                            vprev = vh[g].ap[:, r, 64 * s_:64 * s_ + 64]
                            kprev_keys = [kTh[g].k(c2, t0) for t0 in range(0, NT - halo_lo[g], 512)]
                            vprev_keys = [vh[g].k(r)]
                        else:
                            kprev = kT.ap[pr, c2, blk_cols(g, r, qb - 1)]
                            vprev = vo.ap[:, bi - 1, 64 * s_:64 * s_ + 64]
                            kprev_keys = kT_all
                            vprev_keys = [vo.k(bi - 1)]
                        kcur = kT.ap[pr, c2, qcols]
                        vcur = vo.ap[:, bi, 64 * s_:64 * s_ + 64]
                        qsl = qT.ap[pr, c2, qcols]

                        def fsc(e):
                            e.matmul(out=ps(sbank, 128, 0), lhsT=kprev, rhs=qsl, start=True, stop=True)
                            return e.matmul(out=ps(sbank, 128, 128), lhsT=kcur, rhs=qsl, start=True, stop=True)
                        P.add("pe", fsc, reads=qT_all + kT_all + kprev_keys, writes=psk(sbank))
                        eb = ebuf[rot("ebuf", 4)]
                        P.add("act", lambda e: e.activation(out=eb.ap, in_=ps(sbank, 256), func=AF.Exp), reads=[], writes=psk(sbank) + [eb.k()])
                        pb = pbuf[rot("pbuf", 4)]
                        EB = EBf if first else EBn
                        ebv = EB.ap[:, s_, :]
                        P.add("dve", lambda e: e.tensor_tensor(out=pb.ap, in0=eb.ap, in1=ebv, op=ALU.mult),
                              reads=[eb.k(), EB.k("c"), EB.k("p"), EB.k()], writes=[pb.k()])
                        ust[u] = (pb, vprev, vcur, pr, j, vprev_keys, bi)

                    def att_b(u):
                        pb, vprev, vcur, pr, j, vprev_keys, bi = ust[u]

                        def fpv(e):
                            on = psum[pr, 512 * nb_ + 128 * j: 512 * nb_ + 128 * j + 128]
                            od = psum[pr, 512 * db_ + 128 * j: 512 * db_ + 128 * j + 128]
                            e.matmul(out=on, lhsT=vprev, rhs=pb.ap[:, 0:128], start=True, stop=False)
                            e.matmul(out=on, lhsT=vcur, rhs=pb.ap[:, 128:256], start=False, stop=True)
                            e.matmul(out=od, lhsT=cst.ap[:, 256:320], rhs=pb.ap[:, 0:128], start=True, stop=False)
                            return e.matmul(out=od, lhsT=cst.ap[:, 256:320], rhs=pb.ap[:, 128:256], start=False, stop=True)
                        P.add("pe", fpv, reads=[pb.k(), cst.k("o")] + vprev_keys + [vo.k(bi)], writes=psk(nb_, db_))

                    for u in range(len(units) + 2):
                        if u < len(units):
                            att_a(u)
                        if u > 1:
                            att_b(u - 2)
                    if g == 0:
                        dn = numacc.ap[:, c2, 512 * quad:512 * quad + 512]
                        dd = denacc.ap[:, c2, 512 * quad:512 * quad + 512]
                        sn, sd_ = ps(nb_), ps(db_)
                    elif g == 1:
                        dn = numacc.ap[:, c2, quad::4]
                        dd = denacc.ap[:, c2, quad::4]
                        sn, sd_ = ps(nb_), ps(db_)
                    else:
                        dn = numacc.ap[:, c2, :].rearrange("p (i r) -> p r i", r=16)[:, 4 * quad:4 * quad + 4, :]
                        dd = denacc.ap[:, c2, :].rearrange("p (i r) -> p r i", r=16)[:, 4 * quad:4 * quad + 4, :]
                        sn = ps(nb_).rearrange("p (a b) -> p a b", a=4)
                        sd_ = ps(db_).rearrange("p (a b) -> p a b", a=4)
                    P.add("dve", lambda e, dn=dn, sn=sn: e.tensor_tensor(out=dn, in0=sn, in1=dn, op=ALU.add),
                          reads=[], writes=psk(nb_) + [numacc.k()])
                    P.add("dve", lambda e, dd=dd, sd_=sd_: e.tensor_tensor(out=dd, in0=sd_, in1=dd, op=ALU.add),
                          reads=[], writes=psk(db_) + [denacc.k()])
                    bulk_step(2)
            qT.free(); kT.free(); vo.free(); ebs.free(); EBn.free(); EBf.free()

        P.add("dve", lambda e: e.reciprocal(out=denacc.ap, in_=denacc.ap), reads=[], writes=[denacc.k()])
        P.add("dve", lambda e: e.tensor_tensor(out=oT.ap, in0=numacc.ap, in1=denacc.ap, op=ALU.mult),
              reads=[numacc.k(), denacc.k()], writes=[oT.k()])
        dbg("oT", oT.ap.rearrange("p a b -> p (a b)"), 128, 2 * NT, [oT.k()])
        numacc.free(); denacc.free()
        for b in ebuf + pbuf + kvst:
            b.free()
        for b in kTh + vh:
            b.free()

        selB = B("selB", [NS, 128], F32)
        selT = B("selT", [NS, NS], F32)
        sbs = B("sbs", [12], F32)
        sb0 = B("sb0", [12], F32)
        P.add("sp", lambda e: e.dma_start(out=selB.ap[0:NS].rearrange("p a b -> p (a b)"), in_=c_selB), writes=[selB.k()], dma=True, chan="misc")
        P.add("sp", lambda e: e.dma_start(out=selT.ap.rearrange("p a b -> p (a b)"), in_=c_selT), writes=[selT.k()], dma=True, chan="misc")
        P.add("sp", lambda e: e.dma_start(out=sbs.ap, in_=sbias), writes=[sbs.k()], dma=True, chan="misc")
        P.add("sp", lambda e: e.dma_start(out=sb0.ap[0:NS], in_=sbias0), writes=[sb0.k()], dma=True, chan="misc")
        sa_num = B("sa_num", [3, 260], F32)
        ctile = [B("ctile%d" % i, [512], F32) for i in range(4)]
        wv = [B("wvaug%d" % i, [260], F32) for i in range(3)]
        prodb = [B("prod%d" % i, [256], F32) for i in range(2)]
        sct = B("sct", [4, 4], F32)
        qs_keys = [qs_s.k(g, i) for g in range(3) for i in range(2)]
        its = [(g, b) for g in range(3) for b in range(NS)]
        sa_st = {}

        def sa_a1(i):
            g, b = its[i]
            dil = GROUPS[g][1]
            ci = i % 4
            ct = ctile[ci]
            P.add("sp", lambda e: e.dma_start(out=ct.ap, in_=cch[g][b, 0::dil, :] if dil > 1 else cch[g][b]),
                  writes=[ct.k()], dma=True, chan=("ctile", ci))
            bb = rot("pbank", 4)
            P.add("pe", lambda e: e.matmul(out=ps(bb, 256), lhsT=selB.ap[0:NS, b, :], rhs=qs_s.ap[0:NS, g, 0:256], start=True, stop=True),
                  reads=[selB.k()] + qs_keys, writes=psk(bb))
            pr_ = prodb[i % 2]
            P.add("dve", lambda e: e.tensor_tensor(out=pr_.ap, in0=ct.ap[:, 0:256], in1=ps(bb, 256), op=ALU.mult),
                  reads=[ct.k()], writes=psk(bb) + [pr_.k()])
            si = i % 4
            P.add("dve", lambda e: e.tensor_reduce(out=sct.ap[:, si, :], in_=pr_.ap.rearrange("p (a b) -> p a b", a=4), axis=AX.X, op=ALU.add),
                  reads=[pr_.k()], writes=[sct.k(si)])
            P.add("dve", lambda e: e.scalar_tensor_tensor(out=sct.ap[:, si, :], in0=sct.ap[:, si, :], scalar=SCALE, in1=sbs.ap[:, 4 * g:4 * g + 4], op0=ALU.mult, op1=ALU.add),
                  reads=[sbs.k()], writes=[sct.k(si)])
            w_ = wv[i % 3]
            P.add("act", lambda e: e.activation(out=w_.ap[:, 256:260], in_=sct.ap[:, si, :], func=AF.Exp),
                  reads=[sct.k(si)], writes=[w_.k("e")])
            sa_st[i] = (ct, w_)

        def sa_a2(i):
            ct, w_ = sa_st[i]
            P.add("dve", lambda e: e.tensor_tensor(out=w_.ap[:, 0:256].rearrange("p (a b) -> p a b", a=4),
                                                   in0=ct.ap[:, 256:512].rearrange("p (a b) -> p a b", a=4),
                                                   in1=w_.ap[:, 256:260].unsqueeze(2).broadcast_to([128, 4, 64]), op=ALU.mult),
                  reads=[ct.k(), w_.k("e")], writes=[w_.k("v")])

        def sa_b(i):
            g, b = its[i]
            ct, w_ = sa_st[i]
            P.add("pe", lambda e: e.matmul(out=psum[0:NS, 512 * (5 + g): 512 * (5 + g) + 260], lhsT=selT.ap[:, b, :], rhs=w_.ap,
                                           start=(b == 0), stop=(b == NS - 1)),
                  reads=[w_.k("e"), w_.k("v"), selT.k()], writes=psk(5 + g))
            if b == NS - 1:
                P.add("act", lambda e: e.activation(out=sa_num.ap[0:NS, g, :], in_=psum[0:NS, 512 * (5 + g): 512 * (5 + g) + 260], func=AF.Copy),
                      reads=[], writes=psk(5 + g) + [sa_num.k(g)])

        for i in range(len(its) + 2):
            if i < len(its):
                sa_a1(i)
            if 0 <= i - 1 < len(its):
                sa_a2(i - 1)
            if 0 <= i - 2 < len(its):
                sa_b(i - 2)
        sself = B("sself", [12 * 64 + 12 + 12], F32)
        sp_ = sself.ap[0:NS, 0:768].rearrange("p (a b) -> p a b", a=12)
        s0 = sself.ap[0:NS, 768:780]
        e0 = sself.ap[0:NS, 780:792]
        qv = qs_s.ap[0:NS]
        P.add("dve", lambda e: e.tensor_tensor(out=sself.ap[0:NS, 0:768].rearrange("p (g c) -> p g c", g=3), in0=qv[:, :, 0:256], in1=qv[:, :, 256:512], op=ALU.mult),
              reads=qs_keys, writes=[sself.k("p")])
        P.add("dve", lambda e: e.tensor_reduce(out=s0, in_=sp_, axis=AX.X, op=ALU.add), reads=[sself.k("p")], writes=[sself.k("s")])
        P.add("dve", lambda e: e.scalar_tensor_tensor(out=s0, in0=s0, scalar=SCALE, in1=sb0.ap[0:NS, :], op0=ALU.mult, op1=ALU.add),
              reads=[sb0.k()], writes=[sself.k("s")])
        P.add("act", lambda e: e.activation(out=e0, in_=s0, func=AF.Exp), reads=[sself.k("s")], writes=[sself.k("e")])
        P.add("dve", lambda e: e.tensor_tensor(out=sself.ap[0:NS, 0:768].rearrange("p (g s d) -> p g s d", g=3, s=4),
                                               in0=qv[:, :, 512:768].rearrange("p g (s d) -> p g s d", s=4),
                                               in1=sself.ap[0:NS, 780:792].rearrange("p (g s) -> p g s", g=3).unsqueeze(3).broadcast_to([NS, 3, 4, 64]), op=ALU.mult),
              reads=qs_keys + [sself.k("e")], writes=[sself.k("p")])
        san = sa_num.ap[0:NS]
        P.add("dve", lambda e: e.tensor_tensor(out=san[:, :, 0:256], in0=san[:, :, 0:256], in1=sself.ap[0:NS, 0:768].rearrange("p (g c) -> p g c", g=3), op=ALU.add),
              reads=[sself.k("p")], writes=[sa_num.k(0), sa_num.k(1), sa_num.k(2)])
        P.add("dve", lambda e: e.tensor_tensor(out=san[:, :, 256:260], in0=san[:, :, 256:260], in1=sself.ap[0:NS, 780:792].rearrange("p (g s) -> p g s", g=3), op=ALU.add),
              reads=[sself.k("e")], writes=[sa_num.k(0), sa_num.k(1), sa_num.k(2)])
        P.add("dve", lambda e: e.tensor_tensor(out=san[:, 0, :], in0=san[:, 0, :], in1=san[:, 1, :], op=ALU.add), reads=[], writes=[sa_num.k(0), sa_num.k(1), sa_num.k(2)])
        P.add("dve", lambda e: e.tensor_tensor(out=san[:, 0, :], in0=san[:, 0, :], in1=san[:, 2, :], op=ALU.add), reads=[], writes=[sa_num.k(0), sa_num.k(1), sa_num.k(2)])
        P.add("dve", lambda e: e.reciprocal(out=san[:, 0, 256:260], in_=san[:, 0, 256:260]), reads=[], writes=[sa_num.k(0), sa_num.k(1), sa_num.k(2)])
        P.add("dve", lambda e: e.tensor_tensor(out=san[:, 1, 0:256].rearrange("p (s d) -> p s d", s=4), in0=san[:, 0, 0:256].rearrange("p (s d) -> p s d", s=4),
                                               in1=san[:, 0, 256:260].unsqueeze(2).broadcast_to([NS, 4, 64]), op=ALU.mult),
              reads=[], writes=[sa_num.k(0), sa_num.k(1), sa_num.k(2)])
        osT = B("osT", [2, NS], BF16)

        def tr_os(e):
            r = None
            for kc in range(2):
                r = e.transpose(out=ps(0, NS, NS * kc), in_=sa_num.ap[0:NS, 1, kc * 128:(kc + 1) * 128], identity=identf.ap[0:NS, 0:NS])
            return r
        P.add("pe", tr_os, reads=[sa_num.k(1), identf.k()], writes=psk(0))
        P.add("act", lambda e: e.activation(out=osT.ap.rearrange("p a b -> p (a b)"), in_=ps(0, 2 * NS), func=AF.Copy), reads=[], writes=psk(0) + [osT.k()])
        dbg("osT", osT.ap.rearrange("p a b -> p (a b)"), 128, 2 * NS, [osT.k()])
        for b in [selB, selT, sbs, sb0, sa_num, sct, sself, qs_s] + ctile + wv + prodb:
            b.free()

        w_ao = load_w_own("w_ao", w_attn_out, 2, D, "wsmall")
        w_pp = load_w_own("w_pp", w_ple_proj, 2, D, "wsmall")
        hbb.pop().free()
        xin.extend([B("xin%d" % i, [D], F32) for i in range(2)])
        add_wslot(); add_wslot()
        zT = B("zT", [8, 30 + 512], BF16)
        zsT = B("zsT", [8, NS], F32)
        histT = B("histT", [8, NS * 30], F32)
        zlast = B("zlast", [8, 30], F32)

        for q4 in range(4):
            i = rot("xin", len(xin))
            xb = xin[i]
            P.add("sp", lambda e, xb=xb, q4=q4: e.dma_start(out=xb.ap[0:120, :], in_=sconv[4 * q4:4 * q4 + 4].rearrange("b i c -> (b i) c")),
                  writes=[xb.k()], dma=True, chan=("xin", i))
            for half in range(2):
                bank = rot("pbank", 4)

                def ft(e, xb=xb, bank=bank, half=half):
                    r = None
                    for k4 in range(4):
                        kc = 4 * half + k4
                        r = e.transpose(out=ps(bank, 120, 120 * k4), in_=xb.ap[0:120, kc * 128:(kc + 1) * 128], identity=identf.ap[0:120, 0:120])
                    return r
                P.add("pe", ft, reads=[xb.k(), identf.k()], writes=psk(bank))
                evac_copy(histT.ap[:, 4 * half:4 * half + 4, 120 * q4:120 * q4 + 120], ps(bank, 480).rearrange("p (a b) -> p a b", a=4),
                          [], psk(bank) + [histT.k(q4, half)])
        histT_all = [histT.k(q4, h) for q4 in range(4) for h in range(2)]
        while xin:
            xin.pop().free()

        def ln_silu(ysrc, ykeys, ncols, dstT, dkeys, tag):
            ybf = B("ybf", [8, ncols], BF16)
            ysq = B("ysq", [8, ncols], BF16)
            stt = B("stt", [3, ncols], F32)
            P.add("act", lambda e: e.activation(out=ybf.ap, in_=ysrc, func=AF.Copy), reads=ykeys, writes=[ybf.k()])
            P.add("act", lambda e: e.activation(out=ysq.ap, in_=ysrc, func=AF.Square), reads=ykeys, writes=[ysq.k()])

            def fst(e):
                r = None
                for kc in range(8):
                    e.matmul(out=ps(2, ncols), lhsT=ones_bf, rhs=ybf.ap[:, kc, :], start=(kc == 0), stop=(kc == 7))
                for kc in range(8):
                    r = e.matmul(out=ps(3, ncols), lhsT=ones_bf, rhs=ysq.ap[:, kc, :], start=(kc == 0), stop=(kc == 7))
                return r
            P.add("pe", fst, reads=[ybf.k(), ysq.k(), cst.k("o")], writes=psk(2, 3))
            mean = stt.ap[:, 0, :]; var = stt.ap[:, 1, :]; rstd = stt.ap[:, 2, :]
            P.add("act", lambda e: e.activation(out=mean, in_=ps(2, ncols), func=AF.Copy, scale=1.0 / D), reads=[], writes=psk(2) + [stt.k(0)])
            P.add("dve", lambda e: e.tensor_tensor(out=var, in0=mean, in1=mean, op=ALU.mult), reads=[stt.k(0)], writes=[stt.k(1)])
            P.add("dve", lambda e: e.scalar_tensor_tensor(out=var, in0=ps(3, ncols), scalar=1.0 / D, in1=var, op0=ALU.mult, op1=ALU.subtract),
                  reads=[], writes=psk(3) + [stt.k(1)])
            P.add("act", lambda e: e.activation(out=var, in_=var, func=AF.Sqrt, bias=epsb.ap[:, 0:1], scale=1.0), reads=[epsb.k()], writes=[stt.k(1)])
            P.add("dve", lambda e: e.reciprocal(out=rstd, in_=var), reads=[stt.k(1)], writes=[stt.k(2)])
            for kc in range(8):
                P.add("dve", lambda e, kc=kc: e.tensor_tensor(out=ysrc[:, kc, :], in0=ysrc[:, kc, :], in1=mean, op=ALU.subtract), reads=[stt.k(0)], writes=ykeys)
                P.add("dve", lambda e, kc=kc: e.tensor_tensor(out=ysrc[:, kc, :], in0=ysrc[:, kc, :], in1=rstd, op=ALU.mult), reads=[stt.k(2)], writes=ykeys)
                P.add("act", lambda e, kc=kc: e.activation(out=dstT[:, kc, :], in_=ysrc[:, kc, :], func=AF.Silu, bias=vcol.ap[:, 2, kc:kc + 1], scale=vcol.ap[:, 1, kc:kc + 1]),
                      reads=ykeys + [vcol.k()], writes=dkeys)
            ybf.free(); ysq.free(); stt.free()

        def make_4a(tt):
            T0_ = tt * 512
            hkeys_ = [hT.k(4 * tt + i) for i in range(4)]
            st4 = {}

            def hsrc_(kc):
                return hT.ap[:, kc, T0_:T0_ + 512]

            def step(q4, k4):
                def run():
                    if q4 == 0 and k4 == 0:
                        st4["sgw"] = [B("sgw%d" % i, [512], F32) for i in range(2)]
                    if k4 == 0:
                        st4["w"] = load_w_pair(w_in[:, 2304 + 512 * q4:2304 + 512 * q4 + 512], w_in[:, 3328 + 512 * q4:3328 + 512 * q4 + 512], 512, tile=tt, li=q4)
                    wa, wg = st4["w"]
                    sgw_ = st4["sgw"]
                    kc = 4 * q4 + k4
                    segs = [(512, hsrc_, hkeys_, "main")]
                    if tt == 0:
                        segs.append((30, (lambda k: hTh30.ap[:, k, 0:30]), [hTh30.k()], "halo"))
                        segs.append((NS, (lambda k: hsT.ap[:, k, :]), [hsT.k()], "samp"))
                    for (n, sfn, skeys, kind) in segs:
                        ba = rot("pbank", 4)
                        bg = rot("pbank", 4)

                        def fu(e, ba=ba, bg=bg, n=n, sfn=sfn):
                            r = None
                            for k in range(8):
                                e.matmul(out=ps(ba, n), lhsT=wa.ap[:, k, 128 * k4:128 * k4 + 128], rhs=sfn(k), start=(k == 0), stop=(k == 7))
                            for k in range(8):
                                r = e.matmul(out=ps(bg, n), lhsT=wg.ap[:, k, 128 * k4:128 * k4 + 128], rhs=sfn(k), start=(k == 0), stop=(k == 7))
                            return r
                        P.add("pe", fu, reads=[wa.k(), wg.k()] + skeys, writes=psk(ba, bg))
                        sg = sgw_[rot("sgw", 2)]
                        P.add("act", lambda e, sg=sg, bg=bg, n=n: e.activation(out=sg.ap[:, 0:n], in_=ps(bg, n), func=AF.Sigmoid), reads=[], writes=psk(bg) + [sg.k()])
                        if kind == "main":
                            P.add("dve", lambda e, sg=sg, ba=ba: e.tensor_tensor(out=zT.ap[:, kc, 30:542], in0=ps(ba, 512), in1=sg.ap, op=ALU.mult),
                                  reads=[sg.k()], writes=psk(ba) + [zT.k(kc, "m")])
                            if tt == 3:
                                P.add("dve", lambda e, sg=sg, ba=ba: e.tensor_tensor(out=zlast.ap[:, kc, :], in0=ps(ba, 30, 482), in1=sg.ap[:, 482:512], op=ALU.mult),
                                      reads=[sg.k()], writes=psk(ba) + [zlast.k(kc)])
                        elif kind == "halo":
                            P.add("dve", lambda e, sg=sg, ba=ba: e.tensor_tensor(out=zT.ap[:, kc, 0:30], in0=ps(ba, 30), in1=sg.ap[:, 0:30], op=ALU.mult),
                                  reads=[sg.k()], writes=psk(ba) + [zT.k(kc, "h")])
                        else:
                            P.add("dve", lambda e, sg=sg, ba=ba: e.tensor_tensor(out=zsT.ap[:, kc, :], in0=ps(ba, NS), in1=sg.ap[:, 0:NS], op=ALU.mult),
                                  reads=[sg.k()], writes=psk(ba) + [zsT.k(kc)])
                    if q4 == 1 and k4 == 3:
                        for b_ in sgw_:
                            b_.free()
                        if tt == 3:
                            zrow2 = B("zrow2", [D], F32)

                            def tzl(e):
                                r = None
                                for kc_ in range(8):
                                    r = e.transpose(out=psum[0:30, 128 * kc_:128 * kc_ + 128], in_=zlast.ap[:, kc_, :], identity=identf.ap)
                                return r
                            P.add("pe", tzl, reads=[zlast.k(kc_) for kc_ in range(8)] + [identf.k()], writes=psk(0, 1))
                            P.add("act", lambda e: e.activation(out=zrow2.ap[0:30, :], in_=psum[0:30, 0:1024], func=AF.Copy), reads=[], writes=psk(0, 1) + [zrow2.k()])
                            P.add("sp", lambda e: e.dma_start(out=convo, in_=zrow2.ap[0:30, :]), reads=[zrow2.k()], writes=[("dram", "convo")], dma=True, chan="misc")
                            zrow2.free()
                return run
            return [step(q4, k4) for q4 in range(2) for k4 in range(4)]

        for t in range(4):
            wsamp = (t == 0)
            T0 = t * 512
            wl["tile"] = t
            wl["li"] = 2
            def hsrc(kc, T0=T0):
                return hT.ap[:, kc, T0:T0 + 512]
            hkeys = [hT.k(4 * t + i) for i in range(4)]

            if t == 0:
                for st_ in make_4a(0):
                    st_()

            if wsamp:
                ysb = B("ysb", [8, NS], F32)
                hp = B("hp", [8, NS * 30], F32)
                sTs = B("sTs", [8, NS], BF16)
                P.add("dve", lambda e: e.tensor_tensor(out=hp.ap.rearrange("p k (b i) -> p k b i", i=30), in0=histT.ap.rearrange("p k (b i) -> p k b i", i=30),
                                                       in1=wT.ap[:, :, 0:30].unsqueeze(2).broadcast_to([128, 8, NS, 30]), op=ALU.mult),
                      reads=histT_all + [wT.k()], writes=[hp.k()])
                P.add("dve", lambda e: e.tensor_reduce(out=ysb.ap, in_=hp.ap.rearrange("p k (b i) -> p k b i", i=30), axis=AX.X, op=ALU.add),
                      reads=[hp.k()], writes=[ysb.k()])
                P.add("dve", lambda e: e.tensor_tensor(out=hp.ap[:, :, 0:NS], in0=zsT.ap, in1=wT.ap[:, :, 30:31].broadcast_to([128, 8, NS]), op=ALU.mult),
                      reads=[zsT.k(kc) for kc in range(8)] + [wT.k()], writes=[hp.k()])
                P.add("dve", lambda e: e.tensor_tensor(out=ysb.ap, in0=ysb.ap, in1=hp.ap[:, :, 0:NS], op=ALU.add), reads=[hp.k()], writes=[ysb.k()])
                P.add("dve", lambda e: e.tensor_tensor(out=ysb.ap, in0=ysb.ap, in1=vcol.ap[:, 0, :].unsqueeze(2).broadcast_to([128, 8, NS]), op=ALU.add),
                      reads=[vcol.k()], writes=[ysb.k()])
                ln_silu(ysb.ap, [ysb.k()], NS, sTs.ap, [sTs.k()], "s")
                dbg("sTs", sTs.ap.rearrange("p a b -> p (a b)"), 128, 8 * NS, [sTs.k()])
                ysb.free(); hp.free(); histT.free()
                zrow = B("zrow", [D], F32)

                def tz(e):
                    r = None
                    for kc in range(8):
                        r = e.transpose(out=psum[0:NS, 128 * kc:128 * kc + 128], in_=zsT.ap[:, kc, :], identity=identf.ap)
                    return r
                P.add("pe", tz, reads=[zsT.k(kc) for kc in range(8)] + [identf.k()], writes=psk(0, 1))
                P.add("act", lambda e: e.activation(out=zrow.ap[0:NS, :], in_=psum[0:NS, 0:1024], func=AF.Copy), reads=[], writes=psk(0, 1) + [zrow.k()])
                P.add("sp", lambda e: e.dma_start(out=convs[:, 29, :], in_=zrow.ap[0:NS, :]), reads=[zrow.k()], writes=[("dram", "convs_new")], dma=True, chan="misc")
                zrow.free()

            ybuf = B("ybuf", [8, 512], F32)
            sT = B("sT", [8, 512], BF16)
            diag = [B("diag%d" % i, [31, 128], BF16) for i in range(2)]
            ybfr = [B("ybf%d" % i, [512], BF16) for i in range(3)]
            ysqr = [B("ysq%d" % i, [512], BF16) for i in range(3)]
            zTs = B("zTs", [8, 542], BF16)
            for kc in range(8):
                P.add("dve", lambda e, kc=kc: e.tensor_copy(out=zTs.ap[:, kc, 0:541], in_=zT.ap[:, kc, 1:542]),
                      reads=[zT.k(kc, "m"), zT.k(kc, "h")], writes=[zTs.k(kc)])

            def conv_a(kc):
                dg = diag[kc % 2]
                for i in range(31):
                    P.add("dve", lambda e, i=i: e.tensor_scalar(out=dg.ap[:, i, :], in0=ident, scalar1=wT.ap[:, kc, i:i + 1], scalar2=None, op0=ALU.mult),
                          reads=[cst.k("i"), wT.k()], writes=[dg.k(i)])
                bank = rot("cbank", 2)

                def fc(e):
                    r = None
                    for i in range(31):
                        src = zT.ap[:, kc, i:i + 512] if i % 2 == 0 else zTs.ap[:, kc, i - 1:i - 1 + 512]
                        r = e.matmul(out=ps(bank), lhsT=dg.ap[:, i, :], rhs=src, start=(i == 0), stop=(i == 30))
                    return r
                P.add("pe", fc, reads=[dg.k(i) for i in range(31)] + [zT.k(kc, "m"), zT.k(kc, "h"), zTs.k(kc)], writes=psk(bank))
                P.add("act", lambda e: e.activation(out=ybuf.ap[:, kc, :], in_=ps(bank), func=AF.Identity, bias=vcol.ap[:, 0, kc:kc + 1], scale=1.0),
                      reads=[vcol.k()], writes=psk(bank) + [ybuf.k(kc)])
                ybf = ybfr[kc % 3]; ysq = ysqr[kc % 3]
                P.add("act", lambda e: e.activation(out=ybf.ap, in_=ybuf.ap[:, kc, :], func=AF.Copy), reads=[ybuf.k(kc)], writes=[ybf.k()])
                P.add("act", lambda e: e.activation(out=ysq.ap, in_=ybuf.ap[:, kc, :], func=AF.Square), reads=[ybuf.k(kc)], writes=[ysq.k()])
                if t < 3:
                    P.add("dve", lambda e: e.tensor_copy(out=zT.ap[:, kc, 0:30], in_=zT.ap[:, kc, 512:542]),
                          reads=[zT.k(kc, "m")], writes=[zT.k(kc, "h")])

            def conv_b(kc):
                ybf = ybfr[kc % 3]; ysq = ysqr[kc % 3]

                def fst(e):
                    e.matmul(out=ps(2), lhsT=ones_bf, rhs=ybf.ap, start=(kc == 0), stop=(kc == 7))
                    return e.matmul(out=ps(3), lhsT=ones_bf, rhs=ysq.ap, start=(kc == 0), stop=(kc == 7))
                P.add("pe", fst, reads=[ybf.k(), ysq.k(), cst.k("o")], writes=psk(2, 3))

            for kc in range(9):
                if kc < 8:
                    conv_a(kc)
                    if kc in (2, 5):
                        bulk_step(2)
                if kc > 0:
                    conv_b(kc - 1)
            for dg in diag:
                dg.free()
            zTs.free()
            ybk = [ybuf.k(kc) for kc in range(8)]
            stt = B("stt", [3, 512], F32)
            mean = stt.ap[:, 0, :]; var = stt.ap[:, 1, :]; rstd = stt.ap[:, 2, :]
            P.add("act", lambda e: e.activation(out=mean, in_=ps(2), func=AF.Copy, scale=1.0 / D), reads=[], writes=psk(2) + [stt.k(0)])
            P.add("dve", lambda e: e.tensor_tensor(out=var, in0=mean, in1=mean, op=ALU.mult), reads=[stt.k(0)], writes=[stt.k(1)])
            P.add("dve", lambda e: e.scalar_tensor_tensor(out=var, in0=ps(3), scalar=1.0 / D, in1=var, op0=ALU.mult, op1=ALU.subtract),
                  reads=[], writes=psk(3) + [stt.k(1)])
            P.add("act", lambda e: e.activation(out=var, in_=var, func=AF.Sqrt, bias=epsb.ap[:, 0:1], scale=1.0), reads=[epsb.k()], writes=[stt.k(1)])
            P.add("dve", lambda e: e.reciprocal(out=rstd, in_=var), reads=[stt.k(1)], writes=[stt.k(2)])
            for kc in range(8):
                P.add("dve", lambda e, kc=kc: e.tensor_tensor(out=ybuf.ap[:, kc, :], in0=ybuf.ap[:, kc, :], in1=mean, op=ALU.subtract), reads=[stt.k(0)], writes=[ybuf.k(kc)])
                P.add("dve", lambda e, kc=kc: e.tensor_tensor(out=ybuf.ap[:, kc, :], in0=ybuf.ap[:, kc, :], in1=rstd, op=ALU.mult), reads=[stt.k(2)], writes=[ybuf.k(kc)])
                P.add("act", lambda e, kc=kc: e.activation(out=sT.ap[:, kc, :], in_=ybuf.ap[:, kc, :], func=AF.Silu, bias=vcol.ap[:, 2, kc:kc + 1], scale=vcol.ap[:, 1, kc:kc + 1]),
                      reads=[ybuf.k(kc), vcol.k()], writes=[sT.k(kc)])
            sT_all = [sT.k(kc) for kc in range(8)]
            if t == 0:
                dbg("sT", sT.ap.rearrange("p a b -> p (a b)"), 128, 8 * 512, sT_all)
            ybuf.free(); stt.free()
            for b_ in ybfr + ysqr:
                b_.free()

            tiles = [(512, "p")] + ([(NS, "s")] if wsamp else [])
            blocks = [(128, i, "p") for i in range(4)] + ([(NS, 0, "s")] if wsamp else [])
            nb = len(blocks)
            x1 = B("x1", [nb, D], F32)
            for bi_, (M, i, kind) in enumerate(blocks):
                xsrc = xo[T0 + 128 * i:T0 + 128 * i + 128, :] if kind == "p" else xs
                P.add("sp", lambda e, bi_=bi_, M=M, xsrc=xsrc: e.dma_start(out=x1.ap[0:M, bi_, :], in_=xsrc), writes=[x1.k(bi_)], dma=True, chan=("xin", bi_ % 2))
            g_post = load_gain(1, "g_post"); g_fpre = load_gain(2, "g_fpre")
            mixT = B("mixT", [8, 512], BF16)
            mixTs = B("mixTs", [8, NS], BF16) if wsamp else None
            w_co = load_w("w_co", w_conv_out, 8, D, ("w", rot("wslot", 3)))
            sab = [B("sab%d" % i, [512], F32) for i in range(2)]
            for q2 in range(2):
                wga, wgb = load_w_pair(w_in[:, 4352 + 512 * q2:4352 + 512 * q2 + 512], w_in[:, 5376 + 512 * q2:5376 + 512 * q2 + 512], 512)
                bulk_step(1)
                for j4 in range(4):
                    j = 4 * q2 + j4
                    for (n, kind) in tiles:
                        if kind == "p":
                            o_src = lambda k: oT.ap[:, k, T0:T0 + 512]
                            s_src = lambda k: sT.ap[:, k, :]
                            h_src = hsrc
                            rk1 = [oT.k()] + hkeys
                            rk2 = sT_all
                            dst = mixT.ap[:, j, :]
                            dk = [mixT.k(j)]
                        else:
                            o_src = lambda k: osT.ap[:, k, :]
                            s_src = lambda k: sTs.ap[:, k, :]
                            h_src = lambda k: hsT.ap[:, k, :]
                            rk1 = [osT.k(), hsT.k()]
                            rk2 = [sTs.k()]
                            dst = mixTs.ap[:, j, :]
                            dk = [mixTs.k(j)]

                        def fm1(e, n=n, j=j, j4=j4, o_src=o_src, h_src=h_src, wga=wga, wgb=wgb):
                            r = None
                            for k in range(2):
                                e.matmul(out=ps(4, n), lhsT=w_ao.ap[:, k, 128 * j:128 * j + 128], rhs=o_src(k), start=(k == 0), stop=(k == 1))
                            for k in range(8):
                                e.matmul(out=ps(6, n), lhsT=wga.ap[:, k, 128 * j4:128 * j4 + 128], rhs=h_src(k), start=(k == 0), stop=(k == 7))
                            for k in range(8):
                                r = e.matmul(out=ps(7, n), lhsT=wgb.ap[:, k, 128 * j4:128 * j4 + 128], rhs=h_src(k), start=(k == 0), stop=(k == 7))
                            return r

                        def fm2(e, n=n, j=j, s_src=s_src):
                            r = None
                            for k in range(8):
                                r = e.matmul(out=ps(5, n), lhsT=w_co.ap[:, k, 128 * j:128 * j + 128], rhs=s_src(k), start=(k == 0), stop=(k == 7))
                            return r
                        P.add("pe", fm1, reads=[w_ao.k(), wga.k(), wgb.k()] + rk1, writes=psk(4, 6, 7))
                        P.add("pe", fm2, reads=[w_co.k()] + rk2, writes=psk(5))
                        sa, sb2 = sab[0], sab[1]
                        P.add("act", lambda e, n=n, sa=sa: e.activation(out=sa.ap[:, 0:n], in_=ps(6, n), func=AF.Sigmoid), reads=[], writes=psk(6) + [sa.k()])
                        P.add("act", lambda e, n=n, sb2=sb2: e.activation(out=sb2.ap[:, 0:n], in_=ps(7, n), func=AF.Sigmoid), reads=[], writes=psk(7) + [sb2.k()])
                        P.add("dve", lambda e, n=n, sa=sa: e.tensor_tensor(out=sa.ap[:, 0:n], in0=ps(4, n), in1=sa.ap[:, 0:n], op=ALU.mult), reads=[], writes=psk(4) + [sa.k()])
                        P.add("dve", lambda e, n=n, sb2=sb2: e.tensor_tensor(out=sb2.ap[:, 0:n], in0=ps(5, n), in1=sb2.ap[:, 0:n], op=ALU.mult), reads=[], writes=psk(5) + [sb2.k()])
                        P.add("dve", lambda e, n=n, sa=sa, sb2=sb2, dst=dst: e.tensor_tensor(out=dst, in0=sa.ap[:, 0:n], in1=sb2.ap[:, 0:n], op=ALU.add),
                              reads=[sb2.k()], writes=[sa.k()] + dk)
                wga.free(); wgb.free()
            if t == 0:
                dbg("mixT", mixT.ap.rearrange("p a b -> p (a b)"), 128, 8 * 512, [mixT.k(j) for j in range(8)])
                dbg("mixTs", mixTs.ap.rearrange("p a b -> p (a b)"), 128, 8 * NS, [mixTs.k(j) for j in range(8)])
            w_co.free()
            for b in sab:
                b.free()
            sT.free()
            if wsamp:
                sTs.free()

            h2T = B("h2T", [8, 512], BF16)
            h2Ts = B("h2Ts", [8, NS], BF16) if wsamp else None
            tmp = [B("tmp%d" % i, [D], F32) for i in range(2)]
            w_o = load_w("w_o", w_out, 8, D, ("w", rot("wslot", 3)))

            def tm_proj(wb, nk, lhs_fn, rkeys, M, banks):
                def f(e):
                    r = None
                    for hf in range(2):
                        for k in range(nk):
                            r = e.matmul(out=psum[0:M, 512 * banks[hf]:512 * banks[hf] + 512], lhsT=lhs_fn(k), rhs=wb.ap[:, k, 512 * hf:512 * hf + 512],
                                         start=(k == 0), stop=(k == nk - 1))
                    return r
                P.add("pe", f, reads=[wb.k()] + rkeys, writes=psk(*banks))

            def post_norm_residual(M, banks, gain, res_ap, res_keys, out_ap, out_keys):
                src = psum[0:M, 512 * banks[0]:512 * banks[0] + 1024]
                i = rot("small", 4)
                ms = small.ap[0:M, 0, i:i + 1]; sd = small.ap[0:M, 1, i:i + 1]; rs = small.ap[0:M, 2, i:i + 1]
                P.add("act", lambda e: e.activation(out=junk.ap[0:M, :], in_=src, func=AF.Square, scale=1.0 / 32.0, accum_out=ms), reads=[], writes=psk(*banks) + [small.k(0, i)])
                P.add("act", lambda e: e.activation(out=sd, in_=ms, func=AF.Sqrt, bias=epsb.ap[0:M, 0:1], scale=1.0), reads=[small.k(0, i), epsb.k()], writes=[small.k(1, i)])
                P.add("dve", lambda e: e.reciprocal(out=rs, in_=sd), reads=[small.k(1, i)], writes=[small.k(2, i)])
                ti = rot("tmp", 2)
                tb = tmp[ti]
                P.add("dve", lambda e: e.scalar_tensor_tensor(out=tb.ap[0:M, :], in0=src, scalar=rs, in1=gain.ap[0:M, :], op0=ALU.mult, op1=ALU.mult),
                      reads=[small.k(2, i), gain.k()], writes=psk(*banks) + [tb.k()])
                P.add("dve", lambda e: e.tensor_tensor(out=out_ap, in0=tb.ap[0:M, :], in1=res_ap, op=ALU.add), reads=[tb.k()] + res_keys, writes=out_keys)

            d_hb = {}

            def d_a(bi_):
                M, i, kind = blocks[bi_]
                banks = (0, 1) if bi_ % 2 == 0 else (2, 3)
                if kind == "p":
                    lhs = lambda k: mixT.ap[:, k, 128 * i:128 * i + 128]
                    rk = [mixT.k(j) for j in range(8)]
                    xsrc = xo[T0 + 128 * i:T0 + 128 * i + 128, :]
                else:
                    lhs = lambda k: mixTs.ap[:, k, :]
                    rk = [mixTs.k(j) for j in range(8)]
                    xsrc = xs
                tm_proj(w_o, 8, lhs, rk, M, banks)
                post_norm_residual(M, banks, g_post, x1.ap[0:M, bi_, :], [x1.k(bi_)], x1.ap[0:M, bi_, :], [x1.k(bi_)])
                hb = hbb[rot("hbb", len(hbb))]
                d_hb[bi_] = hb
                rms_to_bf16(x1.ap[0:M, bi_, :], [x1.k(bi_)], M, g_fpre, hb.ap[0:M, :], hb.k())

            def d_b(bi_):
                M, i, kind = blocks[bi_]
                hb = d_hb[bi_]
                tb_ = 4 + rot("tbank", 2)
                if kind == "p":
                    transpose_rows(hb.ap[0:M, :], hb.k(), M, 8, (lambda: h2T.ap[:, :, 128 * i:128 * i + 128]), [h2T.k(i)], tb_, evac=("act" if tb_ == 4 else "dve"))
                else:
                    transpose_rows(hb.ap[0:M, :], hb.k(), M, 8, (lambda: h2Ts.ap), [h2Ts.k()], tb_, evac="act")

            for bi_ in range(nb + 1):
                if bi_ < nb:
                    d_a(bi_)
                if bi_ > 0:
                    d_b(bi_ - 1)
            if t == 0:
                dbg("x1", x1.ap.rearrange("p a b -> p (a b)"), 128, nb * D, [x1.k(i_) for i_ in range(nb)])
                dbg("h2T", h2T.ap.rearrange("p a b -> p (a b)"), 128, 8 * 512, [h2T.k(i_) for i_ in range(4)])
            w_o.free(); g_post.free(); g_fpre.free()
            for b_ in tmp:
                b_.free()
            mixT.free()
            if wsamp:
                mixTs.free()

            g_fpost = load_gain(3, "g_fpost"); g_ple = load_gain(4, "g_ple")
            aTp = [B("aT%d" % i_, [8 if i_ < 2 else 6, 512], BF16) for i_ in range(3)]
            sgw = [B("sgw%d" % i, [512], F32) for i in range(2)]
            aTs = B("aTs", [NFF, NS], BF16) if wsamp else None
            h2keys = [h2T.k(i) for i in range(4)]
            for s6 in range(6):
                nch = 4 if s6 < 5 else 2
                sl_ = wslot()
                wf = WView(sl_, sl_.ap.rearrange("p (a b c) -> p a b c", a=8, b=2))

                def ldf(e, wf=wf, s6=s6, nch=nch):
                    a = e.dma_start(out=wf.ap[:, :, 0, 0:128 * nch], in_=w_ffn_in[:, 512 * s6:512 * s6 + 128 * nch].rearrange("(kc p) c -> p kc c", p=128))
                    b = e.dma_start(out=wf.ap[:, :, 1, 0:128 * nch], in_=w_ffn_in[:, DFF + 512 * s6:DFF + 512 * s6 + 128 * nch].rearrange("(kc p) c -> p kc c", p=128))
                    return [a, b]
                ring_load(sl_, ldf, 2)
                bulk_step(2 if s6 < 5 else 1)
                for j4 in range(nch):
                    j = 4 * s6 + j4
                    for (n, kind) in tiles:
                        if kind == "p":
                            src = lambda k: h2T.ap[:, k, :]
                            rk = h2keys
                            dst = aTp[j // 8].ap[:, j % 8, :]
                            dk = [aTp[j // 8].k(j % 8)]
                        else:
                            src = lambda k: h2Ts.ap[:, k, :]
                            rk = [h2Ts.k()]
                            dst = aTs.ap[:, j, :]
                            dk = [aTs.k(j)]
                        bg = rot("pbank", 4)
                        bu = rot("pbank", 4)

                        def ff(e, bg=bg, bu=bu, n=n, src=src, wf=wf, j4=j4):
                            r = None
                            for k in range(8):
                                e.matmul(out=ps(bg, n), lhsT=wf.ap[:, k, 0, 128 * j4:128 * j4 + 128], rhs=src(k), start=(k == 0), stop=(k == 7))
                            for k in range(8):
                                r = e.matmul(out=ps(bu, n), lhsT=wf.ap[:, k, 1, 128 * j4:128 * j4 + 128], rhs=src(k), start=(k == 0), stop=(k == 7))
                            return r
                        P.add("pe", ff, reads=[wf.k()] + rk, writes=psk(bg, bu))
                        si = rot("sgw", 2)
                        sg = sgw[si]
                        P.add("act", lambda e, sg=sg, bg=bg, n=n: e.activation(out=sg.ap[:, 0:n], in_=ps(bg, n), func=AF.Silu), reads=[], writes=psk(bg) + [sg.k()])
                        P.add("dve", lambda e, sg=sg, bu=bu, n=n, dst=dst: e.tensor_tensor(out=dst, in0=ps(bu, n), in1=sg.ap[:, 0:n], op=ALU.mult),
                              reads=[sg.k()], writes=psk(bu) + dk)
                wf.free()
            h2T.free()
            for b_ in sgw:
                b_.free()
            if wsamp:
                h2Ts.free()

            wfos = []
            for s3 in range(3):
                k0 = 8 * s3
                nk = 8 if s3 < 2 else 6
                wb_ = load_w("w_fo", w_ffn_out[128 * k0:128 * (k0 + nk), :], nk, D)
                wfos.append(wb_)
            wfo_keys = [w_.k() for w_ in wfos]
            tmp = [B("tmp%d" % i, [D], F32) for i in range(2)]
            pin = [B("pin%d" % i, [256], F32) for i in range(nb)]
            for bi_, (M, i, kind) in enumerate(blocks):
                psrc = po[T0 + 128 * i:T0 + 128 * i + 128, :] if kind == "p" else psm
                P.add("sp", lambda e, bi_=bi_, M=M, psrc=psrc: e.dma_start(out=pin[bi_].ap[0:M, :], in_=psrc), writes=[pin[bi_].k()], dma=True, chan=("pin", bi_ % 2))
            bulk_step(1)
            for bi_, (M, i, kind) in enumerate(blocks):
                banks = (0, 1) if bi_ % 2 == 0 else (2, 3)
                if kind == "p":
                    lhs = lambda k, i=i: aTp[k // 8].ap[:, k % 8, 128 * i:128 * i + 128]
                    rk = [aTp[j // 8].k(j % 8) for j in range(NFF)]
                else:
                    lhs = lambda k: aTs.ap[:, k, :]
                    rk = [aTs.k(j) for j in range(NFF)]

                for s3 in range(3):
                    def f(e, M=M, banks=banks, lhs=lhs, s3=s3):
                        r = None
                        for hf in range(2):
                            for k in range(8 * s3, min(NFF, 8 * s3 + 8)):
                                r = e.matmul(out=psum[0:M, 512 * banks[hf]:512 * banks[hf] + 512], lhsT=lhs(k), rhs=wfos[s3].ap[:, k % 8, 512 * hf:512 * hf + 512],
                                             start=(k == 0), stop=(k == NFF - 1))
                        return r
                    P.add("pe", f, reads=[wfo_keys[s3]] + rk, writes=psk(*banks))
                post_norm_residual(M, banks, g_fpost, x1.ap[0:M, bi_, :], [x1.k(bi_)], x1.ap[0:M, bi_, :], [x1.k(bi_)])
            if t == 0:
                dbg("x2", x1.ap.rearrange("p a b -> p (a b)"), 128, nb * D, [x1.k(i_) for i_ in range(nb)])
            for b_ in wfos + aTp + [g_fpost]:
                b_.free()
            if wsamp:
                aTs.free()
            w_pg = load_w("w_pg", w_ple_gate, 8, D, ("w", rot("wslot", 3)))
            h3b = [B("h3b%d" % i, [8, 128], BF16) for i in range(2)]
            pTb = [B("pTb%d" % i, [2, 128], BF16) for i in range(2)]
            pbf = [B("pbf%d" % i, [256], BF16) for i in range(nb)]
            sgg = B("sgg", [D], F32)
            e_hb = {}

            def e_a(bi_):
                M, i, kind = blocks[bi_]
                psrc = po[T0 + 128 * i:T0 + 128 * i + 128, :] if kind == "p" else psm
                hb = hb_all[bi_]
                e_hb[bi_] = hb
                rms_to_bf16(x1.ap[0:M, bi_, :], [x1.k(bi_)], M, g_ple, hb.ap[0:M, :], hb.k())
                pi_ = bi_
                P.add("act", lambda e: e.activation(out=pbf[pi_].ap[0:M, :], in_=pin[bi_].ap[0:M, :], func=AF.Copy), reads=[pin[bi_].k()], writes=[pbf[pi_].k()])

            def e_b(bi_):
                M, i, kind = blocks[bi_]
                ydst = y[T0 + 128 * i:T0 + 128 * i + 128, :] if kind == "p" else ysm
                hb = e_hb[bi_]
                h3 = h3b[bi_ % 2]
                tb_ = 0 + rot("tbank2", 2)
                transpose_rows(hb.ap[0:M, :], hb.k(), M, 8, (lambda: h3.ap[:, :, 0:M]), [h3.k()], tb_, evac=("act" if tb_ == 0 else "dve"))
                pi_ = bi_
                pT = pTb[bi_ % 2]
                tb2 = 2 + rot("tbank3", 2)
                transpose_rows(pbf[pi_].ap[0:M, :], pbf[pi_].k(), M, 2, (lambda: pT.ap[:, :, 0:M]), [pT.k()], tb2, evac="dve")
                tm_proj(w_pg, 8, (lambda k: h3.ap[:, k, 0:M]), [h3.k()], M, (4, 5))
                tm_proj(w_pp, 2, (lambda k: pT.ap[:, k, 0:M]), [pT.k()], M, (6, 7))
                P.add("act", lambda e: e.activation(out=sgg.ap[0:M, :], in_=psum[0:M, 512 * 4:512 * 4 + 1024], func=AF.Sigmoid), reads=[], writes=psk(4, 5) + [sgg.k()])
                P.add("dve", lambda e: e.tensor_tensor(out=sgg.ap[0:M, :], in0=psum[0:M, 512 * 6:512 * 6 + 1024], in1=sgg.ap[0:M, :], op=ALU.mult), reads=[], writes=psk(6, 7) + [sgg.k()])
                ti = rot("tmp", 2)
                tb = tmp[ti]
                P.add("dve", lambda e: e.tensor_tensor(out=tb.ap[0:M, :], in0=sgg.ap[0:M, :], in1=x1.ap[0:M, bi_, :], op=ALU.add),
                      reads=[sgg.k(), x1.k(bi_)], writes=[tb.k()])
                P.add("sp", lambda e: e.dma_start(out=ydst, in_=tb.ap[0:M, :]), reads=[tb.k()], writes=[("dram", "y", t, bi_)], dma=True, chan=("yout", ti))

            nxt4a = make_4a(t + 1) if t < 3 else []
            hbx = [B("hbx%d" % i, [D], BF16) for i in range(max(0, nb - len(hbb)))]
            hb_all = hbb + hbx
            for bi_ in range(nb):
                e_a(bi_)
            for bi_ in range(nb):
                e_b(bi_)
                for _ in range(2):
                    if nxt4a:
                        nxt4a.pop(0)()
            while nxt4a:
                nxt4a.pop(0)()
            for b in [w_pg, g_ple, sgg, x1] + h3b + pTb + pin + pbf + tmp + hbx:
                b.free()

        bulk_step(len(bulk))
        P.emit(st)
        print("arena peak bytes:", AR.peak, "ops:", {e: len(P.ops[e]) for e in ENGS}, "chans:", len(P.chan_cnt))
    return nc


_NC = None
_LAST = None


def _t5_bucket(dist):
    dist = np.asarray(dist).astype(np.int32)
    d = np.maximum(dist, 1).astype(np.float32)
    large = 16 + np.floor(np.log(d / 16) / np.log(2048 / 16) * 16).astype(np.int32)
    large = np.minimum(large, 31)
    return np.where(dist < 16, dist, large).astype(np.int32)


def kernel(**inp):
    global _NC
    if _NC is None:
        _NC = build_program()
    f32 = np.float32
    A = lambda k: np.ascontiguousarray(np.asarray(inp[k], dtype=f32))
    x_prompt = A("x_prompt"); x_sample = A("x_sample"); p_prompt = A("p_prompt")[0]; p_sample = A("p_sample")[0]
    caches = [A("cache_kv_w128")[0], A("cache_kv_w512")[0], A("cache_kv_w2048")[0]]
    state_conv = A("state_conv")[0]
    rel_bias = A("rel_bias")
    w_in = A("w_in")[0]
    perm = []
    for g in range(3):
        perm += list(range(256 * g, 256 * g + 256)) + list(range(768 + 256 * g, 768 + 256 * g + 256)) + list(range(1536 + 256 * g, 1536 + 256 * g + 256))
    perm += list(range(2304, 6400))
    w_in_r = np.ascontiguousarray(w_in[:, perm])
    vecs = np.ascontiguousarray(np.stack([A("norm_mix_pre")[0], A("norm_mix_post")[0], A("norm_ffn_pre")[0], A("norm_ffn_post")[0],
                                          A("ple_norm")[0], A("conv_b")[0], A("conv_ln_g")[0], A("conv_ln_b")[0]]))
    NEGC = f32(-200.0)
    kk = np.arange(128)[:, None]; qq = np.arange(128)[None, :]
    ebsrc = np.empty((128, 12, 2, 128), f32)
    sbias = np.empty((128, 12), f32)
    for h in range(12):
        dil = GROUPS[h // 4][1]
        jp = qq + 128 - kk
        jc = qq - kk
        ebsrc[:, h, 0, :] = np.where(kk >= qq, rel_bias[_t5_bucket(np.clip(jp, 0, 128) * dil), h], NEGC)
        ebsrc[:, h, 1, :] = np.where(kk <= qq, rel_bias[_t5_bucket(np.clip(jc, 0, 128) * dil), h], NEGC)
        sbias[:, h] = rel_bias[_t5_bucket((128 - np.arange(128)) * dil), h]
    ebsrc = np.ascontiguousarray(ebsrc.reshape(128, -1))
    sbias0 = np.ascontiguousarray(np.broadcast_to(rel_bias[0:1, :], (NS, 12))).astype(f32)
    c_selB = np.zeros((NS, NS, 128), f32)
    c_selT = np.zeros((128, NS, NS), f32)
    for b in range(NS):
        c_selB[b, b, :] = 1.0
        c_selT[:, b, b] = 1.0
    shared = {
        "ebsrc": ebsrc, "sbias": sbias, "sbias0": sbias0, "c_selB": c_selB.reshape(NS, -1), "c_selT": c_selT.reshape(128, -1),
        "w_in": w_in_r, "conv_w": A("conv_w")[0], "vecs": vecs, "w_conv_out": A("w_conv_out")[0], "w_attn_out": A("w_attn_out")[0],
        "w_out": A("w_out")[0], "w_ffn_in": A("w_ffn_in")[0], "w_ffn_out": A("w_ffn_out")[0], "w_ple_gate": A("w_ple_gate")[0],
        "w_ple_proj": A("w_ple_proj")[0],
    }
    in_maps = []
    for c in range(8):
        b, q = c // 4, c % 4
        s0 = NT * q
        m = dict(shared)
        m["xo"] = np.ascontiguousarray(x_prompt[b, s0:s0 + NT])
        m["xh"] = np.ascontiguousarray(x_prompt[b, s0 - NT:s0]) if q > 0 else np.zeros((NT, D), f32)
        m["hv"] = np.full((128, 1), 1.0 if q > 0 else 0.0, f32)
        m["po"] = np.ascontiguousarray(p_prompt[b, s0:s0 + NT])
        sl = slice(NS * c, NS * c + NS)
        m["xs"] = np.ascontiguousarray(x_sample[sl, 0])
        m["psm"] = np.ascontiguousarray(p_sample[sl, 0])
        m["c128"] = np.ascontiguousarray(caches[0][sl].reshape(NS, 128, 512))
        m["c512"] = np.ascontiguousarray(caches[1][sl].reshape(NS, 512, 512))
        m["c2048"] = np.ascontiguousarray(caches[2][sl].reshape(NS, 2048, 512))
        m["sconv"] = np.ascontiguousarray(state_conv[sl])
        in_maps.append(m)
    res = run_bass_kernel_spmd(_NC, in_maps, core_ids=list(range(8)))
    R = res.results
    if DEBUG:
        global _LAST
        _LAST = R
    y_prompt = np.stack([np.concatenate([R[4 * b + q]["y"] for q in range(4)], 0) for b in range(2)], 0).astype(f32)
    y_sample = np.concatenate([R[c]["ysm"] for c in range(8)], 0).reshape(128, 1, D).astype(f32)
    kvp = []
    for g, (win, _) in enumerate(GROUPS):
        name = ["kv128o", "kv512o", "kv2048o"][g]
        kvp.append(np.stack([R[4 * b + 3][name].reshape(win, 2, 4, 64) for b in range(2)], 0)[None].astype(f32))
    conv_p = np.stack([R[4 * b + 3]["convo"] for b in range(2)], 0)[None].astype(f32)
    kvsm = []
    for g, (win, _) in enumerate(GROUPS):
        name = ["kv128s", "kv512s", "kv2048s"][g]
        kvsm.append(np.concatenate([R[c][name] for c in range(8)], 0).reshape(128, win, 2, 4, 64)[None].astype(f32))
    conv_s = np.concatenate([R[c]["convs"] for c in range(8)], 0)[None].astype(f32)
    return (y_prompt, y_sample, kvp[0], kvp[1], kvp[2], conv_p, kvsm[0], kvsm[1], kvsm[2], conv_s)
```
